# Optimizing a Trainium2 kernel written in Bass

```python
import math
import jax
import jax.numpy as jnp
from jax import lax
import numpy as np

D_MODEL = 1024
BATCH = 4
SEQ = 8192
DEPTH = 2

GRID_W = 64
CTX_LEN = 256
Q_BLOCK = 128
ROPE_THETA = 10000.0
NORM_EPS = 1e-6

SSD_INNER = D_MODEL // 2
SSD_HEAD_DIM = 64
SSD_HEADS = SSD_INNER // SSD_HEAD_DIM
SSD_GROUPS = 2
SSD_STATE = 64
SSD_CHUNK = 128
SSD_CONV_W = 5
SSD_CONV_CH = SSD_INNER + 2 * SSD_GROUPS * SSD_STATE

DIFF_HEADS = 4
DIFF_HD = 64
DIFF_QK = DIFF_HEADS * 2 * DIFF_HD
DIFF_WIDTH = DIFF_HEADS * 2 * DIFF_HD

GQA_HEADS = 8
GQA_KV_HEADS = 2
GQA_HD = 64
GQA_WIDTH = GQA_HEADS * GQA_HD
GQA_KV_WIDTH = GQA_KV_HEADS * GQA_HD

S5_GROUP_CH = 16
S5_STATE = 64
S5_WIDTH = 3 * D_MODEL // 8
S5_GROUPS = S5_WIDTH // S5_GROUP_CH

FFN_HIDDEN = -(-8 * D_MODEL // (3 * 256)) * 256

N_BRANCHES = 4
IN_SPLITS = (SSD_INNER + SSD_CONV_CH + 2 * SSD_HEADS,
             2 * DIFF_QK + DIFF_WIDTH,
             GQA_WIDTH + 2 * GQA_KV_WIDTH,
             S5_WIDTH)
IN_COLS = sum(IN_SPLITS)

kernel_name = 'hybrid_flow_backbone_block'


def split_cols(x, sizes):
    return jnp.split(x, np.cumsum(sizes)[:-1].tolist(), axis=-1)


def rms_norm(x, g):
    xf = x.astype(jnp.float32)
    y = xf * lax.rsqrt(jnp.mean(xf * xf, axis=-1, keepdims=True) + NORM_EPS)
    return (y * g.astype(jnp.float32)).astype(x.dtype)


def modulate(x, shift, scale):
    return x * (1.0 + scale) + shift


def swiglu(x, w_gate_up, w_down):
    gate, up = jnp.split(x @ w_gate_up, 2, axis=-1)
    return (jax.nn.silu(gate) * up) @ w_down


def _flip(t, direction):
    return jnp.flip(t, axis=1) if direction == 1 else t


def axial_rope_tables(n_tokens, head_dim):
    n_rows = n_tokens // GRID_W
    rows = jnp.repeat(jnp.arange(n_rows, dtype=jnp.float32), GRID_W)
    cols = jnp.tile(jnp.arange(GRID_W, dtype=jnp.float32), n_rows)
    quarter = head_dim // 4
    inv_freq = ROPE_THETA ** (-jnp.arange(quarter, dtype=jnp.float32) / quarter)
    ang_r = rows[:, None] * inv_freq
    ang_c = cols[:, None] * inv_freq
    ang = jnp.concatenate([ang_r, ang_r, ang_c, ang_c], axis=-1)
    return jnp.cos(ang), jnp.sin(ang)


def apply_axial_rope(x, cos, sin):
    r1, r2, c1, c2 = jnp.split(x, 4, axis=-1)
    rotated = jnp.concatenate([-r2, r1, -c2, c1], axis=-1)
    out = x * cos[None, :, None, :].astype(x.dtype) + rotated * sin[None, :, None, :].astype(x.dtype)
    return out.astype(x.dtype)


def sweep_query_blocks(block_fn, q):
    bsz, n = q.shape[:2]
    nb = n // Q_BLOCK
    qb = jnp.moveaxis(q.reshape((bsz, nb, Q_BLOCK) + q.shape[2:]), 1, 0)
    out = lax.map(block_fn, qb)
    return jnp.moveaxis(out, 0, 1).reshape((bsz, n) + out.shape[3:])


def depthwise_conv_centred(x, w, b):
    k = w.shape[0]
    pad = (k - 1) // 2
    y = lax.conv_general_dilated(x, w[:, None, :].astype(x.dtype), window_strides=(1,),
                                 padding=[(pad, pad)], dimension_numbers=('NWC', 'WIO', 'NWC'),
                                 feature_group_count=x.shape[-1])
    return y + b.astype(x.dtype)


def ssd_chunked_scan(xs, dt, a_neg, bs, cs, h0):
    bsz, n, nh, hp = xs.shape
    nc = n // SSD_CHUNK
    rep = nh // SSD_GROUPS
    bh = jnp.repeat(bs, rep, axis=2).reshape(bsz, nc, SSD_CHUNK, nh, SSD_STATE)
    ch = jnp.repeat(cs, rep, axis=2).reshape(bsz, nc, SSD_CHUNK, nh, SSD_STATE)
    xq = xs.reshape(bsz, nc, SSD_CHUNK, nh, hp)
    dtq = dt.reshape(bsz, nc, SSD_CHUNK, nh)
    a = jnp.moveaxis((dtq.astype(jnp.float32) * a_neg), 3, 1)
    a_cum = jnp.cumsum(a, axis=-1)
    lower = jnp.tril(jnp.ones((SSD_CHUNK, SSD_CHUNK), dtype=bool))
    seg = a_cum[..., :, None] - a_cum[..., None, :]
    decay = jnp.exp(jnp.where(lower, seg, -jnp.inf))
    xdt = xq * dtq[..., None]
    scores = jnp.einsum('bcqhn,bckhn->bhcqk', ch, bh) * decay
    y_diag = jnp.einsum('bhcqk,bckhp->bcqhp', scores, xdt)
    decay_to_end = jnp.exp(a_cum[..., -1:] - a_cum)
    states = jnp.einsum('bhck,bckhn,bckhp->cbhpn', decay_to_end, bh, xdt)
    chunk_decay = jnp.moveaxis(jnp.exp(a_cum[..., -1]), 2, 0)

    def carry_step(h, inp):
        s, d = inp
        return h * d[..., None, None] + s, h

    h_final, h_start = lax.scan(carry_step, h0, (states, chunk_decay))
    y_off = jnp.einsum('bcqhn,cbhpn,bhcq->bcqhp', ch, h_start, jnp.exp(a_cum))
    return (y_diag + y_off).reshape(bsz, n, nh, hp), h_final


def ssd_mixer(p_ctx, p_lat, conv_w, conv_b, a_log, dt_bias, d_skip, norm_g, ctx_out):
    def prep(p):
        bsz, n = p.shape[:2]
        z, xbc, dt_raw = split_cols(p, (SSD_INNER, SSD_CONV_CH, 2 * SSD_HEADS))
        xbc = jax.nn.silu(depthwise_conv_centred(xbc, conv_w, conv_b))
        xs, bs, cs = split_cols(xbc, (SSD_INNER, SSD_GROUPS * SSD_STATE, SSD_GROUPS * SSD_STATE))
        return (z, xs.reshape(bsz, n, SSD_HEADS, SSD_HEAD_DIM),
                bs.reshape(bsz, n, SSD_GROUPS, SSD_STATE),
                cs.reshape(bsz, n, SSD_GROUPS, SSD_STATE),
                dt_raw.reshape(bsz, n, 2, SSD_HEADS))

    zc, xc, bc, cc, dtc = prep(p_ctx)
    zl, xl, bl, cl, dtl = prep(p_lat)
    y_lat = d_skip[:, None] * xl
    y_ctx = d_skip[:, None] * xc
    for direction in range(2):
        a_neg = -jnp.exp(a_log[direction].astype(jnp.float32))
        dt_c = jax.nn.softplus((dtc[:, :, direction] + dt_bias[direction]).astype(jnp.float32))
        dt_l = jax.nn.softplus((dtl[:, :, direction] + dt_bias[direction]).astype(jnp.float32))
        h0 = jnp.zeros((xc.shape[0], SSD_HEADS, SSD_HEAD_DIM, SSD_STATE), jnp.float32)
        yc, hc = ssd_chunked_scan(_flip(xc, direction), _flip(dt_c, direction), a_neg,
                                  _flip(bc, direction), _flip(cc, direction), h0)
        yl, _ = ssd_chunked_scan(_flip(xl, direction), _flip(dt_l, direction), a_neg,
                                 _flip(bl, direction), _flip(cl, direction), hc)
        y_lat = y_lat + _flip(yl, direction)
        if ctx_out:
            y_ctx = y_ctx + _flip(yc, direction)

    def finish(y, z):
        bsz, n = y.shape[:2]
        y = y.reshape(bsz, n, SSD_INNER)
        return rms_norm(y * jax.nn.silu(z), norm_g).astype(z.dtype)

    return finish(y_lat, zl), (finish(y_ctx, zc) if ctx_out else None)


def diff_attention_mixer(p_ctx, p_lat, cos, sin, qn_g, kn_g, lam_q1, lam_k1, lam_q2, lam_k2,
                         subln_g, lam_init, ctx_out):
    def prep(p, rope):
        bsz, n = p.shape[:2]
        q, k, v = split_cols(p, (DIFF_QK, DIFF_QK, DIFF_WIDTH))
        q = rms_norm(q.reshape(bsz, n, 2 * DIFF_HEADS, DIFF_HD), qn_g)
        k = rms_norm(k.reshape(bsz, n, 2 * DIFF_HEADS, DIFF_HD), kn_g)
        if rope:
            q = apply_axial_rope(q, cos, sin)
            k = apply_axial_rope(k, cos, sin)
        return (q.reshape(bsz, n, DIFF_HEADS, 2, DIFF_HD), k.reshape(bsz, n, DIFF_HEADS, 2, DIFF_HD),
                v.reshape(bsz, n, DIFF_HEADS, 2 * DIFF_HD))

    f32 = jnp.float32
    lam = (jnp.exp(jnp.sum(lam_q1.astype(f32) * lam_k1.astype(f32)))
           - jnp.exp(jnp.sum(lam_q2.astype(f32) * lam_k2.astype(f32))) + lam_init)
    qc, kc, vc = prep(p_ctx, False)
    ql, kl, vl = prep(p_lat, True)
    k_all = jnp.concatenate([kc, kl], axis=1)
    v_all = jnp.concatenate([vc, vl], axis=1)
    scale = DIFF_HD ** -0.5

    def block(qb, k, v):
        s = jnp.einsum('bqhmd,bkhmd->bhmqk', qb, k).astype(f32) * scale
        p = jax.nn.softmax(s, axis=-1)
        a = p[:, :, 0] - lam * p[:, :, 1]
        return jnp.einsum('bhqk,bkhe->bqhe', a.astype(v.dtype), v)

    def finish(o):
        bsz, n = o.shape[:2]
        o = rms_norm(o, subln_g) * (1.0 - lam_init)
        return o.reshape(bsz, n, DIFF_WIDTH)

    out_lat = finish(sweep_query_blocks(lambda qb: block(qb, k_all, v_all), ql))
    out_ctx = finish(block(qc, kc, vc)) if ctx_out else None
    return out_lat, out_ctx


def gqa_mixer(p_ctx, p_lat, cos, sin, qn_g, kn_g, ctx_out):
    def prep(p, rope):
        bsz, n = p.shape[:2]
        q, k, v = split_cols(p, (GQA_WIDTH, GQA_KV_WIDTH, GQA_KV_WIDTH))
        q = rms_norm(q.reshape(bsz, n, GQA_HEADS, GQA_HD), qn_g)
        k = rms_norm(k.reshape(bsz, n, GQA_KV_HEADS, GQA_HD), kn_g)
        v = v.reshape(bsz, n, GQA_KV_HEADS, GQA_HD)
        if rope:
            q = apply_axial_rope(q, cos, sin)
            k = apply_axial_rope(k, cos, sin)
        q = q.reshape(bsz, n, GQA_KV_HEADS, GQA_HEADS // GQA_KV_HEADS, GQA_HD)
        return q, k, v

    qc, kc, vc = prep(p_ctx, False)
    ql, kl, vl = prep(p_lat, True)
    k_all = jnp.concatenate([kc, kl], axis=1)
    v_all = jnp.concatenate([vc, vl], axis=1)
    scale = GQA_HD ** -0.5

    def block(qb, k, v):
        s = jnp.einsum('bqngd,bsnd->bngqs', qb, k).astype(jnp.float32) * scale
        p = jax.nn.softmax(s, axis=-1).astype(v.dtype)
        return jnp.einsum('bngqs,bsnd->bqngd', p, v)

    def finish(o):
        bsz, n = o.shape[:2]
        return o.reshape(bsz, n, GQA_WIDTH)

    out_lat = finish(sweep_query_blocks(lambda qb: block(qb, k_all, v_all), ql))
    out_ctx = finish(block(qc, kc, vc)) if ctx_out else None
    return out_lat, out_ctx


def s5_discretise(lam_re, lam_im, log_dt, b_re, b_im):
    f32 = jnp.float32
    lr, li = lam_re.astype(f32), lam_im.astype(f32)
    step = jnp.exp(log_dt.astype(f32))[:, None]
    mag = jnp.exp(lr * step)
    ar, ai = mag * jnp.cos(li * step), mag * jnp.sin(li * step)
    den = lr * lr + li * li
    fr = ((ar - 1.0) * lr + ai * li) / den
    fi = (ai * lr - (ar - 1.0) * li) / den
    br, bi = b_re.astype(f32), b_im.astype(f32)
    bbr = fr[..., None] * br - fi[..., None] * bi
    bbi = fr[..., None] * bi + fi[..., None] * br
    return ar, ai, bbr, bbi


def complex_affine_combine(e1, e2):
    a1r, a1i, b1r, b1i = e1
    a2r, a2i, b2r, b2i = e2
    ar = a1r * a2r - a1i * a2i
    ai = a1r * a2i + a1i * a2r
    br = a2r * b1r - a2i * b1i + b2r
    bi = a2r * b1i + a2i * b1r + b2i
    return ar, ai, br, bi


def s5_states(u, ar, ai, bbr, bbi, h0r, h0i):
    bsz, n = u.shape[:2]
    ug = jnp.moveaxis(u.reshape(bsz, n, S5_GROUPS, S5_GROUP_CH), 1, 0).astype(jnp.float32)
    bur = jnp.einsum('gpc,lbgc->lbgp', bbr, ug)
    bui = jnp.einsum('gpc,lbgc->lbgp', bbi, ug)
    bur = bur.at[0].add(ar * h0r - ai * h0i)
    bui = bui.at[0].add(ar * h0i + ai * h0r)
    a_r = jnp.broadcast_to(ar, (n, 1) + ar.shape)
    a_i = jnp.broadcast_to(ai, (n, 1) + ai.shape)
    _, _, hr, hi = lax.associative_scan(complex_affine_combine, (a_r, a_i, bur, bui), axis=0)
    return hr, hi


def s5_readout(hr, hi, c_re, c_im):
    n, bsz = hr.shape[:2]
    y = (jnp.einsum('gcp,lbgp->blgc', c_re.astype(jnp.float32), hr)
         - jnp.einsum('gcp,lbgp->blgc', c_im.astype(jnp.float32), hi))
    return y.reshape(bsz, n, S5_WIDTH)


def s5_mixer(u_ctx, u_lat, lam_re, lam_im, log_dt, b_re, b_im, c_re, c_im, d_skip,
             glu_w, glu_b, ctx_out):
    y_lat = d_skip * u_lat
    y_ctx = d_skip * u_ctx
    for direction in range(2):
        ar, ai, bbr, bbi = s5_discretise(lam_re[direction], lam_im[direction], log_dt[direction], b_re, b_im)
        zero = jnp.zeros((u_ctx.shape[0], S5_GROUPS, S5_STATE), jnp.float32)
        cr, ci = s5_states(_flip(u_ctx, direction), ar, ai, bbr, bbi, zero, zero)
        lr_, li_ = s5_states(_flip(u_lat, direction), ar, ai, bbr, bbi, cr[-1], ci[-1])
        y_lat = y_lat + _flip(s5_readout(lr_, li_, c_re, c_im), direction)
        if ctx_out:
            y_ctx = y_ctx + _flip(s5_readout(cr, ci, c_re, c_im), direction)

    def glu(y, dtype):
        val, gate = jnp.split(jax.nn.gelu(y) @ glu_w + glu_b, 2, axis=-1)
        return (val * jax.nn.sigmoid(gate)).astype(dtype)

    return glu(y_lat, u_lat.dtype), (glu(y_ctx, u_ctx.dtype) if ctx_out else None)


def merge_branches(xn, ys, w_gate_l, w_brs, w_out_l):
    merged = None
    for i in range(N_BRANCHES):
        term = jax.nn.sigmoid(xn @ w_gate_l[i]) * (ys[i] @ w_brs[i])
        merged = term if merged is None else merged + term
    return merged @ w_out_l


def setup_inputs(seed: int = 0) -> dict:
    key = jax.random.key(seed)
    ks = iter(jax.random.split(key, 64))
    f32 = jnp.float32
    D = D_MODEL
    L = DEPTH

    def nrm(shape, scale):
        return jax.random.normal(next(ks), shape, f32) * scale

    def gain(shape):
        return 1.0 + nrm(shape, 0.05)

    ssd_a_log = jnp.log(jax.random.uniform(next(ks), (L, 2, SSD_HEADS), f32, 1.0, 16.0))
    dt0 = jnp.exp(jax.random.uniform(next(ks), (L, 2, SSD_HEADS), f32, math.log(1e-3), math.log(1e-1)))
    ssd_dt_bias = dt0 + jnp.log(-jnp.expm1(-dt0))
    n_idx = jnp.arange(S5_STATE, dtype=f32)
    s5_lam_re = -0.5 + nrm((L, 2, S5_GROUPS, S5_STATE), 0.01)
    s5_lam_im = math.pi * n_idx + nrm((L, 2, S5_GROUPS, S5_STATE), 0.01)
    s5_log_dt = jax.random.uniform(next(ks), (L, 2, S5_GROUPS), f32, math.log(1e-3), math.log(1e-1))
    b_scale = (2.0 * S5_GROUP_CH) ** -0.5
    c_scale = (2.0 * S5_STATE) ** -0.5
    return {
        'x': nrm((BATCH, SEQ, D), 1.0),
        'c': nrm((BATCH, D), 1.0),
        'ctx': nrm((BATCH, CTX_LEN, D), 1.0),
        'c_ctx': nrm((D,), 1.0),
        'w_mod': nrm((L, D, 6 * D), 0.5 * D ** -0.5),
        'b_mod': nrm((L, 6 * D), 0.01),
        'norm1_g': gain((L, D)),
        'norm2_g': gain((L, D)),
        'w_in': nrm((L, D, IN_COLS), D ** -0.5),
        'ssd_conv_w': nrm((L, SSD_CONV_W, SSD_CONV_CH), SSD_CONV_W ** -0.5),
        'ssd_conv_b': nrm((L, SSD_CONV_CH), 0.01),
        'ssd_a_log': ssd_a_log,
        'ssd_dt_bias': ssd_dt_bias,
        'ssd_d': gain((L, SSD_HEADS)),
        'ssd_norm_g': gain((L, SSD_INNER)),
        'diff_qn_g': gain((L, DIFF_HD)),
        'diff_kn_g': gain((L, DIFF_HD)),
        'diff_lam_q1': nrm((L, DIFF_HD), 0.1),
        'diff_lam_k1': nrm((L, DIFF_HD), 0.1),
        'diff_lam_q2': nrm((L, DIFF_HD), 0.1),
        'diff_lam_k2': nrm((L, DIFF_HD), 0.1),
        'diff_subln_g': gain((L, 2 * DIFF_HD)),
        'gqa_qn_g': gain((L, GQA_HD)),
        'gqa_kn_g': gain((L, GQA_HD)),
        's5_lam_re': s5_lam_re,
        's5_lam_im': s5_lam_im,
        's5_log_dt': s5_log_dt,
        's5_b_re': nrm((L, S5_GROUPS, S5_STATE, S5_GROUP_CH), b_scale),
        's5_b_im': nrm((L, S5_GROUPS, S5_STATE, S5_GROUP_CH), b_scale),
        's5_c_re': nrm((L, S5_GROUPS, S5_GROUP_CH, S5_STATE), c_scale),
        's5_c_im': nrm((L, S5_GROUPS, S5_GROUP_CH, S5_STATE), c_scale),
        's5_d': nrm((L, S5_WIDTH), 1.0),
        's5_glu_w': nrm((L, S5_WIDTH, 2 * S5_WIDTH), S5_WIDTH ** -0.5),
        's5_glu_b': nrm((L, 2 * S5_WIDTH), 0.01),
        'w_gate': nrm((L, N_BRANCHES, D, D), D ** -0.5),
        'w_br_ssd': nrm((L, SSD_INNER, D), SSD_INNER ** -0.5),
        'w_br_diff': nrm((L, DIFF_WIDTH, D), DIFF_WIDTH ** -0.5),
        'w_br_gqa': nrm((L, GQA_WIDTH, D), GQA_WIDTH ** -0.5),
        'w_br_s5': nrm((L, S5_WIDTH, D), S5_WIDTH ** -0.5),
        'w_out': nrm((L, D, D), D ** -0.5),
        'ffn_w_gate_up': nrm((L, D, 2 * FFN_HIDDEN), D ** -0.5),
        'ffn_w_down': nrm((L, FFN_HIDDEN, D), FFN_HIDDEN ** -0.5),
    }


def reference(x, c, ctx, c_ctx, w_mod, b_mod, norm1_g, norm2_g, w_in,
              ssd_conv_w, ssd_conv_b, ssd_a_log, ssd_dt_bias, ssd_d, ssd_norm_g,
              diff_qn_g, diff_kn_g, diff_lam_q1, diff_lam_k1, diff_lam_q2, diff_lam_k2, diff_subln_g,
              gqa_qn_g, gqa_kn_g,
              s5_lam_re, s5_lam_im, s5_log_dt, s5_b_re, s5_b_im, s5_c_re, s5_c_im, s5_d,
              s5_glu_w, s5_glu_b,
              w_gate, w_br_ssd, w_br_diff, w_br_gqa, w_br_s5, w_out,
              ffn_w_gate_up, ffn_w_down):
    n_tok = x.shape[1]
    cos, sin = axial_rope_tables(n_tok, GQA_HD)
    h_lat, h_ctx = x, ctx
    for layer in range(DEPTH):
        ctx_out = layer < DEPTH - 1
        mod_lat = jax.nn.silu(c) @ w_mod[layer] + b_mod[layer]
        mod_ctx = jax.nn.silu(c_ctx) @ w_mod[layer] + b_mod[layer]
        sh1, sc1, g1, sh2, sc2, g2 = jnp.split(mod_lat[:, None, :], 6, axis=-1)
        csh1, csc1, cg1, csh2, csc2, cg2 = jnp.split(mod_ctx, 6, axis=-1)

        xn_lat = modulate(rms_norm(h_lat, norm1_g[layer]), sh1, sc1)
        xn_ctx = modulate(rms_norm(h_ctx, norm1_g[layer]), csh1, csc1)
        pa_l, pb_l, pc_l, pd_l = split_cols(xn_lat @ w_in[layer], IN_SPLITS)
        pa_c, pb_c, pc_c, pd_c = split_cols(xn_ctx @ w_in[layer], IN_SPLITS)

        ya_l, ya_c = ssd_mixer(pa_c, pa_l, ssd_conv_w[layer], ssd_conv_b[layer], ssd_a_log[layer],
                               ssd_dt_bias[layer], ssd_d[layer], ssd_norm_g[layer], ctx_out)
        lam_init = 0.8 - 0.6 * math.exp(-0.3 * layer)
        yb_l, yb_c = diff_attention_mixer(pb_c, pb_l, cos, sin, diff_qn_g[layer], diff_kn_g[layer],
                                          diff_lam_q1[layer], diff_lam_k1[layer], diff_lam_q2[layer],
                                          diff_lam_k2[layer], diff_subln_g[layer], lam_init, ctx_out)
        yc_l, yc_c = gqa_mixer(pc_c, pc_l, cos, sin, gqa_qn_g[layer], gqa_kn_g[layer], ctx_out)
        yd_l, yd_c = s5_mixer(pd_c, pd_l, s5_lam_re[layer], s5_lam_im[layer], s5_log_dt[layer],
                              s5_b_re[layer], s5_b_im[layer], s5_c_re[layer], s5_c_im[layer],
                              s5_d[layer], s5_glu_w[layer], s5_glu_b[layer], ctx_out)
        w_brs = (w_br_ssd[layer], w_br_diff[layer], w_br_gqa[layer], w_br_s5[layer])
        h_lat = h_lat + g1 * merge_branches(xn_lat, (ya_l, yb_l, yc_l, yd_l), w_gate[layer], w_brs, w_out[layer])

        xf_lat = modulate(rms_norm(h_lat, norm2_g[layer]), sh2, sc2)
        h_lat = h_lat + g2 * swiglu(xf_lat, ffn_w_gate_up[layer], ffn_w_down[layer])

        if ctx_out:
            h_ctx = h_ctx + cg1 * merge_branches(xn_ctx, (ya_c, yb_c, yc_c, yd_c), w_gate[layer], w_brs, w_out[layer])
            xf_ctx = modulate(rms_norm(h_ctx, norm2_g[layer]), csh2, csc2)
            h_ctx = h_ctx + cg2 * swiglu(xf_ctx, ffn_w_gate_up[layer], ffn_w_down[layer])
    return h_lat
```

```python
import contextlib
import math
import numpy as np
import concourse.bass as bass
import concourse.mybir as mybir
from concourse.bass_utils import run_bass_kernel_spmd

F32 = mybir.dt.float32
BF16 = mybir.dt.bfloat16
AF = mybir.ActivationFunctionType
ALU = mybir.AluOpType
AX = mybir.AxisListType


class Tile:
    def __init__(self, S, h, name, psum=False):
        self.S = S
        self.h = h
        self.name = name
        self.psum = psum
        self.lw = None
        self.rd = {}
        self.dsem = None

    def __getitem__(self, k):
        return self.h[k]


class Sched:
    SAME_ENGINE_SYNC = ("dve", "act", "pool")

    def __init__(self, nc):
        self.nc = nc
        self.E = {"pe": nc.tensor, "dve": nc.vector, "act": nc.scalar, "pool": nc.gpsimd, "sp": nc.sync}
        self.sems = {}
        for k in self.E:
            self.sems[k] = [nc.alloc_semaphore("sem_" + k), 0]
        self.waited = {k: {} for k in self.E}
        self.free_dsems = []
        self.n_dsem = 0
        self.stack = []
        self.ninstr = 0
        self.inflight = {k: [] for k in self.E}
        self.MAXOUT = 6

    def sb(self, stack, name, shape, dtype):
        self.nalloc = getattr(self, "nalloc", 0) + 1
        name = "%s_%d" % (name, self.nalloc)
        h = stack.enter_context(self.nc.sbuf_tensor(name, list(shape), dtype))
        t = Tile(self, h, name)
        if not hasattr(self, "tiles"):
            self.tiles = []
        self.tiles.append(t)
        return t

    def mark(self):
        return len(getattr(self, "tiles", []))

    def release_since(self, mk):
        self.release(self.tiles[mk:])
        del self.tiles[mk:]

    def ps(self, stack, name, shape, dtype=F32):
        h = stack.enter_context(self.nc.psum_tensor(name, list(shape), dtype))
        return Tile(self, h, name, psum=True)

    def _dsem(self, t):
        if t.dsem is None:
            if self.free_dsems:
                t.dsem = self.free_dsems.pop()
            else:
                key = "d%d" % self.n_dsem
                self.n_dsem += 1
                self.sems[key] = [self.nc.alloc_semaphore("sem_" + key), 0]
                t.dsem = key
        return t.dsem

    def release(self, tiles):
        for t in tiles:
            if t.dsem is not None:
                self.free_dsems.append(t.dsem)
                t.dsem = None

    def _wait(self, eng, key, val):
        if val <= 0:
            return
        w = self.waited[eng]
        if w.get(key, 0) >= val:
            return
        self.E[eng].wait_ge(self.sems[key][0], val)
        w[key] = val
        self.ninstr += 1

    def _deps(self, eng, reads, writes):
        deps = {}
        def add(d):
            if d is None:
                return
            k, v = d
            if k == eng and eng not in self.SAME_ENGINE_SYNC:
                return
            if deps.get(k, 0) < v:
                deps[k] = v
        for t in reads:
            add(t.lw)
        for t in writes:
            add(t.lw)
            for k, v in t.rd.items():
                add((k, v))
        for k, v in deps.items():
            self._wait(eng, k, v)

    def op(self, eng, fn, reads=(), writes=()):
        self._deps(eng, reads, writes)
        ins = fn(self.E[eng])
        s = self.sems[eng]
        s[1] += 1
        ins.then_inc(s[0], 1)
        self.ninstr += 1
        me = (eng, s[1])
        for t in reads:
            if t.rd.get(eng, 0) < s[1]:
                t.rd[eng] = s[1]
        for t in writes:
            t.lw = me
            t.rd = {}
        return ins

    def dma(self, q, out, in_, reads=(), writes=(), **kw):
        self._deps(q, reads, writes)
        tl = (list(writes) + list(reads))
        assert tl, "dma needs an sbuf tile for its semaphore"
        key = self._dsem(tl[0])
        self._throttle(q)
        ins = self.E[q].dma_start(out=out, in_=in_, **kw)
        s = self.sems[key]
        s[1] += 16
        ins.then_inc(s[0], 16)
        self.ninstr += 1
        me = (key, s[1])
        self.inflight[q].append(me)
        for t in reads:
            if t.rd.get(key, 0) < s[1]:
                t.rd[key] = s[1]
        for t in writes:
            t.lw = me
            t.rd = {}
        return ins

    def _throttle(self, q):
        fl = self.inflight[q]
        while len(fl) >= self.MAXOUT:
            k, v = fl.pop(0)
            self._wait(q, k, v)

    def dma_dram(self, q, out, in_, **kw):
        key = "dd_" + q
        if key not in self.sems:
            self.sems[key] = [self.nc.alloc_semaphore("sem_" + key), 0]
        self._throttle(q)
        ins = self.E[q].dma_start(out=out, in_=in_, **kw)
        s = self.sems[key]
        s[1] += 16
        ins.then_inc(s[0], 16)
        self.ninstr += 1
        self.inflight[q].append((key, s[1]))
        return ins

    def barrier(self):
        for eng in self.E:
            for key, (h, cnt) in self.sems.items():
                if key == eng and eng not in self.SAME_ENGINE_SYNC and eng != "sp":
                    continue
                self._wait(eng, key, cnt)

D = 1024
NCTX = 256
NLAT = 8192
T = NCTX + NLAT
KC = 8
TILES = [(0, 256)] + [(256 + 512 * i, 512) for i in range(16)]
N_OUT_TILES_LAST = 8
EPS = 1e-6
INC = 3984
C_Z, C_XBC, C_DT, C_DQ, C_DK, C_DV, C_GQ, C_GK, C_GV, C_U = 0, 512, 1280, 1296, 1808, 2320, 2832, 3344, 3472, 3600
FH = 2816
NV = 115
NR = 808
CI_ID, CI_BLK64, CI_ROT, CI_TRI0, CI_TRI1, CI_MN0, CI_MN1, CI_SEL, CI_ONES = range(9)
NCONST = 9


def make_consts():
    c = np.zeros((NCONST, 128, 128), np.float32)
    c[CI_ID] = np.eye(128)
    c[CI_BLK64, :64, :64] = 1.0
    c[CI_BLK64, 64:, 64:] = 1.0
    for hb in (0, 64):
        for q0 in (0, 32):
            for i in range(16):
                c[CI_ROT, hb + q0 + 16 + i, hb + q0 + i] = -1.0
                c[CI_ROT, hb + q0 + i, hb + q0 + 16 + i] = 1.0
    k = np.arange(128)
    c[CI_TRI0] = (k[:, None] <= k[None, :])
    c[CI_TRI1] = (k[:, None] >= k[None, :])
    c[CI_MN0] = np.where(k[:, None] <= k[None, :], 0.0, -1e30)
    c[CI_MN1] = np.where(k[:, None] >= k[None, :], 0.0, -1e30)
    c[CI_SEL, 64, :] = 1.0
    c[CI_ONES] = 1.0
    return c


def rope_tables(flip):
    n_rows = NLAT // 64
    rows = np.repeat(np.arange(n_rows, dtype=np.float32), 64)
    cols = np.tile(np.arange(64, dtype=np.float32), n_rows)
    inv = (10000.0 ** (-np.arange(16, dtype=np.float32) / 16)).astype(np.float32)
    ar = rows[:, None] * inv
    ac = cols[:, None] * inv
    ang = np.concatenate([ar, ar, ac, ac], -1)
    cos = np.cos(ang).astype(np.float32)
    sin = np.sin(ang).astype(np.float32)
    if flip:
        cos = cos[::-1]
        sin = sin[::-1]
    ct = np.ones((128, T), np.float32)
    st = np.zeros((128, T), np.float32)
    ct[:64, NCTX:] = cos.T
    ct[64:, NCTX:] = cos.T
    st[:64, NCTX:] = sin.T
    st[64:, NCTX:] = sin.T
    return ct, st


def dview(t, pattern, **kw):
    return t.ap().rearrange(pattern, **kw)


def build(n_layers=2, phases=None, dbg=(), last_tiles=N_OUT_TILES_LAST, force_full=False, dbg_in=()):
    nc = bass.Bass("TRN2", target_bir_lowering=False)
    S = Sched(nc)
    dbg = set(dbg)
    allph = phases is None

    def want(p):
        return allph or p in phases

    def dram(name, shape, dt, kind=None):
        if kind is None:
            kind = "ExternalOutput" if name in dbg else ("ExternalInput" if name in dbg_in else "Internal")
        return nc.dram_tensor(name, list(shape), dt, kind=kind)

    def din(name, shape, dt=F32):
        return nc.dram_tensor(name, list(shape), dt, kind="ExternalInput")

    hT_in = din("hT0", [D, T])
    c2 = din("c2", [128, KC, 2])
    w_mod = din("w_mod", [2, D, 6 * D])
    w_in = din("w_in", [2, D, INC])
    w_gate = din("w_gate", [2, 4, D, D])
    w_br = [din("w_br_ssd", [2, 512, D]), din("w_br_diff", [2, 512, D]), din("w_br_gqa", [2, 512, D]),
            din("w_br_s5", [2, 384, D])]
    BRK = [4, 4, 4, 3]
    w_out = din("w_out", [2, D, D])
    w_gu = din("ffn_w_gate_up", [2, D, 2 * FH])
    w_dn = din("ffn_w_down", [2, FH, D])
    glu_w = din("s5_glu_w", [2, 384, 768])
    vec128 = din("vec128", [2, 128, NV])
    rowvecs = din("rowvecs", [2, NR])
    rope_c = din("rope_cos", [128, T])
    rope_s = din("rope_sin", [128, T])
    consts = din("consts", [NCONST, 128, 128])
    s5_lr = din("s5_lr", [2, 2, 128, 12])
    s5_li = din("s5_li", [2, 2, 128, 12])
    s5_ldt = din("s5_ldt", [2, 2, 128, 12])
    s5_B = din("s5_Bblk", [2, 2, 12, 128, 128])
    s5_C = din("s5_Cblk", [2, 2, 12, 128, 128])
    y_out = nc.dram_tensor("y", [D, last_tiles * 512], F32, kind="ExternalOutput")

    hT = [hT_in, dram("hT1", [D, T], F32), dram("hT2", [D, T], F32)]
    xnT = dram("xnT", [128, KC, T], BF16)
    xbcT = dram("xbcT", [128, 6, T], F32)
    qkT = dram("qkT", [128, 13, T], BF16)
    uT = dram("uT", [128, 3, T], F32)
    zt = dram("zt", [T, 512], F32)
    vaug = dram("vaug", [T, 10, 65], BF16)
    dtt = dram("dtt", [T, 16], F32)
    yaT = dram("yaT", [128, 4, T], BF16)
    ybT = dram("ybT", [128, 4, T], BF16)
    ycT = dram("ycT", [128, 4, T], BF16)
    ydT = dram("ydT", [128, 3, T], BF16)
    xtok = dram("xtok", [T, 512], BF16)
    btok = dram("btok", [T, 128], BF16)
    bcT = dram("bcT", [2, 2, 64, T], BF16)
    yssd = dram("yssd", [T, 512], F32)
    yssd1 = dram("yssd1", [T, 512], F32)
    ys5 = dram("ys5", [128, 3, T], F32)
    modv_d = dram("modv_d", [2, 128, 48, 2], F32)
    wg_bf = dram("wg_bf", [4 * 8, 128, 8, 128], BF16)
    wb_bf = [dram("wb_bf%d" % i, [8, 128, BRK[i], 128], BF16) for i in range(4)]
    wo_bf = dram("wo_bf", [8, 128, 8, 128], BF16)
    wgu_bf = dram("wgu_bf", [44, 128, 8, 128], BF16)
    wdn_bf = dram("wdn_bf", [8, 128, 22, 128], BF16)

    with contextlib.ExitStack() as gst:
        cst_f = S.sb(gst, "cst_f", [128, NCONST, 128], F32)
        cst_b = S.sb(gst, "cst_b", [128, NCONST, 128], BF16)
        S.dma("sp", cst_f[:], consts.ap().rearrange("c p m -> p c m"), writes=[cst_f])
        S.op("dve", lambda e: e.tensor_copy(cst_b[:], cst_f[:]), reads=[cst_f], writes=[cst_b])
        psall = gst.enter_context(nc.psum_tensor("psall", [128, 4096], F32))
        PS = [Tile(S, psall[:, i * 512:(i + 1) * 512], "ps%d" % i, psum=True) for i in range(8)]

        def cF(i):
            return cst_f[:, i, :]

        def cB(i):
            return cst_b[:, i, :]

        for layer in range(n_layers):
            last = (layer == n_layers - 1) and not force_full
            out_tiles = TILES[1:1 + last_tiles] if last else TILES
            h_src = hT[layer]
            h_dst = y_out if last else hT[layer + 1]
            lam_init = 0.8 - 0.6 * math.exp(-0.3 * layer)

            if want("W"):
                qs = ["pool"]
                n = 0
                for i in range(4):
                    for j in range(8):
                        S.dma_dram("pool", wg_bf.ap()[i * 8 + j],
                                   w_gate.ap()[layer, i, :, j * 128:(j + 1) * 128].rearrange("(kc p) m -> p kc m", p=128))
                    for j in range(8):
                        S.dma_dram("pool", wb_bf[i].ap()[j],
                                   w_br[i].ap()[layer, :, j * 128:(j + 1) * 128].rearrange("(kc p) m -> p kc m", p=128))
                for j in range(8):
                    S.dma_dram("pool", wo_bf.ap()[j],
                               w_out.ap()[layer, :, j * 128:(j + 1) * 128].rearrange("(kc p) m -> p kc m", p=128))
                    S.dma_dram("pool", wdn_bf.ap()[j],
                               w_dn.ap()[layer, :, j * 128:(j + 1) * 128].rearrange("(kc p) m -> p kc m", p=128))
                for j in range(44):
                    S.dma_dram("pool", wgu_bf.ap()[j],
                               w_gu.ap()[layer, :, j * 128:(j + 1) * 128].rearrange("(kc p) m -> p kc m", p=128))

            lmk = S.mark()
            with contextlib.ExitStack() as lst:
                v128 = S.sb(lst, "v128", [128, NV], F32)
                rows = S.sb(lst, "rows", [128, NR], F32)
                modv = S.sb(lst, "modv", [128, 48, 2], F32)
                A1 = S.sb(lst, "A1", [128, 8, 2], F32)
                A2 = S.sb(lst, "A2", [128, 8, 2], F32)
                lamc = S.sb(lst, "lamc", [128, 4], F32)
                subg = S.sb(lst, "subg", [128, 2], F32)
                aneg = S.sb(lst, "aneg", [128, 16], F32)
                S.dma("sp", v128[:], vec128.ap()[layer], writes=[v128])
                S.dma("sp", rows[:], rowvecs.ap()[layer].partition_broadcast(128), writes=[rows])
                if want("P"):
                    with contextlib.ExitStack() as st:
                        sc = S.sb(st, "sc", [128, KC, 2], F32)
                        S.dma("sp", sc[:], c2.ap(), writes=[sc])
                        S.op("act", lambda e: e.activation(sc[:], sc[:], AF.Silu), reads=[sc], writes=[sc])
                        wm = [S.sb(st, "wm%d" % i, [128, KC, 512], F32) for i in range(2)]
                        pm = PS[0]
                        for blk in range(12):
                            w = wm[blk % 2]
                            S.dma("sp", w[:], w_mod.ap()[layer, :, blk * 512:(blk + 1) * 512].rearrange("(kc p) n -> p kc n", p=128),
                                  writes=[w])
                            for jj in range(4):
                                j = blk * 4 + jj
                                for kc in range(KC):
                                    S.op("pe", lambda e, j=j, jj=jj, kc=kc, w=w: e.matmul(
                                        pm[:, j * 2:(j + 1) * 2], w[:, kc, jj * 128:(jj + 1) * 128], sc[:, kc, :],
                                        start=(kc == 0), stop=(kc == KC - 1)), reads=[w, sc], writes=[pm])
                        S.op("dve", lambda e: e.tensor_tensor(
                            modv[:], pm[:, 0:96].rearrange("p (j s) -> p j s", s=2),
                            v128[:, 16:64].unsqueeze(2).broadcast_to([128, 48, 2]), ALU.add),
                            reads=[pm, v128], writes=[modv])
                        S.dma("pool", modv_d.ap()[layer], modv[:], reads=[modv])
                else:
                    S.dma("sp", modv[:], modv_d.ap()[layer], writes=[modv])
                for (A, gcol, scj) in ((A1, 0, 8), (A2, 8, 32)):
                    S.op("dve", lambda e, A=A, scj=scj: e.tensor_scalar(A[:], modv[:, scj:scj + 8, :], 1.0, None, ALU.add),
                         reads=[modv], writes=[A])
                    S.op("dve", lambda e, A=A, gcol=gcol: e.tensor_tensor(
                        A[:], A[:], v128[:, gcol:gcol + 8].unsqueeze(2).broadcast_to([128, 8, 2]), ALU.mult),
                        reads=[A, v128], writes=[A])
                with contextlib.ExitStack() as st:
                    tmp = S.sb(st, "lamtmp", [128, 128], F32)
                    red = S.sb(st, "lamred", [128, 2], F32)
                    R0 = 552
                    S.op("dve", lambda e: e.tensor_tensor(tmp[:, 0:64], rows[:, R0:R0 + 64], rows[:, R0 + 64:R0 + 128], ALU.mult),
                         reads=[rows], writes=[tmp])
                    S.op("dve", lambda e: e.tensor_tensor(tmp[:, 64:128], rows[:, R0 + 128:R0 + 192], rows[:, R0 + 192:R0 + 256], ALU.mult),
                         reads=[rows, tmp], writes=[tmp])
                    S.op("dve", lambda e: e.reduce_sum(red[:], tmp[:].rearrange("p (a b) -> p a b", a=2), AX.X),
                         reads=[tmp], writes=[red])
                    S.op("act", lambda e: e.activation(red[:], red[:], AF.Exp), reads=[red], writes=[red])
                    S.op("dve", lambda e: e.tensor_tensor(lamc[:, 0:1], red[:, 1:2], red[:, 0:1], ALU.subtract),
                         reads=[red], writes=[lamc])
                    S.op("dve", lambda e: e.tensor_scalar(lamc[:, 0:1], lamc[:, 0:1], -lam_init, None, ALU.add),
                         reads=[lamc], writes=[lamc])
                    S.op("dve", lambda e: e.tensor_scalar(subg[:], v128[:, 68:70], 1.0 - lam_init, None, ALU.mult),
                         reads=[v128], writes=[subg])
                    S.op("act", lambda e: e.activation(aneg[:], rows[:, 0:16], AF.Exp), reads=[rows], writes=[aneg])
                    S.op("dve", lambda e: e.tensor_scalar(aneg[:], aneg[:], -1.0, None, ALU.mult), reads=[aneg], writes=[aneg])
                    S.barrier()

                for pname, pfn in (("1", phase1), ("2", phase2), ("3", phase3), ("4", phase4), ("5", phase5)):
                    if want(pname):
                        mk = S.mark()
                        pfn(nc, S, locals())
                        S.barrier()
                        S.release_since(mk)
                S.barrier()
            S.release_since(lmk)
        S.barrier()
    print("ninstr", S.ninstr, "dsems", S.n_dsem)
    return nc


def phase1(nc, S, G):
    PS, layer, v128, modv, A1 = G["PS"], G["layer"], G["v128"], G["modv"], G["A1"]
    cF, cB = G["cF"], G["cB"]
    h_src = G["h_src"]
    with contextlib.ExitStack() as st:
        win = S.sb(st, "win", [128, KC, INC], BF16)
        for c0 in range(0, INC, 512):
            c1 = min(INC, c0 + 512)
            S.dma("pool", win[:, :, c0:c1],
                  G["w_in"].ap()[layer, :, c0:c1].rearrange("(kc p) n -> p kc n", p=128), writes=[win])
        hts = [S.sb(st, "ht%d" % i, [128, KC, 512], F32) for i in range(2)]
        sq = S.sb(st, "sq", [128, KC, 512], BF16)
        rstd = S.sb(st, "rstd", [128, 512], F32)
        xns = [S.sb(st, "xn%d" % i, [128, KC, 512], BF16) for i in range(2)]
        cos_t = [S.sb(st, "cos%d" % i, [128, 512], F32) for i in range(2)]
        sin_t = [S.sb(st, "sin%d" % i, [128, 512], F32) for i in range(2)]
        stg = [S.sb(st, "stg%d" % i, [128, 512], F32) for i in range(4)]
        qsq = [S.sb(st, "qsq%d" % i, [128, 512], BF16) for i in range(2)]
        qy = [S.sb(st, "qy%d" % i, [128, 512], BF16) for i in range(2)]
        qr = [S.sb(st, "qr%d" % i, [128, 512], F32) for i in range(2)]
        qt1 = [S.sb(st, "qt1%d" % i, [128, 512], F32) for i in range(2)]
        qt2 = [S.sb(st, "qt2%d" % i, [128, 512], F32) for i in range(2)]
        qo = [S.sb(st, "qo%d" % i, [128, 512], BF16) for i in range(2)]
        vst = [S.sb(st, "vst%d" % i, [128, 10, 65], BF16) for i in range(2)]
        dst = [S.sb(st, "dst%d" % i, [128, 16], F32) for i in range(2)]
        for v in vst:
            S.op("pool", lambda e, v=v: e.memset(v[:], 1.0), writes=[v])
        nstg = 0
        nq = 0
        own_set = set(G["out_tiles"])
        for ti, (t0, W) in enumerate(TILES):
            s = 1 if t0 < NCTX else 0
            own = (t0, W) in own_set
            ht = hts[ti % 2]
            xn = xns[ti % 2]
            ct, sn = cos_t[ti % 2], sin_t[ti % 2]
            S.dma("sp", ht[:, :, 0:W], h_src.ap().rearrange("(kc p) t -> p kc t", p=128)[:, :, t0:t0 + W], writes=[ht])
            S.dma("sp", ct[:, 0:W], G["rope_c"].ap()[:, t0:t0 + W], writes=[ct])
            S.dma("sp", sn[:, 0:W], G["rope_s"].ap()[:, t0:t0 + W], writes=[sn])
            S.op("act", lambda e: e.activation(sq[:, :, 0:W], ht[:, :, 0:W], AF.Square), reads=[ht], writes=[sq])
            for kc in range(KC):
                S.op("pe", lambda e, kc=kc: e.matmul(PS[0][:, 0:W], cB(CI_ONES), sq[:, kc, 0:W], start=(kc == 0), stop=(kc == KC - 1)),
                     reads=[sq], writes=[PS[0]])
            S.op("act", lambda e: e.activation(rstd[:, 0:W], PS[0][:, 0:W], AF.Sqrt, bias=EPS, scale=1.0 / D), reads=[PS[0]], writes=[rstd])
            S.op("dve", lambda e: e.reciprocal(rstd[:, 0:W], rstd[:, 0:W]), reads=[rstd], writes=[rstd])
            S.op("dve", lambda e: e.tensor_tensor(ht[:, :, 0:W], ht[:, :, 0:W], rstd[:, 0:W].unsqueeze(1).broadcast_to([128, KC, W]), ALU.mult),
                 reads=[ht, rstd], writes=[ht])
            for kc in range(KC):
                S.op("act", lambda e, kc=kc: e.activation(xn[:, kc, 0:W], ht[:, kc, 0:W], AF.Identity,
                                                          bias=modv[:, kc, s:s + 1], scale=A1[:, kc, s:s + 1]),
                     reads=[ht, modv, A1], writes=[xn])
            if own:
                S.dma("pool", G["xnT"].ap()[:, :, t0:t0 + W], xn[:, :, 0:W], reads=[xn])

            def fm_group(col0, pst):
                for kc in range(KC):
                    S.op("pe", lambda e, kc=kc: e.matmul(pst[:, 0:W], win[:, kc, col0:col0 + 128], xn[:, kc, 0:W],
                                                         start=(kc == 0), stop=(kc == KC - 1)), reads=[win, xn], writes=[pst])
            ng = 0
            for c in range(6):
                pst = PS[1 + ng % 2]; ng += 1
                fm_group(C_XBC + 128 * c, pst)
                sg = stg[nstg % 4]; nstg += 1
                S.op("act", lambda e, sg=sg, pst=pst: e.activation(sg[:, 0:W], pst[:, 0:W], AF.Copy), reads=[pst], writes=[sg])
                S.dma("pool", G["xbcT"].ap()[:, c, t0:t0 + W], sg[:, 0:W], reads=[sg])
            for c in range(3):
                pst = PS[1 + ng % 2]; ng += 1
                fm_group(C_U + 128 * c, pst)
                sg = stg[nstg % 4]; nstg += 1
                S.op("dve", lambda e, sg=sg, pst=pst: e.tensor_copy(sg[:, 0:W], pst[:, 0:W]), reads=[pst], writes=[sg])
                S.dma("pool", G["uT"].ap()[:, c, t0:t0 + W], sg[:, 0:W], reads=[sg])
            for c in range(13):
                if not own and (c < 4 or 8 <= c < 12):
                    continue
                if c < 4:
                    col0, gcol = C_DQ + 128 * c, 64
                elif c < 8:
                    col0, gcol = C_DK + 128 * (c - 4), 65
                elif c < 12:
                    col0, gcol = C_GQ + 128 * (c - 8), 66
                else:
                    col0, gcol = C_GK, 67
                pst = PS[1 + ng % 2]; ng += 1
                fm_group(col0, pst)
                b = nq % 2; nq += 1
                a_sq, a_y, a_r, a_t1, a_t2, a_o = qsq[b], qy[b], qr[b], qt1[b], qt2[b], qo[b]
                S.op("act", lambda e: e.activation(a_sq[:, 0:W], pst[:, 0:W], AF.Square), reads=[pst], writes=[a_sq])
                S.op("act", lambda e: e.activation(a_y[:, 0:W], pst[:, 0:W], AF.Identity, scale=v128[:, gcol:gcol + 1]),
                     reads=[pst, v128], writes=[a_y])
                S.op("pe", lambda e: e.matmul(PS[3][:, 0:W], cB(CI_BLK64), a_sq[:, 0:W], start=True, stop=True), reads=[a_sq], writes=[PS[3]])
                S.op("pe", lambda e: e.matmul(PS[4][:, 0:W], cB(CI_ROT), a_y[:, 0:W], start=True, stop=True), reads=[a_y], writes=[PS[4]])
                S.op("act", lambda e: e.activation(a_r[:, 0:W], PS[3][:, 0:W], AF.Sqrt, bias=EPS, scale=1.0 / 64), reads=[PS[3]], writes=[a_r])
                S.op("dve", lambda e: e.reciprocal(a_r[:, 0:W], a_r[:, 0:W]), reads=[a_r], writes=[a_r])
                S.op("pool", lambda e: e.tensor_tensor(a_t1[:, 0:W], a_y[:, 0:W], ct[:, 0:W], ALU.mult), reads=[a_y, ct], writes=[a_t1])
                S.op("dve", lambda e: e.tensor_tensor(a_t2[:, 0:W], PS[4][:, 0:W], sn[:, 0:W], ALU.mult), reads=[PS[4], sn], writes=[a_t2])
                S.op("pool", lambda e: e.tensor_tensor(a_t1[:, 0:W], a_t1[:, 0:W], a_t2[:, 0:W], ALU.add), reads=[a_t1, a_t2], writes=[a_t1])
                S.op("dve", lambda e: e.tensor_tensor(a_o[:, 0:W], a_t1[:, 0:W], a_r[:, 0:W], ALU.mult), reads=[a_t1, a_r], writes=[a_o])
                S.dma("pool", G["qkT"].ap()[:, c, t0:t0 + W], a_o[:, 0:W], reads=[a_o])

            for blk in range(W // 128):
                bs = slice(blk * 128, (blk + 1) * 128)
                r0 = t0 + blk * 128
                vs_, ds_ = vst[blk % 2], dst[blk % 2]
                for kc in range(KC):
                    fl = dict(start=(kc == 0), stop=(kc == KC - 1))
                    if own:
                        S.op("pe", lambda e: e.matmul(PS[5][:, 0:512], xn[:, kc, bs], win[:, kc, C_Z:C_Z + 512], **fl),
                             reads=[win, xn], writes=[PS[5]])
                    S.op("pe", lambda e: e.matmul(PS[6][:, 0:512], xn[:, kc, bs], win[:, kc, C_DV:C_DV + 512], **fl),
                         reads=[win, xn], writes=[PS[6]])
                    S.op("pe", lambda e: e.matmul(PS[7][:, 0:128], xn[:, kc, bs], win[:, kc, C_GV:C_GV + 128], **fl),
                         reads=[win, xn], writes=[PS[7]])
                    S.op("pe", lambda e: e.matmul(PS[0][:, 0:16], xn[:, kc, bs], win[:, kc, C_DT:C_DT + 16], **fl),
                         reads=[win, xn], writes=[PS[0]])
                if own:
                    sg = stg[nstg % 4]; nstg += 1
                    S.op("act", lambda e, sg=sg: e.activation(sg[:, :], PS[5][:, :], AF.Copy), reads=[PS[5]], writes=[sg])
                    S.dma("pool", G["zt"].ap()[r0:r0 + 128, :], sg[:, :], reads=[sg])
                S.op("dve", lambda e: e.tensor_copy(vs_[:, 0:8, 0:64], PS[6][:, :].rearrange("p (g c) -> p g c", c=64)),
                     reads=[PS[6]], writes=[vs_])
                S.op("dve", lambda e: e.tensor_copy(vs_[:, 8:10, 0:64], PS[7][:, 0:128].rearrange("p (g c) -> p g c", c=64)),
                     reads=[PS[7]], writes=[vs_])
                S.op("act", lambda e: e.activation(ds_[:, :], PS[0][:, 0:16], AF.Copy), reads=[PS[0]], writes=[ds_])
                S.dma("pool", G["vaug"].ap()[r0:r0 + 128], vs_[:], reads=[vs_])
                S.dma("pool", G["dtt"].ap()[r0:r0 + 128, :], ds_[:, :], reads=[ds_])


def phase2(nc, S, G):
    PS, layer, v128 = G["PS"], G["layer"], G["v128"]
    cF, cB, lamc, subg = G["cF"], G["cB"], G["lamc"], G["subg"]
    out_tiles = G["out_tiles"]
    qkT, vaug = G["qkT"], G["vaug"]
    NKB = T // 128
    with contextlib.ExitStack() as st:
        kTs = [S.sb(st, "kT%d" % i, [128, T], BF16) for i in range(2)]
        vts = [S.sb(st, "vt%d" % i, [128, NKB, 2, 65], BF16) for i in range(2)]
        qts = [S.sb(st, "qt%d" % i, [128, 2, 512], BF16) for i in range(2)]
        pTw = [S.sb(st, "pTw%d" % i, [128, 2, 512], BF16) for i in range(2)]
        psall = G["psall"]
        xlo = [S.sb(st, "xlo%d" % i, [65, 512], F32) for i in range(2)]
        rinv = [S.sb(st, "rinv%d" % i, [64, 512], F32) for i in range(2)]
        olo = [S.sb(st, "olo%d" % i, [64, 512], F32) for i in range(2)]
        ohi = [S.sb(st, "ohi%d" % i, [64, 512], F32) for i in range(2)]
        dlo = S.sb(st, "dlo", [64, 512], F32)
        dhi = S.sb(st, "dhi", [64, 512], F32)
        sql = S.sb(st, "sql", [64, 512], BF16)
        sqh = S.sb(st, "sqh", [64, 512], BF16)
        rs = S.sb(st, "rs", [64, 512], F32)
        yo = [S.sb(st, "yo%d" % i, [64, 512], BF16) for i in range(4)]
        nrot = 0
        nyo = 0
        nq = 0
        for grp in range(6):
            kT, vt = kTs[grp % 2], vts[grp % 2]
            is_diff = grp < 4
            if is_diff:
                h = grp
                S.dma("sp", kT[:, :], qkT.ap()[:, 4 + h, :], writes=[kT])
                for k0 in range(0, NKB, 11):
                    S.dma("sp", vt[:, k0:k0 + 11, :, :],
                          vaug.ap()[k0 * 128:(k0 + 11) * 128, 2 * h:2 * h + 2, :].rearrange("(kb p) g c -> p kb g c", p=128), writes=[vt])
                units = [(0, 0), (0, 1)]
            else:
                n = grp - 4
                S.dma("sp", kT[0:64, :], qkT.ap()[64 * n:64 * n + 64, 12, :], writes=[kT])
                S.dma("sp", kT[64:128, :], qkT.ap()[64 * n:64 * n + 64, 12, :], writes=[kT])
                for k0 in range(0, NKB, 11):
                    S.dma("sp", vt[:, k0:k0 + 11, 0:1, :],
                          vaug.ap()[k0 * 128:(k0 + 11) * 128, 8 + n:9 + n, :].rearrange("(kb p) g c -> p kb g c", p=128), writes=[vt])
                units = [(0, 0), (0, 1), (1, 0), (1, 1)]
            for (t0, W) in out_tiles:
                qt = qts[nq % 2]; nq += 1
                if is_diff:
                    S.dma("sp", qt[:, 0, 0:W], qkT.ap()[:, h, t0:t0 + W], writes=[qt])
                else:
                    S.dma("sp", qt[:, :, 0:W], qkT.ap()[:, 8 + 2 * n:10 + 2 * n, t0:t0 + W], writes=[qt])
                kbs = [0, 1] if t0 < NCTX else list(range(NKB))
                subs = []
                for kb in kbs:
                    for u0 in range(0, len(units), 2):
                        subs.append((kb, [u0, u0 + 1]))

                def emit_s(i):
                    kb, us = subs[i]
                    r = i % 2
                    pw = pTw[r]
                    for k, ui in enumerate(us):
                        qc, half = units[ui]
                        hs = slice(64 * half, 64 * half + 64)
                        pS = PS[2 * r + k]
                        S.op("pe", lambda e: e.matmul(pS[:, 0:W], kT[hs, kb * 128:(kb + 1) * 128], qt[hs, qc, 0:W], start=True, stop=True),
                             reads=[kT, qt], writes=[pS])
                    if W == 512:
                        S.op("act", lambda e: e.activation(pw[:, :, :].rearrange("p a w -> p (a w)"),
                                                           psall[:, 2 * r * 512:(2 * r + 2) * 512], AF.Exp, scale=0.125),
                             reads=[PS[2 * r], PS[2 * r + 1]], writes=[pw])
                    else:
                        for k in range(2):
                            S.op("act", lambda e: e.activation(pw[:, k, 0:W], PS[2 * r + k][:, 0:W], AF.Exp, scale=0.125),
                                 reads=[PS[2 * r + k]], writes=[pw])

                def emit_pv(i):
                    kb, us = subs[i]
                    pw = pTw[i % 2]
                    fl = dict(start=(kb == kbs[0]), stop=(kb == kbs[-1]))
                    for k, ui in enumerate(us):
                        if is_diff:
                            a0, a1 = PS[4 + 2 * ui], PS[5 + 2 * ui]
                            S.op("pe", lambda e: e.matmul(a0[0:65, 0:W], vt[:, kb, 0, 0:65], pw[:, k, 0:W], **fl), reads=[vt, pw], writes=[a0])
                            S.op("pe", lambda e: e.matmul(a1[0:64, 0:W], vt[:, kb, 1, 0:64], pw[:, k, 0:W], **fl), reads=[vt, pw], writes=[a1])
                        else:
                            a0 = PS[4 + ui]
                            S.op("pe", lambda e: e.matmul(a0[0:65, 0:W], vt[:, kb, 0, 0:65], pw[:, k, 0:W], **fl), reads=[vt, pw], writes=[a0])

                for i in range(len(subs) + 1):
                    if i < len(subs):
                        emit_s(i)
                    if i >= 1:
                        emit_pv(i - 1)
                if is_diff:
                    for m in range(2):
                        a0, a1 = PS[4 + 2 * m], PS[5 + 2 * m]
                        S.op("act", lambda e: e.activation(xlo[m][0:65, 0:W], a0[0:65, 0:W], AF.Copy), reads=[a0], writes=[xlo[m]])
                        S.op("pe", lambda e: e.matmul(PS[0][0:64, 0:W], cF(CI_SEL)[0:65, 0:64], xlo[m][0:65, 0:W], start=True, stop=True),
                             reads=[xlo[m]], writes=[PS[0]])
                        S.op("dve", lambda e: e.reciprocal(rinv[m][:, 0:W], PS[0][0:64, 0:W]), reads=[PS[0]], writes=[rinv[m]])
                        S.op("dve", lambda e: e.tensor_tensor(olo[m][:, 0:W], xlo[m][0:64, 0:W], rinv[m][:, 0:W], ALU.mult),
                             reads=[xlo[m], rinv[m]], writes=[olo[m]])
                        S.op("dve", lambda e: e.tensor_tensor(ohi[m][:, 0:W], a1[0:64, 0:W], rinv[m][:, 0:W], ALU.mult),
                             reads=[a1, rinv[m]], writes=[ohi[m]])
                    S.op("dve", lambda e: e.scalar_tensor_tensor(dlo[:, 0:W], olo[1][:, 0:W], lamc[0:64, 0:1], olo[0][:, 0:W], ALU.mult, ALU.add),
                         reads=[olo[0], olo[1], lamc], writes=[dlo])
                    S.op("dve", lambda e: e.scalar_tensor_tensor(dhi[:, 0:W], ohi[1][:, 0:W], lamc[0:64, 0:1], ohi[0][:, 0:W], ALU.mult, ALU.add),
                         reads=[ohi[0], ohi[1], lamc], writes=[dhi])
                    S.op("act", lambda e: e.activation(sql[:, 0:W], dlo[:, 0:W], AF.Square), reads=[dlo], writes=[sql])
                    S.op("act", lambda e: e.activation(sqh[:, 0:W], dhi[:, 0:W], AF.Square), reads=[dhi], writes=[sqh])
                    S.op("pe", lambda e: e.matmul(PS[0][0:64, 0:W], cB(CI_ONES)[0:64, 0:64], sql[:, 0:W], start=True, stop=False),
                         reads=[sql], writes=[PS[0]])
                    S.op("pe", lambda e: e.matmul(PS[0][0:64, 0:W], cB(CI_ONES)[0:64, 0:64], sqh[:, 0:W], start=False, stop=True),
                         reads=[sqh], writes=[PS[0]])
                    S.op("act", lambda e: e.activation(rs[:, 0:W], PS[0][0:64, 0:W], AF.Sqrt, bias=EPS, scale=1.0 / 128), reads=[PS[0]], writes=[rs])
                    S.op("dve", lambda e: e.reciprocal(rs[:, 0:W], rs[:, 0:W]), reads=[rs], writes=[rs])
                    for (dd, col, p0) in ((dlo, 0, 0), (dhi, 1, 64)):
                        y = yo[nyo % 4]; nyo += 1
                        S.op("dve", lambda e: e.scalar_tensor_tensor(y[:, 0:W], dd[:, 0:W], subg[0:64, col:col + 1], rs[:, 0:W], ALU.mult, ALU.mult),
                             reads=[dd, subg, rs], writes=[y])
                        S.dma("pool", G["ybT"].ap()[p0:p0 + 64, h, t0:t0 + W], y[:, 0:W], reads=[y])
                else:
                    for j in range(4):
                        head = 4 * n + j
                        a0 = PS[4 + j]
                        m = j % 2
                        S.op("act", lambda e: e.activation(xlo[m][0:65, 0:W], a0[0:65, 0:W], AF.Copy), reads=[a0], writes=[xlo[m]])
                        S.op("pe", lambda e: e.matmul(PS[0][0:64, 0:W], cF(CI_SEL)[0:65, 0:64], xlo[m][0:65, 0:W], start=True, stop=True),
                             reads=[xlo[m]], writes=[PS[0]])
                        S.op("dve", lambda e: e.reciprocal(rinv[m][:, 0:W], PS[0][0:64, 0:W]), reads=[PS[0]], writes=[rinv[m]])
                        y = yo[nyo % 4]; nyo += 1
                        S.op("dve", lambda e: e.tensor_tensor(y[:, 0:W], xlo[m][0:64, 0:W], rinv[m][:, 0:W], ALU.mult),
                             reads=[xlo[m], rinv[m]], writes=[y])
                        p0 = 64 * (head % 2)
                        S.dma("pool", G["ycT"].ap()[p0:p0 + 64, head // 2, t0:t0 + W], y[:, 0:W], reads=[y])


def phase3(nc, S, G):
    PS, layer, v128, rows, aneg = G["PS"], G["layer"], G["v128"], G["rows"], G["aneg"]
    cF, cB = G["cF"], G["cB"]
    xbcT, bcT, xtok, btok, yssd = G["xbcT"], G["bcT"], G["xtok"], G["btok"], G["yssd"]
    NCH = T // 128
    with contextlib.ExitStack() as st:
        xin = [S.sb(st, "c_xin%d" % i, [128, 6, 516], F32) for i in range(2)]
        cv = S.sb(st, "c_cv", [128, 6, 512], F32)
        sl = [S.sb(st, "c_sl%d" % i, [128, 6, 512], BF16) for i in range(2)]
        xtk = [S.sb(st, "c_xtk%d" % i, [128, 640], BF16) for i in range(2)]
        psT = PS[7][:, :].bitcast(BF16)
        nb = 0
        for ti, (t0, W) in enumerate(TILES):
            seg0, seg1 = (0, NCTX) if t0 < NCTX else (NCTX, T)
            xi, sli = xin[ti % 2], sl[ti % 2]
            S.op("pool", lambda e: e.memset(xi[:, :, 0:2], 0.0), writes=[xi])
            S.op("pool", lambda e: e.memset(xi[:, :, W + 2:W + 4], 0.0), writes=[xi])
            lo = t0 - 2 if t0 > seg0 else t0
            hi = t0 + W + 2 if t0 + W < seg1 else t0 + W
            S.dma("sp", xi[:, :, lo - (t0 - 2):hi - (t0 - 2)], xbcT.ap()[:, :, lo:hi], writes=[xi])
            for c in range(6):
                wc = 76 + c * 5
                S.op("dve", lambda e: e.tensor_scalar(cv[:, c, 0:W], xi[:, c, 0:W], v128[:, wc:wc + 1], v128[:, 70 + c:71 + c], ALU.mult, ALU.add),
                     reads=[xi, v128], writes=[cv])
                for k in range(1, 5):
                    S.op("dve", lambda e: e.scalar_tensor_tensor(cv[:, c, 0:W], xi[:, c, k:k + W], v128[:, wc + k:wc + k + 1], cv[:, c, 0:W], ALU.mult, ALU.add),
                         reads=[xi, v128, cv], writes=[cv])
            S.op("act", lambda e: e.activation(sli[:, :, 0:W], cv[:, :, 0:W], AF.Silu), reads=[cv], writes=[sli])
            S.dma("pool", bcT.ap()[0].rearrange("g n t -> (g n) t")[:, t0:t0 + W], sli[:, 4, 0:W], reads=[sli])
            S.dma("pool", bcT.ap()[1].rearrange("g n t -> (g n) t")[:, t0:t0 + W], sli[:, 5, 0:W], reads=[sli])
            for blk in range(W // 128):
                r0 = t0 + blk * 128
                xt_ = xtk[nb % 2]; nb += 1
                for c in range(5):
                    S.op("pe", lambda e: e.transpose(psT[:, c * 128:(c + 1) * 128], sli[:, c, blk * 128:(blk + 1) * 128], cB(CI_ID)),
                         reads=[sli], writes=[PS[7]])
                S.op("act", lambda e: e.activation(xt_[:, :], psT[:, 0:640], AF.Copy), reads=[PS[7]], writes=[xt_])
                S.dma("pool", xtok.ap()[r0:r0 + 128, :], xt_[:, 0:512], reads=[xt_])
                S.dma("pool", btok.ap()[r0:r0 + 128, :], xt_[:, 512:640], reads=[xt_])
    S.barrier()
    yssd1 = G["yssd1"]
    with contextlib.ExitStack() as st:
        own_ck = set()
        for (t0_, W_) in G["out_tiles"]:
            own_ck.update(range(t0_ // 128, (t0_ + W_) // 128))
        last_own = max(own_ck)
        Dd = []
        for d in range(2):
            B = {}
            B["xk"] = [S.sb(st, "s_xk%d_%d" % (d, i), [128, 512], BF16) for i in range(2)]
            B["bk"] = [S.sb(st, "s_bk%d_%d" % (d, i), [128, 128], BF16) for i in range(2)]
            B["bct"] = [S.sb(st, "s_bct%d_%d" % (d, i), [64, 4, 128], BF16) for i in range(2)]
            B["dr"] = [S.sb(st, "s_dr%d_%d" % (d, i), [128, 16], F32) for i in range(2)]
            B["yv"] = [S.sb(st, "s_yv%d_%d" % (d, i), [128, 512], F32) for i in range(2)]
            for nm, shp, dt_ in (("av", [128, 16], F32), ("ab", [128, 8, 128], F32), ("ac", [128, 8], F32), ("ea", [128, 8], F32),
                                 ("cdec", [128, 8], F32), ("arg", [128, 8, 128], F32), ("dec", [128, 8, 128], F32),
                                 ("MT", [128, 8, 128], BF16), ("Bw", [128, 8, 64], BF16), ("xdt", [128, 8, 64], BF16),
                                 ("tmp", [128, 512], F32), ("h32", [64, 8, 64], F32), ("hb", [64, 8, 64], BF16), ("dte8", [128, 16], F32)):
                B[nm] = S.sb(st, "s_%s%d" % (nm, d), shp, dt_)
            B["n"] = 0
            B["pA"], B["pB"], B["pC"], B["pD"] = PS[4 * d], PS[4 * d + 1], PS[4 * d + 2], PS[4 * d + 3]
            S.op("pool", lambda e: e.memset(B["h32"][:], 0.0), writes=[B["h32"]])
            S.op("pool", lambda e: e.memset(B["hb"][:], 0.0), writes=[B["hb"]])
            Dd.append(B)
        orders = [[0, 1] + list(range(2, last_own + 1)), [1, 0] + list(range(NCH - 1, 1, -1))]

        def emit_chunk(d, ck):
            B = Dd[d]
            last_i = 127 if d == 0 else 0
            tri = cF(CI_TRI0 if d == 0 else CI_TRI1)
            mn = cF(CI_MN0 if d == 0 else CI_MN1)
            cols = slice(d * 8, d * 8 + 8)
            pA, pB, pC, pD = B["pA"], B["pB"], B["pC"], B["pD"]
            av, ab, ac, ea, cdec, arg, dec = B["av"], B["ab"], B["ac"], B["ea"], B["cdec"], B["arg"], B["dec"]
            MT, Bw, xdt, tmp, h32, hb, dte8 = B["MT"], B["Bw"], B["xdt"], B["tmp"], B["h32"], B["hb"], B["dte8"]
            c0 = ck * 128
            b = B["n"] % 2
            B["n"] += 1
            xk, bk, bct, dt = B["xk"][b], B["bk"][b], B["bct"][b], B["dr"][b]
            full = (d == 0) or (ck in own_ck)
            S.dma("sp", xk[:, :], xtok.ap()[c0:c0 + 128, :], writes=[xk])
            S.dma("sp", bk[:, :], btok.ap()[c0:c0 + 128, :], writes=[bk])
            if full:
                S.dma("sp", bct[:, :, :], bcT.ap()[:, :, :, c0:c0 + 128].rearrange("k g n t -> n (k g) t"), writes=[bct])
            S.dma("sp", dt[:, :], G["dtt"].ap()[c0:c0 + 128, :], writes=[dt])
            S.op("dve", lambda e: e.tensor_tensor(dt[:, :], dt[:, :], rows[:, 16:32], ALU.add), reads=[dt, rows], writes=[dt])
            S.op("act", lambda e: e.activation(dt[:, :], dt[:, :], AF.Exp), reads=[dt], writes=[dt])
            S.op("act", lambda e: e.activation(dt[:, :], dt[:, :], AF.Ln, bias=1.0, scale=1.0), reads=[dt], writes=[dt])
            S.op("dve", lambda e: e.tensor_tensor(av[:, :], dt[:, :], aneg[:, :], ALU.mult), reads=[dt, aneg], writes=[av])
            yield
            if full:
                S.op("pe", lambda e: e.matmul(pC[:, 0:8], tri, av[:, cols], start=True, stop=True), reads=[av], writes=[pC])
                S.op("dve", lambda e: e.tensor_copy(ab[:, :, :], av[:, cols].unsqueeze(2).broadcast_to([128, 8, 128])), reads=[av], writes=[ab])
                for h in range(8):
                    pr = pA if h < 4 else pB
                    S.op("pe", lambda e: e.matmul(pr[:, (h % 4) * 128:(h % 4 + 1) * 128], ab[:, h, :], tri, start=True, stop=True),
                         reads=[ab], writes=[pr])
                yield
                S.op("act", lambda e: e.activation(ac[:, :], pC[:, 0:8], AF.Copy), reads=[pC], writes=[ac])
                for h in range(8):
                    pr = pA if h < 4 else pB
                    S.op("dve", lambda e: e.scalar_tensor_tensor(arg[:, h, :], pr[:, (h % 4) * 128:(h % 4 + 1) * 128], ac[:, h:h + 1], mn,
                                                                 ALU.subtract, ALU.add), reads=[pr, ac], writes=[arg])
                yield
                S.op("act", lambda e: e.activation(dec[:, :, :], arg[:, :, :], AF.Exp), reads=[arg], writes=[dec])
                S.op("act", lambda e: e.activation(ea[:, :], ac[:, :], AF.Exp), reads=[ac], writes=[ea])
                for hh, pr in enumerate((pA, pB)):
                    S.op("act", lambda e: e.activation(cdec[:, hh * 4:hh * 4 + 4], pr[:, :].rearrange("p (h t) -> p h t", t=128)[:, :, last_i],
                                                       AF.Exp), reads=[pr], writes=[cdec])
                for g in range(2):
                    S.op("pe", lambda e: e.matmul(pC[:, 16 + g * 128:16 + (g + 1) * 128], bct[:, g, :], bct[:, 2 + g, :], start=True, stop=True),
                         reads=[bct], writes=[pC])
                yield
                for g in range(2):
                    S.op("dve", lambda e: e.tensor_tensor(MT[:, 4 * g:4 * g + 4, :], dec[:, 4 * g:4 * g + 4, :],
                                                          pC[:, 16 + g * 128:16 + (g + 1) * 128].unsqueeze(1).broadcast_to([128, 4, 128]), ALU.mult),
                         reads=[dec, pC], writes=[MT])
                    S.op("pool", lambda e: e.tensor_tensor(Bw[:, 4 * g:4 * g + 4, :],
                                                           bk[:, g * 64:(g + 1) * 64].unsqueeze(1).broadcast_to([128, 4, 64]),
                                                           dec[:, 4 * g:4 * g + 4, last_i:last_i + 1].broadcast_to([128, 4, 64]), ALU.mult),
                         reads=[bk, dec], writes=[Bw])
            else:
                S.op("pe", lambda e: e.matmul(pC[:, 0:8], tri, av[:, cols], start=True, stop=True), reads=[av], writes=[pC])
                S.op("pe", lambda e: e.matmul(pC[:, 8:16], cF(CI_ONES), av[:, cols], start=True, stop=True), reads=[av], writes=[pC])
                S.op("act", lambda e: e.activation(dte8[:, :], pC[:, 0:16], AF.Copy), reads=[pC], writes=[dte8])
                S.op("act", lambda e: e.activation(cdec[:, :], dte8[:, 8:16], AF.Exp), reads=[dte8], writes=[cdec])
                S.op("dve", lambda e: e.tensor_tensor(dte8[:, 0:8], dte8[:, 8:16], dte8[:, 0:8], ALU.subtract), reads=[dte8], writes=[dte8])
                S.op("act", lambda e: e.activation(dte8[:, 0:8], dte8[:, 0:8], AF.Exp), reads=[dte8], writes=[dte8])
                for g in range(2):
                    S.op("pool", lambda e: e.tensor_tensor(Bw[:, 4 * g:4 * g + 4, :],
                                                           bk[:, g * 64:(g + 1) * 64].unsqueeze(1).broadcast_to([128, 4, 64]),
                                                           dte8[:, 4 * g:4 * g + 4].unsqueeze(2).broadcast_to([128, 4, 64]), ALU.mult),
                         reads=[bk, dte8], writes=[Bw])
            S.op("pool", lambda e: e.tensor_tensor(xdt[:, :, :], xk[:, :].rearrange("p (h q) -> p h q", q=64),
                                                   dt[:, cols].unsqueeze(2).broadcast_to([128, 8, 64]), ALU.mult),
                 reads=[xk, dt], writes=[xdt])
            yield
            for h in range(8):
                g = h // 4
                hs = slice(h * 64, (h + 1) * 64)
                if full:
                    S.op("pe", lambda e: e.matmul(pA[:, hs], MT[:, h, :], xdt[:, h, :], start=True, stop=True), reads=[MT, xdt], writes=[pA])
                    S.op("pe", lambda e: e.matmul(pB[:, hs], bct[:, 2 + g, :], hb[:, h, :], start=True, stop=True), reads=[bct, hb], writes=[pB])
                S.op("pe", lambda e: e.matmul(pD[0:64, hs], Bw[:, h, :], xdt[:, h, :], start=True, stop=True), reads=[Bw, xdt], writes=[pD])
            yield
            y = B["yv"][b]
            if full:
                S.op("dve", lambda e: e.tensor_tensor(y[:, :].rearrange("p (h q) -> p h q", q=64), pB[:, :].rearrange("p (h q) -> p h q", q=64),
                                                      ea[:, :].unsqueeze(2).broadcast_to([128, 8, 64]), ALU.mult), reads=[pB, ea], writes=[y])
                S.op("dve", lambda e: e.tensor_tensor(y[:, :], y[:, :], pA[:, :], ALU.add), reads=[y, pA], writes=[y])
            S.op("dve", lambda e: e.tensor_tensor(h32[:, :, :], h32[:, :, :], cdec[0:64, :].unsqueeze(2).broadcast_to([64, 8, 64]), ALU.mult),
                 reads=[h32, cdec], writes=[h32])
            S.op("dve", lambda e: e.tensor_tensor(h32[:, :, :], h32[:, :, :], pD[0:64, :].rearrange("p (h q) -> p h q", q=64), ALU.add),
                 reads=[h32, pD], writes=[h32])
            S.op("act", lambda e: e.activation(hb[:, :, :], h32[:, :, :], AF.Copy), reads=[h32], writes=[hb])
            yield
            if not full:
                return
            if d == 0:
                S.op("pool", lambda e: e.tensor_tensor(tmp[:, :].rearrange("p (h q) -> p h q", q=64), xk[:, :].rearrange("p (h q) -> p h q", q=64),
                                                       rows[:, 32:40].unsqueeze(2).broadcast_to([128, 8, 64]), ALU.mult),
                     reads=[xk, rows], writes=[tmp])
                S.op("pool", lambda e: e.tensor_tensor(y[:, :], y[:, :], tmp[:, :], ALU.add), reads=[y, tmp], writes=[y])
                S.dma("pool", yssd.ap()[c0:c0 + 128, :], y[:, :], reads=[y])
            else:
                S.dma("pool", yssd1.ap()[c0:c0 + 128, :], y[:, :], reads=[y])

        for i in range(max(len(orders[0]), len(orders[1]))):
            active = [emit_chunk(d, orders[d][i]) for d in range(2) if i < len(orders[d])]
            while active:
                for g_ in list(active):
                    try:
                        next(g_)
                    except StopIteration:
                        active.remove(g_)
        S.barrier()
    with contextlib.ExitStack() as st:
        y0 = [S.sb(st, "f_y0%d" % i, [128, 512], F32) for i in range(2)]
        y1 = [S.sb(st, "f_y1%d" % i, [128, 512], F32) for i in range(2)]
        zk = [S.sb(st, "f_zk%d" % i, [128, 512], F32) for i in range(2)]
        tmp = S.sb(st, "f_tmp", [128, 512], F32)
        ss = [S.sb(st, "f_ss%d" % i, [128, 2], F32) for i in range(2)]
        yn = [S.sb(st, "f_yn%d" % i, [128, 512], BF16) for i in range(2)]
        yaS = [S.sb(st, "f_ya%d" % i, [128, 4, 128], BF16) for i in range(2)]
        for i, ck in enumerate(sorted(own_ck)):
            c0 = ck * 128
            b = i % 2
            psT = PS[6 + b][:, :].bitcast(BF16)
            S.dma("sp", y0[b][:, :], yssd.ap()[c0:c0 + 128, :], writes=[y0[b]])
            S.dma("sp", y1[b][:, :], yssd1.ap()[c0:c0 + 128, :], writes=[y1[b]])
            S.dma("sp", zk[b][:, :], G["zt"].ap()[c0:c0 + 128, :], writes=[zk[b]])
            y = y0[b]
            S.op("pool", lambda e: e.tensor_tensor(y[:, :], y[:, :], y1[b][:, :], ALU.add), reads=[y, y1[b]], writes=[y])
            S.op("act", lambda e: e.activation(zk[b][:, :], zk[b][:, :], AF.Silu), reads=[zk[b]], writes=[zk[b]])
            S.op("dve", lambda e: e.tensor_tensor(y[:, :], y[:, :], zk[b][:, :], ALU.mult), reads=[y, zk[b]], writes=[y])
            S.op("act", lambda e: e.activation(tmp[:, :], y[:, :], AF.Square, accum_out=ss[b][:, 0:1]), reads=[y], writes=[tmp, ss[b]])
            S.op("act", lambda e: e.activation(ss[b][:, 1:2], ss[b][:, 0:1], AF.Sqrt, bias=EPS, scale=1.0 / 512), reads=[ss[b]], writes=[ss[b]])
            S.op("dve", lambda e: e.reciprocal(ss[b][:, 1:2], ss[b][:, 1:2]), reads=[ss[b]], writes=[ss[b]])
            S.op("dve", lambda e: e.scalar_tensor_tensor(yn[b][:, :], y[:, :], ss[b][:, 1:2], rows[:, 40:552], ALU.mult, ALU.mult),
                 reads=[y, ss[b], rows], writes=[yn[b]])
            for c in range(4):
                S.op("pe", lambda e: e.transpose(psT[:, c * 128:(c + 1) * 128], yn[b][:, c * 128:(c + 1) * 128], cB(CI_ID)),
                     reads=[yn[b]], writes=[PS[6 + b]])
            ya = yaS[b]
            S.op("act", lambda e: e.activation(ya[:, :, :], psT[:, 0:512].rearrange("p (c t) -> p c t", t=128), AF.Copy),
                 reads=[PS[6 + b]], writes=[ya])
            S.dma("pool", G["yaT"].ap()[:, :, c0:c0 + 128], ya[:, :, :], reads=[ya])
        S.barrier()


def _rev(a, n):
    return bass.AP(a.tensor, a.offset + n - 1, [list(a.ap[0]), [-1, n]])


def phase4(nc, S, G):
    PS, layer, v128 = G["PS"], G["layer"], G["v128"]
    cF, cB = G["cF"], G["cB"]
    uT, ys5 = G["uT"], G["ys5"]
    I32 = mybir.dt.int32
    TWO_PI = 2.0 * math.pi
    with contextlib.ExitStack() as st:
        Bb = S.sb(st, "z_Bb", [128, 2, 12, 128], BF16)
        Cb = S.sb(st, "z_Cb", [128, 2, 12, 128], BF16)
        gw = S.sb(st, "z_gw", [128, 3, 768], BF16)
        for ri in range(2):
            S.dma("pool", Bb[:, ri, :, :], G["s5_B"].ap()[layer, ri].rearrange("gp k m -> k gp m"), writes=[Bb])
            S.dma("pool", Cb[:, ri, :, :], G["s5_C"].ap()[layer, ri].rearrange("gp k m -> k gp m"), writes=[Cb])
        S.dma("pool", gw[:, :, :], G["glu_w"].ap()[layer].rearrange("(kc p) n -> p kc n", p=128), writes=[gw])
        def tt(out, a, b, op, tiles_r, tiles_w, eng="dve"):
            S.op(eng, lambda e: e.tensor_tensor(out, a, b, op), reads=tiles_r, writes=tiles_w)

        def ts(out, a, s1, s2, op0, op1, tiles_r, tiles_w):
            if op1 is None:
                S.op("dve", lambda e: e.tensor_scalar(out, a, s1, None, op0), reads=tiles_r, writes=tiles_w)
            else:
                S.op("dve", lambda e: e.tensor_scalar(out, a, s1, s2, op0, op1), reads=tiles_r, writes=tiles_w)

        TL = 256
        ub = [S.sb(st, "z_ub%d" % i, [128, 3, 512], F32) for i in range(2)]
        ubb = [S.sb(st, "z_ubb%d" % i, [128, 3, 512], BF16) for i in range(2)]
        br = [S.sb(st, "z_br%d" % i, [128, 512], F32) for i in range(2)]
        bi = [S.sb(st, "z_bi%d" % i, [128, 512], F32) for i in range(2)]
        m = [[S.sb(st, "z_m%d_%d" % (i, j), [128, 512], F32) for j in range(4)] for i in range(2)]
        gr = [S.sb(st, "z_gr%d" % i, [128, 512], F32) for i in range(2)]
        gi = [S.sb(st, "z_gi%d" % i, [128, 512], F32) for i in range(2)]
        hrb = [S.sb(st, "z_hrb%d" % i, [128, 512], BF16) for i in range(2)]
        hib = [S.sb(st, "z_hib%d" % i, [128, 512], BF16) for i in range(2)]
        hst = S.sb(st, "z_hst", [128, 12, 2], F32)
        tn = S.sb(st, "z_tn", [128, 4], F32)
        yst = [S.sb(st, "z_yst%d" % i, [128, 512], F32) for i in range(2)]
        y0 = S.sb(st, "z_y0", [128, 3, 512], F32)
        gy = S.sb(st, "z_gy", [128, 3, 512], BF16)
        vl = [S.sb(st, "z_vl%d" % i, [128, 512], F32) for i in range(3)]
        sgt = [S.sb(st, "z_sg%d" % i, [128, 512], F32) for i in range(2)]
        ydo = [S.sb(st, "z_ydo%d" % i, [128, 512], BF16) for i in range(2)]
        nn = 0
        for d in range(2):
            with contextlib.ExitStack() as sd:
                Er = S.sb(sd, "z_Er", [128, 12, TL], F32)
                Ei = S.sb(sd, "z_Ei", [128, 12, TL], F32)
                Fr = S.sb(sd, "z_Fr", [128, 12, TL], F32)
                Fi = S.sb(sd, "z_Fi", [128, 12, TL], F32)
                rho = S.sb(sd, "z_rho", [128, 12], F32)
                EW = S.sb(sd, "z_EW", [128, 12, 2], F32)
                with contextlib.ExitStack() as st2:
                    t1 = S.sb(st2, "z_t1", [128, 12, TL], F32)
                    t2 = S.sb(st2, "z_t2", [128, 12, TL], F32)
                    names = ["lr", "li", "stp", "u", "f", "sphi", "s2", "c1", "ar", "ai", "den", "am1", "fr", "fi", "x1", "x2", "msk"]
                    P = {nm: S.sb(st2, "zp_%s" % nm, [128, 12], F32) for nm in names}
                    P["rho"] = rho
                    ki = S.sb(st2, "zp_ki", [128, 12], I32)
                    A = lambda nm: P[nm][:, :]
                    S.dma("sp", A("lr"), G["s5_lr"].ap()[layer, d], writes=[P["lr"]])
                    S.dma("sp", A("li"), G["s5_li"].ap()[layer, d], writes=[P["li"]])
                    S.dma("sp", A("stp"), G["s5_ldt"].ap()[layer, d], writes=[P["stp"]])
                    S.op("act", lambda e: e.activation(A("stp"), A("stp"), AF.Exp), reads=[P["stp"]], writes=[P["stp"]])
                    tt(A("rho"), A("lr"), A("stp"), ALU.mult, [P["lr"], P["stp"]], [P["rho"]])
                    S.op("act", lambda e: e.activation(A("rho"), A("rho"), AF.Exp), reads=[P["rho"]], writes=[P["rho"]])
                    tt(A("u"), A("li"), A("stp"), ALU.mult, [P["li"], P["stp"]], [P["u"]])
                    ts(A("u"), A("u"), 1.0 / TWO_PI, 0.5, ALU.mult, ALU.add, [P["u"]], [P["u"]])
                    S.op("dve", lambda e: e.tensor_copy(ki[:, :], A("u")), reads=[P["u"]], writes=[ki])
                    S.op("dve", lambda e: e.tensor_copy(A("f"), ki[:, :]), reads=[ki], writes=[P["f"]])
                    tt(A("f"), A("u"), A("f"), ALU.subtract, [P["u"], P["f"]], [P["f"]])
                    ts(A("msk"), A("f"), 0.5, None, ALU.is_ge, None, [P["f"]], [P["msk"]])
                    tt(A("f"), A("f"), A("msk"), ALU.subtract, [P["f"], P["msk"]], [P["f"]])
                    ts(A("f"), A("f"), -0.49999, 0.49999, ALU.max, ALU.min, [P["f"]], [P["f"]])
                    S.op("act", lambda e: e.activation(A("sphi"), A("f"), AF.Sin, scale=TWO_PI), reads=[P["f"]], writes=[P["sphi"]])
                    S.op("act", lambda e: e.activation(A("s2"), A("f"), AF.Sin, scale=math.pi), reads=[P["f"]], writes=[P["s2"]])
                    tt(A("c1"), A("s2"), A("s2"), ALU.mult, [P["s2"]], [P["c1"]])
                    ts(A("c1"), A("c1"), 2.0, -1.0, ALU.mult, ALU.add, [P["c1"]], [P["c1"]])
                    S.op("dve", lambda e: e.tensor_copy(Er[:, :, 0], A("c1")), reads=[P["c1"]], writes=[Er])
                    S.op("dve", lambda e: e.tensor_copy(Ei[:, :, 0], A("sphi")), reads=[P["sphi"]], writes=[Ei])
                    tt(A("ar"), A("rho"), A("c1"), ALU.mult, [P["rho"], P["c1"]], [P["ar"]])
                    tt(A("ai"), A("rho"), A("sphi"), ALU.mult, [P["rho"], P["sphi"]], [P["ai"]])
                    ts(A("ai"), A("ai"), -1.0, None, ALU.mult, None, [P["ai"]], [P["ai"]])
                    tt(A("den"), A("lr"), A("lr"), ALU.mult, [P["lr"]], [P["den"]])
                    tt(A("x1"), A("li"), A("li"), ALU.mult, [P["li"]], [P["x1"]])
                    tt(A("den"), A("den"), A("x1"), ALU.add, [P["den"], P["x1"]], [P["den"]])
                    S.op("dve", lambda e: e.reciprocal(A("den"), A("den")), reads=[P["den"]], writes=[P["den"]])
                    ts(A("am1"), A("ar"), -1.0, None, ALU.add, None, [P["ar"]], [P["am1"]])
                    tt(A("x1"), A("am1"), A("lr"), ALU.mult, [P["am1"], P["lr"]], [P["x1"]])
                    tt(A("x2"), A("ai"), A("li"), ALU.mult, [P["ai"], P["li"]], [P["x2"]])
                    tt(A("fr"), A("x1"), A("x2"), ALU.add, [P["x1"], P["x2"]], [P["fr"]])
                    tt(A("fr"), A("fr"), A("den"), ALU.mult, [P["fr"], P["den"]], [P["fr"]])
                    tt(A("x1"), A("ai"), A("lr"), ALU.mult, [P["ai"], P["lr"]], [P["x1"]])
                    tt(A("x2"), A("am1"), A("li"), ALU.mult, [P["am1"], P["li"]], [P["x2"]])
                    tt(A("fi"), A("x1"), A("x2"), ALU.subtract, [P["x1"], P["x2"]], [P["fi"]])
                    tt(A("fi"), A("fi"), A("den"), ALU.mult, [P["fi"], P["den"]], [P["fi"]])
                    n = 1
                    while n < TL:
                        cr = Er[:, :, n - 1:n].broadcast_to([128, 12, n])
                        ci = Ei[:, :, n - 1:n].broadcast_to([128, 12, n])
                        tt(t1[:, :, 0:n], Er[:, :, 0:n], cr, ALU.mult, [Er], [t1])
                        tt(t2[:, :, 0:n], Ei[:, :, 0:n], ci, ALU.mult, [Ei], [t2])
                        tt(Er[:, :, n:2 * n], t1[:, :, 0:n], t2[:, :, 0:n], ALU.subtract, [t1, t2], [Er])
                        tt(t1[:, :, 0:n], Er[:, :, 0:n], ci, ALU.mult, [Er, Ei], [t1])
                        tt(t2[:, :, 0:n], Ei[:, :, 0:n], cr, ALU.mult, [Ei, Er], [t2])
                        tt(Ei[:, :, n:2 * n], t1[:, :, 0:n], t2[:, :, 0:n], ALU.add, [t1, t2], [Ei])
                        n *= 2
                    S.op("dve", lambda e: e.tensor_copy(EW[:, :, 0], Er[:, :, TL - 1]), reads=[Er], writes=[EW])
                    S.op("dve", lambda e: e.tensor_copy(EW[:, :, 1], Ei[:, :, TL - 1]), reads=[Ei], writes=[EW])
                    frb = P["fr"][:, :].unsqueeze(2).broadcast_to([128, 12, TL])
                    fib = P["fi"][:, :].unsqueeze(2).broadcast_to([128, 12, TL])
                    tt(t1[:, :, :], Er[:, :, :], frb, ALU.mult, [Er, P["fr"]], [t1])
                    tt(t2[:, :, :], Ei[:, :, :], fib, ALU.mult, [Ei, P["fi"]], [t2])
                    tt(Fr[:, :, :], t1[:, :, :], t2[:, :, :], ALU.subtract, [t1, t2], [Fr])
                    tt(t1[:, :, :], Ei[:, :, :], frb, ALU.mult, [Ei, P["fr"]], [t1])
                    tt(t2[:, :, :], Er[:, :, :], fib, ALU.mult, [Er, P["fi"]], [t2])
                    tt(Fi[:, :, :], t1[:, :, :], t2[:, :, :], ALU.add, [t1, t2], [Fi])
                    if d == 1:
                        for tb in (Er, Ei, Fr, Fi):
                            for gp in range(12):
                                S.op("dve", lambda e: e.tensor_copy(t1[:, gp, :], _rev(tb[:, gp, :], TL)), reads=[tb], writes=[t1])
                            S.op("dve", lambda e: e.tensor_copy(tb[:, :, :], t1[:, :, :]), reads=[t1], writes=[tb])
                    S.barrier()

                own_set = set(G["out_tiles"]) | {TILES[0]}
                if d == 0:
                    last_own = max(i for i, tl in enumerate(TILES) if tl in own_set)
                    tiles = TILES[:last_own + 1]
                else:
                    tiles = [TILES[0]] + TILES[:0:-1]
                S.op("pool", lambda e: e.memset(hst[:], 0.0), writes=[hst])
                for ti, (t0, W) in enumerate(tiles):
                    nfr = W // TL
                    full = (t0, W) in set(G["out_tiles"]) or d == 0
                    u_, ub_ = ub[ti % 2], ubb[ti % 2]
                    S.dma("sp", u_[:, :, 0:W], uT.ap()[:, :, t0:t0 + W], writes=[u_])
                    S.op("act", lambda e: e.activation(ub_[:, :, 0:W], u_[:, :, 0:W], AF.Copy), reads=[u_], writes=[ub_])
                    if d == 1 and full:
                        S.dma("sp", y0[:, :, 0:W], ys5.ap()[:, :, t0:t0 + W], writes=[y0])
                    for gp in range(12):
                        uc = gp // 4
                        b = nn % 2; nn += 1
                        mm = m[b]
                        S.op("pe", lambda e: e.matmul(PS[0][:, 0:W], Bb[:, 0, gp, :], ub_[:, uc, 0:W], start=True, stop=True), reads=[Bb, ub_], writes=[PS[0]])
                        S.op("pe", lambda e: e.matmul(PS[1][:, 0:W], Bb[:, 1, gp, :], ub_[:, uc, 0:W], start=True, stop=True), reads=[Bb, ub_], writes=[PS[1]])
                        S.op("act", lambda e: e.activation(br[b][:, 0:W], PS[0][:, 0:W], AF.Copy), reads=[PS[0]], writes=[br[b]])
                        S.op("act", lambda e: e.activation(bi[b][:, 0:W], PS[1][:, 0:W], AF.Copy), reads=[PS[1]], writes=[bi[b]])

                        def bc(tb):
                            return tb[:, gp, :].unsqueeze(1).broadcast_to([128, nfr, TL])

                        def v3(t):
                            return t[:, 0:W].rearrange("p (c k) -> p c k", k=TL)
                        tt(v3(mm[0]), v3(br[b]), bc(Fr), ALU.mult, [br[b], Fr], [mm[0]], "dve")
                        tt(v3(mm[1]), v3(bi[b]), bc(Fi), ALU.mult, [bi[b], Fi], [mm[1]], "pool")
                        tt(v3(mm[2]), v3(bi[b]), bc(Fr), ALU.mult, [bi[b], Fr], [mm[2]], "dve")
                        tt(v3(mm[3]), v3(br[b]), bc(Fi), ALU.mult, [br[b], Fi], [mm[3]], "pool")
                        tt(mm[0][:, 0:W], mm[0][:, 0:W], mm[1][:, 0:W], ALU.subtract, [mm[0], mm[1]], [mm[0]], "dve")
                        tt(mm[2][:, 0:W], mm[2][:, 0:W], mm[3][:, 0:W], ALU.add, [mm[2], mm[3]], [mm[2]], "pool")
                        frs = list(range(nfr)) if d == 0 else list(range(nfr - 1, -1, -1))
                        for fk in frs:
                            cs = slice(fk * TL, (fk + 1) * TL)
                            for (gt, vt_, comp) in ((gr[b], mm[0], 0), (gi[b], mm[2], 1)):
                                o_ap, v_ap = gt[:, cs], vt_[:, cs]
                                if d == 1:
                                    o_ap, v_ap = _rev(o_ap, TL), _rev(v_ap, TL)
                                S.op("dve", lambda e: e.tensor_tensor_scan(o_ap, rho[:, gp:gp + 1].broadcast_to([128, TL]), v_ap,
                                                                           hst[:, gp, comp:comp + 1], ALU.mult, ALU.add),
                                     reads=[rho, vt_, hst], writes=[gt])
                            ie = fk * TL + (TL - 1 if d == 0 else 0)
                            gre, gie = gr[b][:, ie:ie + 1], gi[b][:, ie:ie + 1]
                            e_r, e_i = EW[:, gp, 0:1], EW[:, gp, 1:2]
                            tt(tn[:, 0:1], gie, e_i, ALU.mult, [gi[b], EW], [tn], "pool")
                            tt(tn[:, 1:2], gre, e_i, ALU.mult, [gr[b], EW], [tn], "pool")
                            tt(tn[:, 2:3], gre, e_r, ALU.mult, [gr[b], EW], [tn], "pool")
                            tt(tn[:, 3:4], gie, e_r, ALU.mult, [gi[b], EW], [tn], "pool")
                            tt(hst[:, gp, 0:1], tn[:, 2:3], tn[:, 0:1], ALU.add, [tn], [hst], "pool")
                            tt(hst[:, gp, 1:2], tn[:, 3:4], tn[:, 1:2], ALU.subtract, [tn], [hst], "pool")
                        if not full:
                            continue
                        tt(v3(mm[1]), v3(gr[b]), bc(Er), ALU.mult, [gr[b], Er], [mm[1]], "pool")
                        tt(v3(mm[3]), v3(gi[b]), bc(Ei), ALU.mult, [gi[b], Ei], [mm[3]], "pool")
                        tt(v3(br[b]), v3(gr[b]), bc(Ei), ALU.mult, [gr[b], Ei], [br[b]], "pool")
                        tt(v3(bi[b]), v3(gi[b]), bc(Er), ALU.mult, [gi[b], Er], [bi[b]], "dve")
                        tt(hrb[b][:, 0:W], mm[1][:, 0:W], mm[3][:, 0:W], ALU.add, [mm[1], mm[3]], [hrb[b]], "pool")
                        tt(hib[b][:, 0:W], br[b][:, 0:W], bi[b][:, 0:W], ALU.subtract, [br[b], bi[b]], [hib[b]], "dve")
                        py = PS[2 + uc]
                        S.op("pe", lambda e: e.matmul(py[:, 0:W], Cb[:, 0, gp, :], hrb[b][:, 0:W], start=(gp % 4 == 0), stop=False), reads=[Cb, hrb[b]], writes=[py])
                        S.op("pe", lambda e: e.matmul(py[:, 0:W], Cb[:, 1, gp, :], hib[b][:, 0:W], start=False, stop=(gp % 4 == 3)), reads=[Cb, hib[b]], writes=[py])
                    if not full:
                        continue
                    for uc in range(3):
                        py = PS[2 + uc]
                        ys_ = yst[uc % 2]
                        if d == 0:
                            S.op("dve", lambda e: e.scalar_tensor_tensor(ys_[:, 0:W], u_[:, uc, 0:W], v128[:, 106 + uc:107 + uc], py[:, 0:W], ALU.mult, ALU.add),
                                 reads=[u_, v128, py], writes=[ys_])
                            S.dma("pool", ys5.ap()[:, uc, t0:t0 + W], ys_[:, 0:W], reads=[ys_])
                        else:
                            tt(ys_[:, 0:W], y0[:, uc, 0:W], py[:, 0:W], ALU.add, [y0, py], [ys_], "dve")
                            S.op("act", lambda e: e.activation(gy[:, uc, 0:W], ys_[:, 0:W], AF.Gelu_apprx_tanh), reads=[ys_], writes=[gy])
                    if d == 1:
                        for j in range(6):
                            pg = PS[5 + j % 2]
                            for kc in range(3):
                                S.op("pe", lambda e: e.matmul(pg[:, 0:W], gw[:, kc, j * 128:(j + 1) * 128], gy[:, kc, 0:W], start=(kc == 0), stop=(kc == 2)),
                                     reads=[gw, gy], writes=[pg])
                            if j < 3:
                                S.op("act", lambda e: e.activation(vl[j][:, 0:W], pg[:, 0:W], AF.Identity, bias=v128[:, 109 + j:110 + j], scale=1.0),
                                     reads=[pg, v128], writes=[vl[j]])
                            else:
                                sg_ = sgt[j % 2]
                                yo_ = ydo[j % 2]
                                S.op("act", lambda e: e.activation(sg_[:, 0:W], pg[:, 0:W], AF.Sigmoid, bias=v128[:, 109 + j:110 + j], scale=1.0),
                                     reads=[pg, v128], writes=[sg_])
                                tt(yo_[:, 0:W], vl[j - 3][:, 0:W], sg_[:, 0:W], ALU.mult, [vl[j - 3], sg_], [yo_], "dve")
                                S.dma("pool", G["ydT"].ap()[:, j - 3, t0:t0 + W], yo_[:, 0:W], reads=[yo_])
                S.barrier()


def phase5(nc, S, G):
    PS, layer, v128, modv, A2 = G["PS"], G["layer"], G["v128"], G["modv"], G["A2"]
    cF, cB = G["cF"], G["cB"]
    h_src, h_dst, last = G["h_src"], G["h_dst"], G["last"]
    BRK = G["BRK"]
    ysrc = [G["yaT"], G["ybT"], G["ycT"], G["ydT"]]
    with contextlib.ExitStack() as st:
        xn = S.sb(st, "m_xn", [128, KC, 512], BF16)
        ys = [S.sb(st, "m_y%d" % i, [128, BRK[i], 512], BF16) for i in range(4)]
        ht = S.sb(st, "m_h", [128, KC, 512], F32)
        acc = S.sb(st, "m_acc", [128, KC, 512], F32)
        mb = S.sb(st, "m_mb", [128, KC, 512], BF16)
        h1 = S.sb(st, "m_h1", [128, KC, 512], F32)
        sq = S.sb(st, "m_sq", [128, KC, 512], BF16)
        rstd = S.sb(st, "m_rstd", [128, 512], F32)
        xf = S.sb(st, "m_xf", [128, KC, 512], BF16)
        hid = S.sb(st, "m_hid", [128, 22, 512], BF16)
        h2 = [S.sb(st, "m_h2%d" % i, [128, 512], F32) for i in range(2)]
        sg = [S.sb(st, "m_sg%d" % i, [128, 512], F32) for i in range(2)]
        tm = [S.sb(st, "m_tm%d" % i, [128, 512], F32) for i in range(2)]
        wg = [S.sb(st, "m_wg%d" % i, [128, 4, KC, 128], BF16) for i in range(2)]
        wb = [S.sb(st, "m_wb%d" % i, [128, 15, 128], BF16) for i in range(2)]
        wo = [S.sb(st, "m_wo%d" % i, [128, KC, 128], BF16) for i in range(2)]
        wgu = [S.sb(st, "m_wgu%d" % i, [128, 2, KC, 128], BF16) for i in range(2)]
        wdn = [S.sb(st, "m_wdn%d" % i, [128, 22, 128], BF16) for i in range(2)]
        BOFF = [0, 4, 8, 12]
        npp = 0
        for (t0, W) in G["out_tiles"]:
            s = 1 if t0 < NCTX else 0
            S.dma("sp", xn[:, :, 0:W], G["xnT"].ap()[:, :, t0:t0 + W], writes=[xn])
            for i in range(4):
                S.dma("sp", ys[i][:, :, 0:W], ysrc[i].ap()[:, :, t0:t0 + W], writes=[ys[i]])
            S.dma("sp", ht[:, :, 0:W], h_src.ap().rearrange("(kc p) t -> p kc t", p=128)[:, :, t0:t0 + W], writes=[ht])
            for j in range(8):
                wgj, wbj = wg[j % 2], wb[j % 2]
                for i in range(4):
                    S.dma("sp", wgj[:, i, :, :], G["wg_bf"].ap()[i * 8 + j], writes=[wgj])
                    S.dma("sp", wbj[:, BOFF[i]:BOFF[i] + BRK[i], :], G["wb_bf"][i].ap()[j], writes=[wbj])
                for i in range(4):
                    pg, pb = PS[npp % 2], PS[2 + npp % 2]
                    sgi, tmi = sg[npp % 2], tm[npp % 2]
                    npp += 1
                    for kc in range(KC):
                        S.op("pe", lambda e: e.matmul(pg[:, 0:W], wgj[:, i, kc, :], xn[:, kc, 0:W], start=(kc == 0), stop=(kc == KC - 1)),
                             reads=[wgj, xn], writes=[pg])
                    for kc in range(BRK[i]):
                        S.op("pe", lambda e: e.matmul(pb[:, 0:W], wbj[:, BOFF[i] + kc, :], ys[i][:, kc, 0:W], start=(kc == 0), stop=(kc == BRK[i] - 1)),
                             reads=[wbj, ys[i]], writes=[pb])
                    S.op("act", lambda e: e.activation(sgi[:, 0:W], pg[:, 0:W], AF.Sigmoid), reads=[pg], writes=[sgi])
                    if i == 0:
                        S.op("dve", lambda e: e.tensor_tensor(acc[:, j, 0:W], sgi[:, 0:W], pb[:, 0:W], ALU.mult), reads=[sgi, pb], writes=[acc])
                    else:
                        S.op("dve", lambda e: e.tensor_tensor(tmi[:, 0:W], sgi[:, 0:W], pb[:, 0:W], ALU.mult), reads=[sgi, pb], writes=[tmi])
                        S.op("pool", lambda e: e.tensor_tensor(acc[:, j, 0:W], acc[:, j, 0:W], tmi[:, 0:W], ALU.add), reads=[acc, tmi], writes=[acc])
                S.op("act", lambda e: e.activation(mb[:, j, 0:W], acc[:, j, 0:W], AF.Copy), reads=[acc], writes=[mb])
            for j in range(8):
                woj = wo[j % 2]
                S.dma("sp", woj[:], G["wo_bf"].ap()[j], writes=[woj])
                po = PS[4 + j % 2]
                for kc in range(KC):
                    S.op("pe", lambda e: e.matmul(po[:, 0:W], woj[:, kc, :], mb[:, kc, 0:W], start=(kc == 0), stop=(kc == KC - 1)),
                         reads=[woj, mb], writes=[po])
                S.op("dve", lambda e: e.scalar_tensor_tensor(h1[:, j, 0:W], po[:, 0:W], modv[:, 16 + j, s:s + 1], ht[:, j, 0:W], ALU.mult, ALU.add),
                     reads=[po, modv, ht], writes=[h1])
            S.op("act", lambda e: e.activation(sq[:, :, 0:W], h1[:, :, 0:W], AF.Square), reads=[h1], writes=[sq])
            for kc in range(KC):
                S.op("pe", lambda e: e.matmul(PS[6][:, 0:W], cB(CI_ONES), sq[:, kc, 0:W], start=(kc == 0), stop=(kc == KC - 1)),
                     reads=[sq], writes=[PS[6]])
            S.op("act", lambda e: e.activation(rstd[:, 0:W], PS[6][:, 0:W], AF.Sqrt, bias=EPS, scale=1.0 / D), reads=[PS[6]], writes=[rstd])
            S.op("dve", lambda e: e.reciprocal(rstd[:, 0:W], rstd[:, 0:W]), reads=[rstd], writes=[rstd])
            for kc in range(KC):
                tmi = tm[kc % 2]
                S.op("dve", lambda e: e.tensor_tensor(tmi[:, 0:W], h1[:, kc, 0:W], rstd[:, 0:W], ALU.mult), reads=[h1, rstd], writes=[tmi])
                S.op("act", lambda e: e.activation(xf[:, kc, 0:W], tmi[:, 0:W], AF.Identity, bias=modv[:, 24 + kc, s:s + 1], scale=A2[:, kc, s:s + 1]),
                     reads=[tmi, modv, A2], writes=[xf])
            for jj in range(22):
                w = wgu[jj % 2]
                S.dma("sp", w[:, 0, :, :], G["wgu_bf"].ap()[jj], writes=[w])
                S.dma("sp", w[:, 1, :, :], G["wgu_bf"].ap()[22 + jj], writes=[w])
                pg, pu = PS[jj % 2], PS[2 + jj % 2]
                sgi = sg[jj % 2]
                for kc in range(KC):
                    S.op("pe", lambda e: e.matmul(pg[:, 0:W], w[:, 0, kc, :], xf[:, kc, 0:W], start=(kc == 0), stop=(kc == KC - 1)),
                         reads=[w, xf], writes=[pg])
                for kc in range(KC):
                    S.op("pe", lambda e: e.matmul(pu[:, 0:W], w[:, 1, kc, :], xf[:, kc, 0:W], start=(kc == 0), stop=(kc == KC - 1)),
                         reads=[w, xf], writes=[pu])
                S.op("act", lambda e: e.activation(sgi[:, 0:W], pg[:, 0:W], AF.Silu), reads=[pg], writes=[sgi])
                S.op("dve", lambda e: e.tensor_tensor(hid[:, jj, 0:W], sgi[:, 0:W], pu[:, 0:W], ALU.mult), reads=[sgi, pu], writes=[hid])
            for j in range(8):
                w = wdn[j % 2]
                S.dma("sp", w[:], G["wdn_bf"].ap()[j], writes=[w])
                po = PS[4 + j % 2]
                for kc in range(22):
                    S.op("pe", lambda e: e.matmul(po[:, 0:W], w[:, kc, :], hid[:, kc, 0:W], start=(kc == 0), stop=(kc == 21)),
                         reads=[w, hid], writes=[po])
                o = h2[j % 2]
                S.op("dve", lambda e: e.scalar_tensor_tensor(o[:, 0:W], po[:, 0:W], modv[:, 40 + j, s:s + 1], h1[:, j, 0:W], ALU.mult, ALU.add),
                     reads=[po, modv, h1], writes=[o])
                if last:
                    S.dma("pool", h_dst.ap()[j * 128:(j + 1) * 128, t0 - NCTX:t0 - NCTX + W], o[:, 0:W], reads=[o])
                else:
                    S.dma("pool", h_dst.ap()[j * 128:(j + 1) * 128, t0:t0 + W], o[:, 0:W], reads=[o])


def _prep_shared(inp):
    out = {}
    f32 = np.float32
    for half in (0, 1):
        d = {}
        dsel = [0, 1] if half == 0 else [1, 0]
        for k in ("w_mod", "w_gate", "w_br_ssd", "w_br_diff", "w_br_gqa", "w_br_s5", "w_out", "ffn_w_gate_up",
                  "ffn_w_down", "s5_glu_w"):
            d[k] = np.ascontiguousarray(inp[k], dtype=f32)
        w_in = np.array(inp["w_in"], dtype=f32)
        if half == 1:
            tmp = w_in[:, :, C_DT:C_DT + 8].copy()
            w_in[:, :, C_DT:C_DT + 8] = w_in[:, :, C_DT + 8:C_DT + 16]
            w_in[:, :, C_DT + 8:C_DT + 16] = tmp
        d["w_in"] = w_in
        v = np.zeros((2, 128, NV), f32)
        r = np.zeros((2, NR), f32)
        for l in range(2):
            v[l, :, 0:8] = inp["norm1_g"][l].reshape(8, 128).T
            v[l, :, 8:16] = inp["norm2_g"][l].reshape(8, 128).T
            v[l, :, 16:64] = inp["b_mod"][l].reshape(48, 128).T
            v[l, :, 64] = np.tile(inp["diff_qn_g"][l], 2)
            v[l, :, 65] = np.tile(inp["diff_kn_g"][l], 2)
            v[l, :, 66] = np.tile(inp["gqa_qn_g"][l], 2)
            v[l, :, 67] = np.tile(inp["gqa_kn_g"][l], 2)
            v[l, :, 68] = np.tile(inp["diff_subln_g"][l][:64], 2)
            v[l, :, 69] = np.tile(inp["diff_subln_g"][l][64:], 2)
            v[l, :, 70:76] = inp["ssd_conv_b"][l].reshape(6, 128).T
            cw = inp["ssd_conv_w"][l]
            if half == 1:
                cw = cw[::-1]
            v[l, :, 76:106] = cw.reshape(5, 6, 128).transpose(2, 1, 0).reshape(128, 30)
            v[l, :, 106:109] = inp["s5_d"][l].reshape(3, 128).T
            v[l, :, 109:115] = inp["s5_glu_b"][l].reshape(6, 128).T
            r[l, 0:16] = inp["ssd_a_log"][l][dsel].reshape(16)
            r[l, 16:32] = inp["ssd_dt_bias"][l][dsel].reshape(16)
            r[l, 32:40] = inp["ssd_d"][l]
            r[l, 40:552] = inp["ssd_norm_g"][l]
            r[l, 552:616] = inp["diff_lam_q1"][l]
            r[l, 616:680] = inp["diff_lam_k1"][l]
            r[l, 680:744] = inp["diff_lam_q2"][l]
            r[l, 744:808] = inp["diff_lam_k2"][l]
        d["vec128"] = v
        d["rowvecs"] = r

        def s5lay(a):
            a = np.asarray(a, f32)[:, dsel]
            return np.ascontiguousarray(a.reshape(2, 2, 12, 2, 64).transpose(0, 1, 3, 4, 2).reshape(2, 2, 128, 12))
        d["s5_lr"] = s5lay(inp["s5_lam_re"])
        d["s5_li"] = s5lay(inp["s5_lam_im"])
        ldt = np.asarray(inp["s5_log_dt"], f32)
        d["s5_ldt"] = s5lay(np.broadcast_to(ldt[..., None], (2, 2, 24, 64)))
        Bb = np.zeros((2, 2, 12, 128, 128), f32)
        Cb = np.zeros((2, 2, 12, 128, 128), f32)
        for ri, (bk, ck) in enumerate((("s5_b_re", "s5_c_re"), ("s5_b_im", "s5_c_im"))):
            b = np.asarray(inp[bk], f32)
            c = np.asarray(inp[ck], f32)
            for gp in range(12):
                for two in range(2):
                    g = 2 * gp + two
                    r0 = 32 * (gp % 4) + 16 * two
                    Bb[:, ri, gp, r0:r0 + 16, 64 * two:64 * two + 64] = b[:, g].transpose(0, 2, 1)
                    Cb[:, ri, gp, 64 * two:64 * two + 64, r0:r0 + 16] = c[:, g].transpose(0, 2, 1)
        d["s5_Bblk"] = Bb
        d["s5_Cblk"] = Cb
        ct, sn = rope_tables(half == 1)
        d["rope_cos"] = ct
        d["rope_sin"] = sn
        d["consts"] = make_consts()
        out[half] = d
    return out


def make_in_map(inp, shared, core):
    b, half = core // 2, core % 2
    x = np.asarray(inp["x"][b], np.float32)
    cx = np.asarray(inp["ctx"][b], np.float32)
    if half == 1:
        x = x[::-1]
        cx = cx[::-1]
    m = dict(shared[half])
    m["hT0"] = np.ascontiguousarray(np.concatenate([cx, x], 0).T)
    c2 = np.stack([np.asarray(inp["c"][b], np.float32), np.asarray(inp["c_ctx"], np.float32)], -1)
    m["c2"] = np.ascontiguousarray(c2.reshape(8, 128, 2).transpose(1, 0, 2))
    return m


_NC_CACHE = {}


def kernel(**inputs):
    if "nc" not in _NC_CACHE:
        _NC_CACHE["nc"] = build()
    nc = _NC_CACHE["nc"]
    shared = _prep_shared(inputs)
    in_maps = [make_in_map(inputs, shared, core) for core in range(8)]
    res = run_bass_kernel_spmd(nc, in_maps, core_ids=list(range(8)))
    out = np.zeros((4, NLAT, D), np.float32)
    n = N_OUT_TILES_LAST * 512
    for core in range(8):
        b, half = core // 2, core % 2
        y = np.asarray(res.results[core]["y"]).T
        if half == 0:
            out[b, :n] = y
        else:
            out[b, NLAT - n:] = y[::-1]
    return out
```

```python
import contextlib
import math
import numpy as np
import concourse.bass as bass
import concourse.mybir as mybir
from concourse.bass_utils import run_bass_kernel_spmd

F32 = mybir.dt.float32
BF16 = mybir.dt.bfloat16
AF = mybir.ActivationFunctionType
ALU = mybir.AluOpType
AX = mybir.AxisListType


class Tile:
    def __init__(self, S, h, name, psum=False):
        self.S = S
        self.h = h
        self.name = name
        self.psum = psum
        self.lw = None
        self.rd = {}
        self.dsem = None

    def __getitem__(self, k):
        return self.h[k]


class Sched:
    SAME_ENGINE_SYNC = ("dve", "act", "pool")

    def __init__(self, nc):
        self.nc = nc
        self.E = {"pe": nc.tensor, "dve": nc.vector, "act": nc.scalar, "pool": nc.gpsimd, "sp": nc.sync}
        self.sems = {}
        for k in self.E:
            self.sems[k] = [nc.alloc_semaphore("sem_" + k), 0]
        self.waited = {k: {} for k in self.E}
        self.free_dsems = []
        self.n_dsem = 0
        self.stack = []
        self.ninstr = 0
        self.inflight = {k: [] for k in self.E}
        self.MAXOUT = 6

    def sb(self, stack, name, shape, dtype):
        self.nalloc = getattr(self, "nalloc", 0) + 1
        name = "%s_%d" % (name, self.nalloc)
        h = stack.enter_context(self.nc.sbuf_tensor(name, list(shape), dtype))
        t = Tile(self, h, name)
        if not hasattr(self, "tiles"):
            self.tiles = []
        self.tiles.append(t)
        return t

    def mark(self):
        return len(getattr(self, "tiles", []))

    def release_since(self, mk):
        self.release(self.tiles[mk:])
        del self.tiles[mk:]

    def ps(self, stack, name, shape, dtype=F32):
        h = stack.enter_context(self.nc.psum_tensor(name, list(shape), dtype))
        return Tile(self, h, name, psum=True)

    def _dsem(self, t):
        if t.dsem is None:
            if self.free_dsems:
                t.dsem = self.free_dsems.pop()
            else:
                key = "d%d" % self.n_dsem
                self.n_dsem += 1
                self.sems[key] = [self.nc.alloc_semaphore("sem_" + key), 0]
                t.dsem = key
        return t.dsem

    def release(self, tiles):
        for t in tiles:
            if t.dsem is not None:
                self.free_dsems.append(t.dsem)
                t.dsem = None

    def _wait(self, eng, key, val):
        if val <= 0:
            return
        w = self.waited[eng]
        if w.get(key, 0) >= val:
            return
        self.E[eng].wait_ge(self.sems[key][0], val)
        w[key] = val
        self.ninstr += 1

    def _deps(self, eng, reads, writes):
        deps = {}
        def add(d):
            if d is None:
                return
            k, v = d
            if k == eng and eng not in self.SAME_ENGINE_SYNC:
                return
            if deps.get(k, 0) < v:
                deps[k] = v
        for t in reads:
            add(t.lw)
        for t in writes:
            add(t.lw)
            for k, v in t.rd.items():
                add((k, v))
        for k, v in deps.items():
            self._wait(eng, k, v)

    def op(self, eng, fn, reads=(), writes=()):
        self._deps(eng, reads, writes)
        ins = fn(self.E[eng])
        s = self.sems[eng]
        s[1] += 1
        ins.then_inc(s[0], 1)
        self.ninstr += 1
        me = (eng, s[1])
        for t in reads:
            if t.rd.get(eng, 0) < s[1]:
                t.rd[eng] = s[1]
        for t in writes:
            t.lw = me
            t.rd = {}
        return ins

    def dma(self, q, out, in_, reads=(), writes=(), **kw):
        self._deps(q, reads, writes)
        tl = (list(writes) + list(reads))
        assert tl, "dma needs an sbuf tile for its semaphore"
        key = self._dsem(tl[0])
        self._throttle(q)
        ins = self.E[q].dma_start(out=out, in_=in_, **kw)
        s = self.sems[key]
        s[1] += 16
        ins.then_inc(s[0], 16)
        self.ninstr += 1
        me = (key, s[1])
        self.inflight[q].append(me)
        for t in reads:
            if t.rd.get(key, 0) < s[1]:
                t.rd[key] = s[1]
        for t in writes:
            t.lw = me
            t.rd = {}
        return ins

    def _throttle(self, q):
        fl = self.inflight[q]
        while len(fl) >= self.MAXOUT:
            k, v = fl.pop(0)
            self._wait(q, k, v)

    def dma_dram(self, q, out, in_, **kw):
        key = "dd_" + q
        if key not in self.sems:
            self.sems[key] = [self.nc.alloc_semaphore("sem_" + key), 0]
        self._throttle(q)
        ins = self.E[q].dma_start(out=out, in_=in_, **kw)
        s = self.sems[key]
        s[1] += 16
        ins.then_inc(s[0], 16)
        self.ninstr += 1
        self.inflight[q].append((key, s[1]))
        return ins

    def barrier(self):
        for eng in self.E:
            for key, (h, cnt) in self.sems.items():
                if key == eng and eng not in self.SAME_ENGINE_SYNC and eng != "sp":
                    continue
                self._wait(eng, key, cnt)

D = 1024
NCTX = 256
NLAT = 8192
T = NCTX + NLAT
KC = 8
TILES = [(0, 256)] + [(256 + 512 * i, 512) for i in range(16)]
N_OUT_TILES_LAST = 8
EPS = 1e-6
INC = 3984
C_Z, C_XBC, C_DT, C_DQ, C_DK, C_DV, C_GQ, C_GK, C_GV, C_U = 0, 512, 1280, 1296, 1808, 2320, 2832, 3344, 3472, 3600
FH = 2816
NV = 115
NR = 808
CI_ID, CI_BLK64, CI_ROT, CI_TRI0, CI_TRI1, CI_MN0, CI_MN1, CI_SEL, CI_ONES = range(9)
NCONST = 9


def make_consts():
    c = np.zeros((NCONST, 128, 128), np.float32)
    c[CI_ID] = np.eye(128)
    c[CI_BLK64, :64, :64] = 1.0
    c[CI_BLK64, 64:, 64:] = 1.0
    for hb in (0, 64):
        for q0 in (0, 32):
            for i in range(16):
                c[CI_ROT, hb + q0 + 16 + i, hb + q0 + i] = -1.0
                c[CI_ROT, hb + q0 + i, hb + q0 + 16 + i] = 1.0
    k = np.arange(128)
    c[CI_TRI0] = (k[:, None] <= k[None, :])
    c[CI_TRI1] = (k[:, None] >= k[None, :])
    c[CI_MN0] = np.where(k[:, None] <= k[None, :], 0.0, -1e30)
    c[CI_MN1] = np.where(k[:, None] >= k[None, :], 0.0, -1e30)
    c[CI_SEL, 64, :] = 1.0
    c[CI_ONES] = 1.0
    return c


def rope_tables(flip):
    n_rows = NLAT // 64
    rows = np.repeat(np.arange(n_rows, dtype=np.float32), 64)
    cols = np.tile(np.arange(64, dtype=np.float32), n_rows)
    inv = (10000.0 ** (-np.arange(16, dtype=np.float32) / 16)).astype(np.float32)
    ar = rows[:, None] * inv
    ac = cols[:, None] * inv
    ang = np.concatenate([ar, ar, ac, ac], -1)
    cos = np.cos(ang).astype(np.float32)
    sin = np.sin(ang).astype(np.float32)
    if flip:
        cos = cos[::-1]
        sin = sin[::-1]
    ct = np.ones((128, T), np.float32)
    st = np.zeros((128, T), np.float32)
    ct[:64, NCTX:] = cos.T
    ct[64:, NCTX:] = cos.T
    st[:64, NCTX:] = sin.T
    st[64:, NCTX:] = sin.T
    return ct, st


def dview(t, pattern, **kw):
    return t.ap().rearrange(pattern, **kw)


def build(n_layers=2, phases=None, dbg=(), last_tiles=N_OUT_TILES_LAST, force_full=False, dbg_in=()):
    nc = bass.Bass("TRN2", target_bir_lowering=False)
    S = Sched(nc)
    dbg = set(dbg)
    allph = phases is None

    def want(p):
        return allph or p in phases

    def dram(name, shape, dt, kind=None):
        if kind is None:
            kind = "ExternalOutput" if name in dbg else ("ExternalInput" if name in dbg_in else "Internal")
        return nc.dram_tensor(name, list(shape), dt, kind=kind)

    def din(name, shape, dt=F32):
        return nc.dram_tensor(name, list(shape), dt, kind="ExternalInput")

    hT_in = din("hT0", [D, T])
    c2 = din("c2", [128, KC, 2])
    w_mod = din("w_mod", [2, D, 6 * D])
    w_in = din("w_in", [2, D, INC])
    w_gate = din("w_gate", [2, 4, D, D])
    w_br = [din("w_br_ssd", [2, 512, D]), din("w_br_diff", [2, 512, D]), din("w_br_gqa", [2, 512, D]),
            din("w_br_s5", [2, 384, D])]
    BRK = [4, 4, 4, 3]
    w_out = din("w_out", [2, D, D])
    w_gu = din("ffn_w_gate_up", [2, D, 2 * FH])
    w_dn = din("ffn_w_down", [2, FH, D])
    glu_w = din("s5_glu_w", [2, 384, 768])
    vec128 = din("vec128", [2, 128, NV])
    rowvecs = din("rowvecs", [2, NR])
    rope_c = din("rope_cos", [128, T])
    rope_s = din("rope_sin", [128, T])
    consts = din("consts", [NCONST, 128, 128])
    s5_lr = din("s5_lr", [2, 2, 128, 12])
    s5_li = din("s5_li", [2, 2, 128, 12])
    s5_ldt = din("s5_ldt", [2, 2, 128, 12])
    s5_B = din("s5_Bblk", [2, 2, 12, 128, 128])
    s5_C = din("s5_Cblk", [2, 2, 12, 128, 128])
    y_out = nc.dram_tensor("y", [D, last_tiles * 512], F32, kind="ExternalOutput")

    hT = [hT_in, dram("hT1", [D, T], F32), dram("hT2", [D, T], F32)]
    xnT = dram("xnT", [128, KC, T], BF16)
    xbcT = dram("xbcT", [128, 6, T], F32)
    qkT = dram("qkT", [128, 13, T], BF16)
    uT = dram("uT", [128, 3, T], F32)
    zt = dram("zt", [T, 512], F32)
    vaug = dram("vaug", [T, 10, 65], BF16)
    dtt = dram("dtt", [T, 16], F32)
    yaT = dram("yaT", [128, 4, T], BF16)
    ybT = dram("ybT", [128, 4, T], BF16)
    ycT = dram("ycT", [128, 4, T], BF16)
    ydT = dram("ydT", [128, 3, T], BF16)
    xtok = dram("xtok", [T, 512], BF16)
    btok = dram("btok", [T, 128], BF16)
    bcT = dram("bcT", [2, 2, 64, T], BF16)
    yssd = dram("yssd", [T, 512], F32)
    yssd1 = dram("yssd1", [T, 512], F32)
    ys5 = dram("ys5", [128, 3, T], F32)
    modv_d = dram("modv_d", [2, 128, 48, 2], F32)
    wg_bf = dram("wg_bf", [4 * 8, 128, 8, 128], BF16)
    wb_bf = [dram("wb_bf%d" % i, [8, 128, BRK[i], 128], BF16) for i in range(4)]
    wo_bf = dram("wo_bf", [8, 128, 8, 128], BF16)
    wgu_bf = dram("wgu_bf", [44, 128, 8, 128], BF16)
    wdn_bf = dram("wdn_bf", [8, 128, 22, 128], BF16)

    with contextlib.ExitStack() as gst:
        cst_f = S.sb(gst, "cst_f", [128, NCONST, 128], F32)
        cst_b = S.sb(gst, "cst_b", [128, NCONST, 128], BF16)
        S.dma("sp", cst_f[:], consts.ap().rearrange("c p m -> p c m"), writes=[cst_f])
        S.op("dve", lambda e: e.tensor_copy(cst_b[:], cst_f[:]), reads=[cst_f], writes=[cst_b])
        psall = gst.enter_context(nc.psum_tensor("psall", [128, 4096], F32))
        PS = [Tile(S, psall[:, i * 512:(i + 1) * 512], "ps%d" % i, psum=True) for i in range(8)]

        def cF(i):
            return cst_f[:, i, :]

        def cB(i):
            return cst_b[:, i, :]

        for layer in range(n_layers):
            last = (layer == n_layers - 1) and not force_full
            out_tiles = TILES[1:1 + last_tiles] if last else TILES
            h_src = hT[layer]
            h_dst = y_out if last else hT[layer + 1]
            lam_init = 0.8 - 0.6 * math.exp(-0.3 * layer)

            if want("W"):
                qs = ["pool"]
                n = 0
                for i in range(4):
                    for j in range(8):
                        S.dma_dram("pool", wg_bf.ap()[i * 8 + j],
                                   w_gate.ap()[layer, i, :, j * 128:(j + 1) * 128].rearrange("(kc p) m -> p kc m", p=128))
                    for j in range(8):
                        S.dma_dram("pool", wb_bf[i].ap()[j],
                                   w_br[i].ap()[layer, :, j * 128:(j + 1) * 128].rearrange("(kc p) m -> p kc m", p=128))
                for j in range(8):
                    S.dma_dram("pool", wo_bf.ap()[j],
                               w_out.ap()[layer, :, j * 128:(j + 1) * 128].rearrange("(kc p) m -> p kc m", p=128))
                    S.dma_dram("pool", wdn_bf.ap()[j],
                               w_dn.ap()[layer, :, j * 128:(j + 1) * 128].rearrange("(kc p) m -> p kc m", p=128))
                for j in range(44):
                    S.dma_dram("pool", wgu_bf.ap()[j],
                               w_gu.ap()[layer, :, j * 128:(j + 1) * 128].rearrange("(kc p) m -> p kc m", p=128))

            lmk = S.mark()
            with contextlib.ExitStack() as lst:
                v128 = S.sb(lst, "v128", [128, NV], F32)
                rows = S.sb(lst, "rows", [128, NR], F32)
                modv = S.sb(lst, "modv", [128, 48, 2], F32)
                A1 = S.sb(lst, "A1", [128, 8, 2], F32)
                A2 = S.sb(lst, "A2", [128, 8, 2], F32)
                lamc = S.sb(lst, "lamc", [128, 4], F32)
                subg = S.sb(lst, "subg", [128, 2], F32)
                aneg = S.sb(lst, "aneg", [128, 16], F32)
                S.dma("sp", v128[:], vec128.ap()[layer], writes=[v128])
                S.dma("sp", rows[:], rowvecs.ap()[layer].partition_broadcast(128), writes=[rows])
                if want("P"):
                    with contextlib.ExitStack() as st:
                        sc = S.sb(st, "sc", [128, KC, 2], F32)
                        S.dma("sp", sc[:], c2.ap(), writes=[sc])
                        S.op("act", lambda e: e.activation(sc[:], sc[:], AF.Silu), reads=[sc], writes=[sc])
                        wm = [S.sb(st, "wm%d" % i, [128, KC, 512], F32) for i in range(2)]
                        pm = PS[0]
                        for blk in range(12):
                            w = wm[blk % 2]
                            S.dma("sp", w[:], w_mod.ap()[layer, :, blk * 512:(blk + 1) * 512].rearrange("(kc p) n -> p kc n", p=128),
                                  writes=[w])
                            for jj in range(4):
                                j = blk * 4 + jj
                                for kc in range(KC):
                                    S.op("pe", lambda e, j=j, jj=jj, kc=kc, w=w: e.matmul(
                                        pm[:, j * 2:(j + 1) * 2], w[:, kc, jj * 128:(jj + 1) * 128], sc[:, kc, :],
                                        start=(kc == 0), stop=(kc == KC - 1)), reads=[w, sc], writes=[pm])
                        S.op("dve", lambda e: e.tensor_tensor(
                            modv[:], pm[:, 0:96].rearrange("p (j s) -> p j s", s=2),
                            v128[:, 16:64].unsqueeze(2).broadcast_to([128, 48, 2]), ALU.add),
                            reads=[pm, v128], writes=[modv])
                        S.dma("pool", modv_d.ap()[layer], modv[:], reads=[modv])
                else:
                    S.dma("sp", modv[:], modv_d.ap()[layer], writes=[modv])
                for (A, gcol, scj) in ((A1, 0, 8), (A2, 8, 32)):
                    S.op("dve", lambda e, A=A, scj=scj: e.tensor_scalar(A[:], modv[:, scj:scj + 8, :], 1.0, None, ALU.add),
                         reads=[modv], writes=[A])
                    S.op("dve", lambda e, A=A, gcol=gcol: e.tensor_tensor(
                        A[:], A[:], v128[:, gcol:gcol + 8].unsqueeze(2).broadcast_to([128, 8, 2]), ALU.mult),
                        reads=[A, v128], writes=[A])
                with contextlib.ExitStack() as st:
                    tmp = S.sb(st, "lamtmp", [128, 128], F32)
                    red = S.sb(st, "lamred", [128, 2], F32)
                    R0 = 552
                    S.op("dve", lambda e: e.tensor_tensor(tmp[:, 0:64], rows[:, R0:R0 + 64], rows[:, R0 + 64:R0 + 128], ALU.mult),
                         reads=[rows], writes=[tmp])
                    S.op("dve", lambda e: e.tensor_tensor(tmp[:, 64:128], rows[:, R0 + 128:R0 + 192], rows[:, R0 + 192:R0 + 256], ALU.mult),
                         reads=[rows, tmp], writes=[tmp])
                    S.op("dve", lambda e: e.reduce_sum(red[:], tmp[:].rearrange("p (a b) -> p a b", a=2), AX.X),
                         reads=[tmp], writes=[red])
                    S.op("act", lambda e: e.activation(red[:], red[:], AF.Exp), reads=[red], writes=[red])
                    S.op("dve", lambda e: e.tensor_tensor(lamc[:, 0:1], red[:, 1:2], red[:, 0:1], ALU.subtract),
                         reads=[red], writes=[lamc])
                    S.op("dve", lambda e: e.tensor_scalar(lamc[:, 0:1], lamc[:, 0:1], -lam_init, None, ALU.add),
                         reads=[lamc], writes=[lamc])
                    S.op("dve", lambda e: e.tensor_scalar(subg[:], v128[:, 68:70], 1.0 - lam_init, None, ALU.mult),
                         reads=[v128], writes=[subg])
                    S.op("act", lambda e: e.activation(aneg[:], rows[:, 0:16], AF.Exp), reads=[rows], writes=[aneg])
                    S.op("dve", lambda e: e.tensor_scalar(aneg[:], aneg[:], -1.0, None, ALU.mult), reads=[aneg], writes=[aneg])
                    S.barrier()

                for pname, pfn in (("1", phase1), ("2", phase2), ("3", phase3), ("4", phase4), ("5", phase5)):
                    if want(pname):
                        mk = S.mark()
                        pfn(nc, S, locals())
                        S.barrier()
                        S.release_since(mk)
                S.barrier()
            S.release_since(lmk)
        S.barrier()
    print("ninstr", S.ninstr, "dsems", S.n_dsem)
    return nc


def phase1(nc, S, G):
    PS, layer, v128, modv, A1 = G["PS"], G["layer"], G["v128"], G["modv"], G["A1"]
    cF, cB = G["cF"], G["cB"]
    h_src = G["h_src"]
    with contextlib.ExitStack() as st:
        win = S.sb(st, "win", [128, KC, INC], BF16)
        for c0 in range(0, INC, 512):
            c1 = min(INC, c0 + 512)
            S.dma("pool", win[:, :, c0:c1],
                  G["w_in"].ap()[layer, :, c0:c1].rearrange("(kc p) n -> p kc n", p=128), writes=[win])
        hts = [S.sb(st, "ht%d" % i, [128, KC, 512], F32) for i in range(2)]
        sq = S.sb(st, "sq", [128, KC, 512], BF16)
        rstd = S.sb(st, "rstd", [128, 512], F32)
        xns = [S.sb(st, "xn%d" % i, [128, KC, 512], BF16) for i in range(2)]
        cos_t = [S.sb(st, "cos%d" % i, [128, 512], F32) for i in range(2)]
        sin_t = [S.sb(st, "sin%d" % i, [128, 512], F32) for i in range(2)]
        stg = [S.sb(st, "stg%d" % i, [128, 512], F32) for i in range(4)]
        qsq = [S.sb(st, "qsq%d" % i, [128, 512], BF16) for i in range(2)]
        qy = [S.sb(st, "qy%d" % i, [128, 512], BF16) for i in range(2)]
        qr = [S.sb(st, "qr%d" % i, [128, 512], F32) for i in range(2)]
        qt1 = [S.sb(st, "qt1%d" % i, [128, 512], F32) for i in range(2)]
        qt2 = [S.sb(st, "qt2%d" % i, [128, 512], F32) for i in range(2)]
        qo = [S.sb(st, "qo%d" % i, [128, 512], BF16) for i in range(2)]
        vst = [S.sb(st, "vst%d" % i, [128, 10, 65], BF16) for i in range(2)]
        dst = [S.sb(st, "dst%d" % i, [128, 16], F32) for i in range(2)]
        for v in vst:
            S.op("pool", lambda e, v=v: e.memset(v[:], 1.0), writes=[v])
        nstg = 0
        nq = 0
        own_set = set(G["out_tiles"])
        for ti, (t0, W) in enumerate(TILES):
            s = 1 if t0 < NCTX else 0
            own = (t0, W) in own_set
            ht = hts[ti % 2]
            xn = xns[ti % 2]
            ct, sn = cos_t[ti % 2], sin_t[ti % 2]
            S.dma("sp", ht[:, :, 0:W], h_src.ap().rearrange("(kc p) t -> p kc t", p=128)[:, :, t0:t0 + W], writes=[ht])
            S.dma("sp", ct[:, 0:W], G["rope_c"].ap()[:, t0:t0 + W], writes=[ct])
            S.dma("sp", sn[:, 0:W], G["rope_s"].ap()[:, t0:t0 + W], writes=[sn])
            S.op("act", lambda e: e.activation(sq[:, :, 0:W], ht[:, :, 0:W], AF.Square), reads=[ht], writes=[sq])
            for kc in range(KC):
                S.op("pe", lambda e, kc=kc: e.matmul(PS[0][:, 0:W], cB(CI_ONES), sq[:, kc, 0:W], start=(kc == 0), stop=(kc == KC - 1)),
                     reads=[sq], writes=[PS[0]])
            S.op("act", lambda e: e.activation(rstd[:, 0:W], PS[0][:, 0:W], AF.Sqrt, bias=EPS, scale=1.0 / D), reads=[PS[0]], writes=[rstd])
            S.op("dve", lambda e: e.reciprocal(rstd[:, 0:W], rstd[:, 0:W]), reads=[rstd], writes=[rstd])
            S.op("dve", lambda e: e.tensor_tensor(ht[:, :, 0:W], ht[:, :, 0:W], rstd[:, 0:W].unsqueeze(1).broadcast_to([128, KC, W]), ALU.mult),
                 reads=[ht, rstd], writes=[ht])
            for kc in range(KC):
                S.op("act", lambda e, kc=kc: e.activation(xn[:, kc, 0:W], ht[:, kc, 0:W], AF.Identity,
                                                          bias=modv[:, kc, s:s + 1], scale=A1[:, kc, s:s + 1]),
                     reads=[ht, modv, A1], writes=[xn])
            if own:
                S.dma("pool", G["xnT"].ap()[:, :, t0:t0 + W], xn[:, :, 0:W], reads=[xn])

            def fm_group(col0, pst):
                for kc in range(KC):
                    S.op("pe", lambda e, kc=kc: e.matmul(pst[:, 0:W], win[:, kc, col0:col0 + 128], xn[:, kc, 0:W],
                                                         start=(kc == 0), stop=(kc == KC - 1)), reads=[win, xn], writes=[pst])
            ng = 0
            for c in range(6):
                pst = PS[1 + ng % 2]; ng += 1
                fm_group(C_XBC + 128 * c, pst)
                sg = stg[nstg % 4]; nstg += 1
                S.op("act", lambda e, sg=sg, pst=pst: e.activation(sg[:, 0:W], pst[:, 0:W], AF.Copy), reads=[pst], writes=[sg])
                S.dma("pool", G["xbcT"].ap()[:, c, t0:t0 + W], sg[:, 0:W], reads=[sg])
            for c in range(3):
                pst = PS[1 + ng % 2]; ng += 1
                fm_group(C_U + 128 * c, pst)
                sg = stg[nstg % 4]; nstg += 1
                S.op("dve", lambda e, sg=sg, pst=pst: e.tensor_copy(sg[:, 0:W], pst[:, 0:W]), reads=[pst], writes=[sg])
                S.dma("pool", G["uT"].ap()[:, c, t0:t0 + W], sg[:, 0:W], reads=[sg])
            for c in range(13):
                if not own and (c < 4 or 8 <= c < 12):
                    continue
                if c < 4:
                    col0, gcol = C_DQ + 128 * c, 64
                elif c < 8:
                    col0, gcol = C_DK + 128 * (c - 4), 65
                elif c < 12:
                    col0, gcol = C_GQ + 128 * (c - 8), 66
                else:
                    col0, gcol = C_GK, 67
                pst = PS[1 + ng % 2]; ng += 1
                fm_group(col0, pst)
                b = nq % 2; nq += 1
                a_sq, a_y, a_r, a_t1, a_t2, a_o = qsq[b], qy[b], qr[b], qt1[b], qt2[b], qo[b]
                S.op("act", lambda e: e.activation(a_sq[:, 0:W], pst[:, 0:W], AF.Square), reads=[pst], writes=[a_sq])
                S.op("act", lambda e: e.activation(a_y[:, 0:W], pst[:, 0:W], AF.Identity, scale=v128[:, gcol:gcol + 1]),
                     reads=[pst, v128], writes=[a_y])
                S.op("pe", lambda e: e.matmul(PS[3][:, 0:W], cB(CI_BLK64), a_sq[:, 0:W], start=True, stop=True), reads=[a_sq], writes=[PS[3]])
                S.op("pe", lambda e: e.matmul(PS[4][:, 0:W], cB(CI_ROT), a_y[:, 0:W], start=True, stop=True), reads=[a_y], writes=[PS[4]])
                S.op("act", lambda e: e.activation(a_r[:, 0:W], PS[3][:, 0:W], AF.Sqrt, bias=EPS, scale=1.0 / 64), reads=[PS[3]], writes=[a_r])
                S.op("dve", lambda e: e.reciprocal(a_r[:, 0:W], a_r[:, 0:W]), reads=[a_r], writes=[a_r])
                S.op("pool", lambda e: e.tensor_tensor(a_t1[:, 0:W], a_y[:, 0:W], ct[:, 0:W], ALU.mult), reads=[a_y, ct], writes=[a_t1])
                S.op("dve", lambda e: e.tensor_tensor(a_t2[:, 0:W], PS[4][:, 0:W], sn[:, 0:W], ALU.mult), reads=[PS[4], sn], writes=[a_t2])
                S.op("pool", lambda e: e.tensor_tensor(a_t1[:, 0:W], a_t1[:, 0:W], a_t2[:, 0:W], ALU.add), reads=[a_t1, a_t2], writes=[a_t1])
                S.op("dve", lambda e: e.tensor_tensor(a_o[:, 0:W], a_t1[:, 0:W], a_r[:, 0:W], ALU.mult), reads=[a_t1, a_r], writes=[a_o])
                S.dma("pool", G["qkT"].ap()[:, c, t0:t0 + W], a_o[:, 0:W], reads=[a_o])

            for blk in range(W // 128):
                bs = slice(blk * 128, (blk + 1) * 128)
                r0 = t0 + blk * 128
                vs_, ds_ = vst[blk % 2], dst[blk % 2]
                for kc in range(KC):
                    fl = dict(start=(kc == 0), stop=(kc == KC - 1))
                    if own:
                        S.op("pe", lambda e: e.matmul(PS[5][:, 0:512], xn[:, kc, bs], win[:, kc, C_Z:C_Z + 512], **fl),
                             reads=[win, xn], writes=[PS[5]])
                    S.op("pe", lambda e: e.matmul(PS[6][:, 0:512], xn[:, kc, bs], win[:, kc, C_DV:C_DV + 512], **fl),
                         reads=[win, xn], writes=[PS[6]])
                    S.op("pe", lambda e: e.matmul(PS[7][:, 0:128], xn[:, kc, bs], win[:, kc, C_GV:C_GV + 128], **fl),
                         reads=[win, xn], writes=[PS[7]])
                    S.op("pe", lambda e: e.matmul(PS[0][:, 0:16], xn[:, kc, bs], win[:, kc, C_DT:C_DT + 16], **fl),
                         reads=[win, xn], writes=[PS[0]])
                if own:
                    sg = stg[nstg % 4]; nstg += 1
                    S.op("act", lambda e, sg=sg: e.activation(sg[:, :], PS[5][:, :], AF.Copy), reads=[PS[5]], writes=[sg])
                    S.dma("pool", G["zt"].ap()[r0:r0 + 128, :], sg[:, :], reads=[sg])
                S.op("dve", lambda e: e.tensor_copy(vs_[:, 0:8, 0:64], PS[6][:, :].rearrange("p (g c) -> p g c", c=64)),
                     reads=[PS[6]], writes=[vs_])
                S.op("dve", lambda e: e.tensor_copy(vs_[:, 8:10, 0:64], PS[7][:, 0:128].rearrange("p (g c) -> p g c", c=64)),
                     reads=[PS[7]], writes=[vs_])
                S.op("act", lambda e: e.activation(ds_[:, :], PS[0][:, 0:16], AF.Copy), reads=[PS[0]], writes=[ds_])
                S.dma("pool", G["vaug"].ap()[r0:r0 + 128], vs_[:], reads=[vs_])
                S.dma("pool", G["dtt"].ap()[r0:r0 + 128, :], ds_[:, :], reads=[ds_])


def phase2(nc, S, G):
    PS, layer, v128 = G["PS"], G["layer"], G["v128"]
    cF, cB, lamc, subg = G["cF"], G["cB"], G["lamc"], G["subg"]
    out_tiles = G["out_tiles"]
    qkT, vaug = G["qkT"], G["vaug"]
    NKB = T // 128
    with contextlib.ExitStack() as st:
        kTs = [S.sb(st, "kT%d" % i, [128, T], BF16) for i in range(2)]
        vts = [S.sb(st, "vt%d" % i, [128, NKB, 2, 65], BF16) for i in range(2)]
        qts = [S.sb(st, "qt%d" % i, [128, 2, 512], BF16) for i in range(2)]
        pTw = [S.sb(st, "pTw%d" % i, [128, 2, 512], BF16) for i in range(2)]
        psall = G["psall"]
        xlo = [S.sb(st, "xlo%d" % i, [65, 512], F32) for i in range(2)]
        rinv = [S.sb(st, "rinv%d" % i, [64, 512], F32) for i in range(2)]
        olo = [S.sb(st, "olo%d" % i, [64, 512], F32) for i in range(2)]
        ohi = [S.sb(st, "ohi%d" % i, [64, 512], F32) for i in range(2)]
        dlo = S.sb(st, "dlo", [64, 512], F32)
        dhi = S.sb(st, "dhi", [64, 512], F32)
        sql = S.sb(st, "sql", [64, 512], BF16)
        sqh = S.sb(st, "sqh", [64, 512], BF16)
        rs = S.sb(st, "rs", [64, 512], F32)
        yo = [S.sb(st, "yo%d" % i, [64, 512], BF16) for i in range(4)]
        nrot = 0
        nyo = 0
        nq = 0
        for grp in range(6):
            kT, vt = kTs[grp % 2], vts[grp % 2]
            is_diff = grp < 4
            if is_diff:
                h = grp
                S.dma("sp", kT[:, :], qkT.ap()[:, 4 + h, :], writes=[kT])
                for k0 in range(0, NKB, 11):
                    S.dma("sp", vt[:, k0:k0 + 11, :, :],
                          vaug.ap()[k0 * 128:(k0 + 11) * 128, 2 * h:2 * h + 2, :].rearrange("(kb p) g c -> p kb g c", p=128), writes=[vt])
                units = [(0, 0), (0, 1)]
            else:
                n = grp - 4
                S.dma("sp", kT[0:64, :], qkT.ap()[64 * n:64 * n + 64, 12, :], writes=[kT])
                S.dma("sp", kT[64:128, :], qkT.ap()[64 * n:64 * n + 64, 12, :], writes=[kT])
                for k0 in range(0, NKB, 11):
                    S.dma("sp", vt[:, k0:k0 + 11, 0:1, :],
                          vaug.ap()[k0 * 128:(k0 + 11) * 128, 8 + n:9 + n, :].rearrange("(kb p) g c -> p kb g c", p=128), writes=[vt])
                units = [(0, 0), (0, 1), (1, 0), (1, 1)]
            for (t0, W) in out_tiles:
                qt = qts[nq % 2]; nq += 1
                if is_diff:
                    S.dma("sp", qt[:, 0, 0:W], qkT.ap()[:, h, t0:t0 + W], writes=[qt])
                else:
                    S.dma("sp", qt[:, :, 0:W], qkT.ap()[:, 8 + 2 * n:10 + 2 * n, t0:t0 + W], writes=[qt])
                kbs = [0, 1] if t0 < NCTX else list(range(NKB))
                subs = []
                for kb in kbs:
                    for u0 in range(0, len(units), 2):
                        subs.append((kb, [u0, u0 + 1]))

                def emit_s(i):
                    kb, us = subs[i]
                    r = i % 2
                    pw = pTw[r]
                    for k, ui in enumerate(us):
                        qc, half = units[ui]
                        hs = slice(64 * half, 64 * half + 64)
                        pS = PS[2 * r + k]
                        S.op("pe", lambda e: e.matmul(pS[:, 0:W], kT[hs, kb * 128:(kb + 1) * 128], qt[hs, qc, 0:W], start=True, stop=True),
                             reads=[kT, qt], writes=[pS])
                    if W == 512:
                        S.op("act", lambda e: e.activation(pw[:, :, :].rearrange("p a w -> p (a w)"),
                                                           psall[:, 2 * r * 512:(2 * r + 2) * 512], AF.Exp, scale=0.125),
                             reads=[PS[2 * r], PS[2 * r + 1]], writes=[pw])
                    else:
                        for k in range(2):
                            S.op("act", lambda e: e.activation(pw[:, k, 0:W], PS[2 * r + k][:, 0:W], AF.Exp, scale=0.125),
                                 reads=[PS[2 * r + k]], writes=[pw])

                def emit_pv(i):
                    kb, us = subs[i]
                    pw = pTw[i % 2]
                    fl = dict(start=(kb == kbs[0]), stop=(kb == kbs[-1]))
                    for k, ui in enumerate(us):
                        if is_diff:
                            a0, a1 = PS[4 + 2 * ui], PS[5 + 2 * ui]
                            S.op("pe", lambda e: e.matmul(a0[0:65, 0:W], vt[:, kb, 0, 0:65], pw[:, k, 0:W], **fl), reads=[vt, pw], writes=[a0])
                            S.op("pe", lambda e: e.matmul(a1[0:64, 0:W], vt[:, kb, 1, 0:64], pw[:, k, 0:W], **fl), reads=[vt, pw], writes=[a1])
                        else:
                            a0 = PS[4 + ui]
                            S.op("pe", lambda e: e.matmul(a0[0:65, 0:W], vt[:, kb, 0, 0:65], pw[:, k, 0:W], **fl), reads=[vt, pw], writes=[a0])

                for i in range(len(subs) + 1):
                    if i < len(subs):
                        emit_s(i)
                    if i >= 1:
                        emit_pv(i - 1)
                if is_diff:
                    for m in range(2):
                        a0, a1 = PS[4 + 2 * m], PS[5 + 2 * m]
                        S.op("act", lambda e: e.activation(xlo[m][0:65, 0:W], a0[0:65, 0:W], AF.Copy), reads=[a0], writes=[xlo[m]])
                        S.op("pe", lambda e: e.matmul(PS[0][0:64, 0:W], cF(CI_SEL)[0:65, 0:64], xlo[m][0:65, 0:W], start=True, stop=True),
                             reads=[xlo[m]], writes=[PS[0]])
                        S.op("dve", lambda e: e.reciprocal(rinv[m][:, 0:W], PS[0][0:64, 0:W]), reads=[PS[0]], writes=[rinv[m]])
                        S.op("dve", lambda e: e.tensor_tensor(olo[m][:, 0:W], xlo[m][0:64, 0:W], rinv[m][:, 0:W], ALU.mult),
                             reads=[xlo[m], rinv[m]], writes=[olo[m]])
                        S.op("dve", lambda e: e.tensor_tensor(ohi[m][:, 0:W], a1[0:64, 0:W], rinv[m][:, 0:W], ALU.mult),
                             reads=[a1, rinv[m]], writes=[ohi[m]])
                    S.op("dve", lambda e: e.scalar_tensor_tensor(dlo[:, 0:W], olo[1][:, 0:W], lamc[0:64, 0:1], olo[0][:, 0:W], ALU.mult, ALU.add),
                         reads=[olo[0], olo[1], lamc], writes=[dlo])
                    S.op("dve", lambda e: e.scalar_tensor_tensor(dhi[:, 0:W], ohi[1][:, 0:W], lamc[0:64, 0:1], ohi[0][:, 0:W], ALU.mult, ALU.add),
                         reads=[ohi[0], ohi[1], lamc], writes=[dhi])
                    S.op("act", lambda e: e.activation(sql[:, 0:W], dlo[:, 0:W], AF.Square), reads=[dlo], writes=[sql])
                    S.op("act", lambda e: e.activation(sqh[:, 0:W], dhi[:, 0:W], AF.Square), reads=[dhi], writes=[sqh])
                    S.op("pe", lambda e: e.matmul(PS[0][0:64, 0:W], cB(CI_ONES)[0:64, 0:64], sql[:, 0:W], start=True, stop=False),
                         reads=[sql], writes=[PS[0]])
                    S.op("pe", lambda e: e.matmul(PS[0][0:64, 0:W], cB(CI_ONES)[0:64, 0:64], sqh[:, 0:W], start=False, stop=True),
                         reads=[sqh], writes=[PS[0]])
                    S.op("act", lambda e: e.activation(rs[:, 0:W], PS[0][0:64, 0:W], AF.Sqrt, bias=EPS, scale=1.0 / 128), reads=[PS[0]], writes=[rs])
                    S.op("dve", lambda e: e.reciprocal(rs[:, 0:W], rs[:, 0:W]), reads=[rs], writes=[rs])
                    for (dd, col, p0) in ((dlo, 0, 0), (dhi, 1, 64)):
                        y = yo[nyo % 4]; nyo += 1
                        S.op("dve", lambda e: e.scalar_tensor_tensor(y[:, 0:W], dd[:, 0:W], subg[0:64, col:col + 1], rs[:, 0:W], ALU.mult, ALU.mult),
                             reads=[dd, subg, rs], writes=[y])
                        S.dma("pool", G["ybT"].ap()[p0:p0 + 64, h, t0:t0 + W], y[:, 0:W], reads=[y])
                else:
                    for j in range(4):
                        head = 4 * n + j
                        a0 = PS[4 + j]
                        m = j % 2
                        S.op("act", lambda e: e.activation(xlo[m][0:65, 0:W], a0[0:65, 0:W], AF.Copy), reads=[a0], writes=[xlo[m]])
                        S.op("pe", lambda e: e.matmul(PS[0][0:64, 0:W], cF(CI_SEL)[0:65, 0:64], xlo[m][0:65, 0:W], start=True, stop=True),
                             reads=[xlo[m]], writes=[PS[0]])
                        S.op("dve", lambda e: e.reciprocal(rinv[m][:, 0:W], PS[0][0:64, 0:W]), reads=[PS[0]], writes=[rinv[m]])
                        y = yo[nyo % 4]; nyo += 1
                        S.op("dve", lambda e: e.tensor_tensor(y[:, 0:W], xlo[m][0:64, 0:W], rinv[m][:, 0:W], ALU.mult),
                             reads=[xlo[m], rinv[m]], writes=[y])
                        p0 = 64 * (head % 2)
                        S.dma("pool", G["ycT"].ap()[p0:p0 + 64, head // 2, t0:t0 + W], y[:, 0:W], reads=[y])


def phase3(nc, S, G):
    PS, layer, v128, rows, aneg = G["PS"], G["layer"], G["v128"], G["rows"], G["aneg"]
    cF, cB = G["cF"], G["cB"]
    xbcT, bcT, xtok, btok, yssd = G["xbcT"], G["bcT"], G["xtok"], G["btok"], G["yssd"]
    NCH = T // 128
    with contextlib.ExitStack() as st:
        xin = [S.sb(st, "c_xin%d" % i, [128, 6, 516], F32) for i in range(2)]
        cv = S.sb(st, "c_cv", [128, 6, 512], F32)
        sl = [S.sb(st, "c_sl%d" % i, [128, 6, 512], BF16) for i in range(2)]
        xtk = [S.sb(st, "c_xtk%d" % i, [128, 640], BF16) for i in range(2)]
        psT = PS[7][:, :].bitcast(BF16)
        nb = 0
        for ti, (t0, W) in enumerate(TILES):
            seg0, seg1 = (0, NCTX) if t0 < NCTX else (NCTX, T)
            xi, sli = xin[ti % 2], sl[ti % 2]
            S.op("pool", lambda e: e.memset(xi[:, :, 0:2], 0.0), writes=[xi])
            S.op("pool", lambda e: e.memset(xi[:, :, W + 2:W + 4], 0.0), writes=[xi])
            lo = t0 - 2 if t0 > seg0 else t0
            hi = t0 + W + 2 if t0 + W < seg1 else t0 + W
            S.dma("sp", xi[:, :, lo - (t0 - 2):hi - (t0 - 2)], xbcT.ap()[:, :, lo:hi], writes=[xi])
            for c in range(6):
                wc = 76 + c * 5
                S.op("dve", lambda e: e.tensor_scalar(cv[:, c, 0:W], xi[:, c, 0:W], v128[:, wc:wc + 1], v128[:, 70 + c:71 + c], ALU.mult, ALU.add),
                     reads=[xi, v128], writes=[cv])
                for k in range(1, 5):
                    S.op("dve", lambda e: e.scalar_tensor_tensor(cv[:, c, 0:W], xi[:, c, k:k + W], v128[:, wc + k:wc + k + 1], cv[:, c, 0:W], ALU.mult, ALU.add),
                         reads=[xi, v128, cv], writes=[cv])
            S.op("act", lambda e: e.activation(sli[:, :, 0:W], cv[:, :, 0:W], AF.Silu), reads=[cv], writes=[sli])
            S.dma("pool", bcT.ap()[0].rearrange("g n t -> (g n) t")[:, t0:t0 + W], sli[:, 4, 0:W], reads=[sli])
            S.dma("pool", bcT.ap()[1].rearrange("g n t -> (g n) t")[:, t0:t0 + W], sli[:, 5, 0:W], reads=[sli])
            for blk in range(W // 128):
                r0 = t0 + blk * 128
                xt_ = xtk[nb % 2]; nb += 1
                for c in range(5):
                    S.op("pe", lambda e: e.transpose(psT[:, c * 128:(c + 1) * 128], sli[:, c, blk * 128:(blk + 1) * 128], cB(CI_ID)),
                         reads=[sli], writes=[PS[7]])
                S.op("act", lambda e: e.activation(xt_[:, :], psT[:, 0:640], AF.Copy), reads=[PS[7]], writes=[xt_])
                S.dma("pool", xtok.ap()[r0:r0 + 128, :], xt_[:, 0:512], reads=[xt_])
                S.dma("pool", btok.ap()[r0:r0 + 128, :], xt_[:, 512:640], reads=[xt_])
    S.barrier()
    yssd1 = G["yssd1"]
    with contextlib.ExitStack() as st:
        own_ck = set()
        for (t0_, W_) in G["out_tiles"]:
            own_ck.update(range(t0_ // 128, (t0_ + W_) // 128))
        last_own = max(own_ck)
        Dd = []
        for d in range(2):
            B = {}
            B["xk"] = [S.sb(st, "s_xk%d_%d" % (d, i), [128, 512], BF16) for i in range(2)]
            B["bk"] = [S.sb(st, "s_bk%d_%d" % (d, i), [128, 128], BF16) for i in range(2)]
            B["bct"] = [S.sb(st, "s_bct%d_%d" % (d, i), [64, 4, 128], BF16) for i in range(2)]
            B["dr"] = [S.sb(st, "s_dr%d_%d" % (d, i), [128, 16], F32) for i in range(2)]
            B["yv"] = [S.sb(st, "s_yv%d_%d" % (d, i), [128, 512], F32) for i in range(2)]
            for nm, shp, dt_ in (("av", [128, 16], F32), ("ab", [128, 8, 128], F32), ("ac", [128, 8], F32), ("ea", [128, 8], F32),
                                 ("cdec", [128, 8], F32), ("arg", [128, 8, 128], F32), ("dec", [128, 8, 128], F32),
                                 ("MT", [128, 8, 128], BF16), ("Bw", [128, 8, 64], BF16), ("xdt", [128, 8, 64], BF16),
                                 ("tmp", [128, 512], F32), ("h32", [64, 8, 64], F32), ("hb", [64, 8, 64], BF16), ("dte8", [128, 16], F32)):
                B[nm] = S.sb(st, "s_%s%d" % (nm, d), shp, dt_)
            B["n"] = 0
            B["pA"], B["pB"], B["pC"], B["pD"] = PS[4 * d], PS[4 * d + 1], PS[4 * d + 2], PS[4 * d + 3]
            S.op("pool", lambda e: e.memset(B["h32"][:], 0.0), writes=[B["h32"]])
            S.op("pool", lambda e: e.memset(B["hb"][:], 0.0), writes=[B["hb"]])
            Dd.append(B)
        orders = [[0, 1] + list(range(2, last_own + 1)), [1, 0] + list(range(NCH - 1, 1, -1))]

        def emit_chunk(d, ck):
            B = Dd[d]
            last_i = 127 if d == 0 else 0
            tri = cF(CI_TRI0 if d == 0 else CI_TRI1)
            mn = cF(CI_MN0 if d == 0 else CI_MN1)
            cols = slice(d * 8, d * 8 + 8)
            pA, pB, pC, pD = B["pA"], B["pB"], B["pC"], B["pD"]
            av, ab, ac, ea, cdec, arg, dec = B["av"], B["ab"], B["ac"], B["ea"], B["cdec"], B["arg"], B["dec"]
            MT, Bw, xdt, tmp, h32, hb, dte8 = B["MT"], B["Bw"], B["xdt"], B["tmp"], B["h32"], B["hb"], B["dte8"]
            c0 = ck * 128
            b = B["n"] % 2
            B["n"] += 1
            xk, bk, bct, dt = B["xk"][b], B["bk"][b], B["bct"][b], B["dr"][b]
            full = (d == 0) or (ck in own_ck)
            S.dma("sp", xk[:, :], xtok.ap()[c0:c0 + 128, :], writes=[xk])
            S.dma("sp", bk[:, :], btok.ap()[c0:c0 + 128, :], writes=[bk])
            if full:
                S.dma("sp", bct[:, :, :], bcT.ap()[:, :, :, c0:c0 + 128].rearrange("k g n t -> n (k g) t"), writes=[bct])
            S.dma("sp", dt[:, :], G["dtt"].ap()[c0:c0 + 128, :], writes=[dt])
            S.op("dve", lambda e: e.tensor_tensor(dt[:, :], dt[:, :], rows[:, 16:32], ALU.add), reads=[dt, rows], writes=[dt])
            S.op("act", lambda e: e.activation(dt[:, :], dt[:, :], AF.Exp), reads=[dt], writes=[dt])
            S.op("act", lambda e: e.activation(dt[:, :], dt[:, :], AF.Ln, bias=1.0, scale=1.0), reads=[dt], writes=[dt])
            S.op("dve", lambda e: e.tensor_tensor(av[:, :], dt[:, :], aneg[:, :], ALU.mult), reads=[dt, aneg], writes=[av])
            yield
            if full:
                S.op("pe", lambda e: e.matmul(pC[:, 0:8], tri, av[:, cols], start=True, stop=True), reads=[av], writes=[pC])
                S.op("dve", lambda e: e.tensor_copy(ab[:, :, :], av[:, cols].unsqueeze(2).broadcast_to([128, 8, 128])), reads=[av], writes=[ab])
                for h in range(8):
                    pr = pA if h < 4 else pB
                    S.op("pe", lambda e: e.matmul(pr[:, (h % 4) * 128:(h % 4 + 1) * 128], ab[:, h, :], tri, start=True, stop=True),
                         reads=[ab], writes=[pr])
                yield
                S.op("act", lambda e: e.activation(ac[:, :], pC[:, 0:8], AF.Copy), reads=[pC], writes=[ac])
                for h in range(8):
                    pr = pA if h < 4 else pB
                    S.op("dve", lambda e: e.scalar_tensor_tensor(arg[:, h, :], pr[:, (h % 4) * 128:(h % 4 + 1) * 128], ac[:, h:h + 1], mn,
                                                                 ALU.subtract, ALU.add), reads=[pr, ac], writes=[arg])
                yield
                S.op("act", lambda e: e.activation(dec[:, :, :], arg[:, :, :], AF.Exp), reads=[arg], writes=[dec])
                S.op("act", lambda e: e.activation(ea[:, :], ac[:, :], AF.Exp), reads=[ac], writes=[ea])
                for hh, pr in enumerate((pA, pB)):
                    S.op("act", lambda e: e.activation(cdec[:, hh * 4:hh * 4 + 4], pr[:, :].rearrange("p (h t) -> p h t", t=128)[:, :, last_i],
                                                       AF.Exp), reads=[pr], writes=[cdec])
                for g in range(2):
                    S.op("pe", lambda e: e.matmul(pC[:, 16 + g * 128:16 + (g + 1) * 128], bct[:, g, :], bct[:, 2 + g, :], start=True, stop=True),
                         reads=[bct], writes=[pC])
                yield
                for g in range(2):
                    S.op("dve", lambda e: e.tensor_tensor(MT[:, 4 * g:4 * g + 4, :], dec[:, 4 * g:4 * g + 4, :],
                                                          pC[:, 16 + g * 128:16 + (g + 1) * 128].unsqueeze(1).broadcast_to([128, 4, 128]), ALU.mult),
                         reads=[dec, pC], writes=[MT])
                    S.op("pool", lambda e: e.tensor_tensor(Bw[:, 4 * g:4 * g + 4, :],
                                                           bk[:, g * 64:(g + 1) * 64].unsqueeze(1).broadcast_to([128, 4, 64]),
                                                           dec[:, 4 * g:4 * g + 4, last_i:last_i + 1].broadcast_to([128, 4, 64]), ALU.mult),
                         reads=[bk, dec], writes=[Bw])
            else:
                S.op("pe", lambda e: e.matmul(pC[:, 0:8], tri, av[:, cols], start=True, stop=True), reads=[av], writes=[pC])
                S.op("pe", lambda e: e.matmul(pC[:, 8:16], cF(CI_ONES), av[:, cols], start=True, stop=True), reads=[av], writes=[pC])
                S.op("act", lambda e: e.activation(dte8[:, :], pC[:, 0:16], AF.Copy), reads=[pC], writes=[dte8])
                S.op("act", lambda e: e.activation(cdec[:, :], dte8[:, 8:16], AF.Exp), reads=[dte8], writes=[cdec])
                S.op("dve", lambda e: e.tensor_tensor(dte8[:, 0:8], dte8[:, 8:16], dte8[:, 0:8], ALU.subtract), reads=[dte8], writes=[dte8])
                S.op("act", lambda e: e.activation(dte8[:, 0:8], dte8[:, 0:8], AF.Exp), reads=[dte8], writes=[dte8])
                for g in range(2):
                    S.op("pool", lambda e: e.tensor_tensor(Bw[:, 4 * g:4 * g + 4, :],
                                                           bk[:, g * 64:(g + 1) * 64].unsqueeze(1).broadcast_to([128, 4, 64]),
                                                           dte8[:, 4 * g:4 * g + 4].unsqueeze(2).broadcast_to([128, 4, 64]), ALU.mult),
                         reads=[bk, dte8], writes=[Bw])
            S.op("pool", lambda e: e.tensor_tensor(xdt[:, :, :], xk[:, :].rearrange("p (h q) -> p h q", q=64),
                                                   dt[:, cols].unsqueeze(2).broadcast_to([128, 8, 64]), ALU.mult),
                 reads=[xk, dt], writes=[xdt])
            yield
            for h in range(8):
                g = h // 4
                hs = slice(h * 64, (h + 1) * 64)
                if full:
                    S.op("pe", lambda e: e.matmul(pA[:, hs], MT[:, h, :], xdt[:, h, :], start=True, stop=True), reads=[MT, xdt], writes=[pA])
                    S.op("pe", lambda e: e.matmul(pB[:, hs], bct[:, 2 + g, :], hb[:, h, :], start=True, stop=True), reads=[bct, hb], writes=[pB])
                S.op("pe", lambda e: e.matmul(pD[0:64, hs], Bw[:, h, :], xdt[:, h, :], start=True, stop=True), reads=[Bw, xdt], writes=[pD])
            yield
            y = B["yv"][b]
            if full:
                S.op("dve", lambda e: e.tensor_tensor(y[:, :].rearrange("p (h q) -> p h q", q=64), pB[:, :].rearrange("p (h q) -> p h q", q=64),
                                                      ea[:, :].unsqueeze(2).broadcast_to([128, 8, 64]), ALU.mult), reads=[pB, ea], writes=[y])
                S.op("dve", lambda e: e.tensor_tensor(y[:, :], y[:, :], pA[:, :], ALU.add), reads=[y, pA], writes=[y])
            S.op("dve", lambda e: e.tensor_tensor(h32[:, :, :], h32[:, :, :], cdec[0:64, :].unsqueeze(2).broadcast_to([64, 8, 64]), ALU.mult),
                 reads=[h32, cdec], writes=[h32])
            S.op("dve", lambda e: e.tensor_tensor(h32[:, :, :], h32[:, :, :], pD[0:64, :].rearrange("p (h q) -> p h q", q=64), ALU.add),
                 reads=[h32, pD], writes=[h32])
            S.op("act", lambda e: e.activation(hb[:, :, :], h32[:, :, :], AF.Copy), reads=[h32], writes=[hb])
            yield
            if not full:
                return
            if d == 0:
                S.op("pool", lambda e: e.tensor_tensor(tmp[:, :].rearrange("p (h q) -> p h q", q=64), xk[:, :].rearrange("p (h q) -> p h q", q=64),
                                                       rows[:, 32:40].unsqueeze(2).broadcast_to([128, 8, 64]), ALU.mult),
                     reads=[xk, rows], writes=[tmp])
                S.op("pool", lambda e: e.tensor_tensor(y[:, :], y[:, :], tmp[:, :], ALU.add), reads=[y, tmp], writes=[y])
                S.dma("pool", yssd.ap()[c0:c0 + 128, :], y[:, :], reads=[y])
            else:
                S.dma("pool", yssd1.ap()[c0:c0 + 128, :], y[:, :], reads=[y])

        for i in range(max(len(orders[0]), len(orders[1]))):
            active = [emit_chunk(d, orders[d][i]) for d in range(2) if i < len(orders[d])]
            while active:
                for g_ in list(active):
                    try:
                        next(g_)
                    except StopIteration:
                        active.remove(g_)
        S.barrier()
    with contextlib.ExitStack() as st:
        y0 = [S.sb(st, "f_y0%d" % i, [128, 512], F32) for i in range(2)]
        y1 = [S.sb(st, "f_y1%d" % i, [128, 512], F32) for i in range(2)]
        zk = [S.sb(st, "f_zk%d" % i, [128, 512], F32) for i in range(2)]
        tmp = S.sb(st, "f_tmp", [128, 512], F32)
        ss = [S.sb(st, "f_ss%d" % i, [128, 2], F32) for i in range(2)]
        yn = [S.sb(st, "f_yn%d" % i, [128, 512], BF16) for i in range(2)]
        yaS = [S.sb(st, "f_ya%d" % i, [128, 4, 128], BF16) for i in range(2)]
        for i, ck in enumerate(sorted(own_ck)):
            c0 = ck * 128
            b = i % 2
            psT = PS[6 + b][:, :].bitcast(BF16)
            S.dma("sp", y0[b][:, :], yssd.ap()[c0:c0 + 128, :], writes=[y0[b]])
            S.dma("sp", y1[b][:, :], yssd1.ap()[c0:c0 + 128, :], writes=[y1[b]])
            S.dma("sp", zk[b][:, :], G["zt"].ap()[c0:c0 + 128, :], writes=[zk[b]])
            y = y0[b]
            S.op("pool", lambda e: e.tensor_tensor(y[:, :], y[:, :], y1[b][:, :], ALU.add), reads=[y, y1[b]], writes=[y])
            S.op("act", lambda e: e.activation(zk[b][:, :], zk[b][:, :], AF.Silu), reads=[zk[b]], writes=[zk[b]])
            S.op("dve", lambda e: e.tensor_tensor(y[:, :], y[:, :], zk[b][:, :], ALU.mult), reads=[y, zk[b]], writes=[y])
            S.op("act", lambda e: e.activation(tmp[:, :], y[:, :], AF.Square, accum_out=ss[b][:, 0:1]), reads=[y], writes=[tmp, ss[b]])
            S.op("act", lambda e: e.activation(ss[b][:, 1:2], ss[b][:, 0:1], AF.Sqrt, bias=EPS, scale=1.0 / 512), reads=[ss[b]], writes=[ss[b]])
            S.op("dve", lambda e: e.reciprocal(ss[b][:, 1:2], ss[b][:, 1:2]), reads=[ss[b]], writes=[ss[b]])
            S.op("dve", lambda e: e.scalar_tensor_tensor(yn[b][:, :], y[:, :], ss[b][:, 1:2], rows[:, 40:552], ALU.mult, ALU.mult),
                 reads=[y, ss[b], rows], writes=[yn[b]])
            for c in range(4):
                S.op("pe", lambda e: e.transpose(psT[:, c * 128:(c + 1) * 128], yn[b][:, c * 128:(c + 1) * 128], cB(CI_ID)),
                     reads=[yn[b]], writes=[PS[6 + b]])
            ya = yaS[b]
            S.op("act", lambda e: e.activation(ya[:, :, :], psT[:, 0:512].rearrange("p (c t) -> p c t", t=128), AF.Copy),
                 reads=[PS[6 + b]], writes=[ya])
            S.dma("pool", G["yaT"].ap()[:, :, c0:c0 + 128], ya[:, :, :], reads=[ya])
        S.barrier()


def _rev(a, n):
    return bass.AP(a.tensor, a.offset + n - 1, [list(a.ap[0]), [-1, n]])


def phase4(nc, S, G):
    PS, layer, v128 = G["PS"], G["layer"], G["v128"]
    cF, cB = G["cF"], G["cB"]
    uT, ys5 = G["uT"], G["ys5"]
    I32 = mybir.dt.int32
    TWO_PI = 2.0 * math.pi
    with contextlib.ExitStack() as st:
        Bb = S.sb(st, "z_Bb", [128, 2, 12, 128], BF16)
        Cb = S.sb(st, "z_Cb", [128, 2, 12, 128], BF16)
        gw = S.sb(st, "z_gw", [128, 3, 768], BF16)
        for ri in range(2):
            S.dma("pool", Bb[:, ri, :, :], G["s5_B"].ap()[layer, ri].rearrange("gp k m -> k gp m"), writes=[Bb])
            S.dma("pool", Cb[:, ri, :, :], G["s5_C"].ap()[layer, ri].rearrange("gp k m -> k gp m"), writes=[Cb])
        S.dma("pool", gw[:, :, :], G["glu_w"].ap()[layer].rearrange("(kc p) n -> p kc n", p=128), writes=[gw])
        def tt(out, a, b, op, tiles_r, tiles_w, eng="dve"):
            S.op(eng, lambda e: e.tensor_tensor(out, a, b, op), reads=tiles_r, writes=tiles_w)

        def ts(out, a, s1, s2, op0, op1, tiles_r, tiles_w):
            if op1 is None:
                S.op("dve", lambda e: e.tensor_scalar(out, a, s1, None, op0), reads=tiles_r, writes=tiles_w)
            else:
                S.op("dve", lambda e: e.tensor_scalar(out, a, s1, s2, op0, op1), reads=tiles_r, writes=tiles_w)

        TL = 256
        ub = [S.sb(st, "z_ub%d" % i, [128, 3, 512], F32) for i in range(2)]
        ubb = [S.sb(st, "z_ubb%d" % i, [128, 3, 512], BF16) for i in range(2)]
        NB = 3
        br = [S.sb(st, "z_br%d" % i, [128, 512], F32) for i in range(NB)]
        bi = [S.sb(st, "z_bi%d" % i, [128, 512], F32) for i in range(NB)]
        m = [[S.sb(st, "z_m%d_%d" % (i, j), [128, 512], F32) for j in range(4)] for i in range(NB)]
        gr = [m[i][1] for i in range(NB)]
        gi = [m[i][3] for i in range(NB)]
        hrb = [S.sb(st, "z_hrb%d" % i, [128, 512], BF16) for i in range(NB)]
        hib = [S.sb(st, "z_hib%d" % i, [128, 512], BF16) for i in range(NB)]
        hst = S.sb(st, "z_hst", [128, 12, 2], F32)
        tn = S.sb(st, "z_tn", [128, 4], F32)
        yst = [S.sb(st, "z_yst%d" % i, [128, 512], F32) for i in range(2)]
        y0 = S.sb(st, "z_y0", [128, 3, 512], F32)
        gy = S.sb(st, "z_gy", [128, 3, 512], BF16)
        vl = [S.sb(st, "z_vl%d" % i, [128, 512], F32) for i in range(3)]
        sgt = [S.sb(st, "z_sg%d" % i, [128, 512], F32) for i in range(2)]
        ydo = [S.sb(st, "z_ydo%d" % i, [128, 512], BF16) for i in range(2)]
        nn = 0
        for d in range(2):
            with contextlib.ExitStack() as sd:
                Er = S.sb(sd, "z_Er", [128, 12, TL], F32)
                Ei = S.sb(sd, "z_Ei", [128, 12, TL], F32)
                Fr = S.sb(sd, "z_Fr", [128, 12, TL], F32)
                Fi = S.sb(sd, "z_Fi", [128, 12, TL], F32)
                rho = S.sb(sd, "z_rho", [128, 12], F32)
                EW = S.sb(sd, "z_EW", [128, 12, 2], F32)
                with contextlib.ExitStack() as st2:
                    t1 = S.sb(st2, "z_t1", [128, 12, TL], F32)
                    t2 = S.sb(st2, "z_t2", [128, 12, TL], F32)
                    names = ["lr", "li", "stp", "u", "f", "sphi", "s2", "c1", "ar", "ai", "den", "am1", "fr", "fi", "x1", "x2", "msk"]
                    P = {nm: S.sb(st2, "zp_%s" % nm, [128, 12], F32) for nm in names}
                    P["rho"] = rho
                    ki = S.sb(st2, "zp_ki", [128, 12], I32)
                    A = lambda nm: P[nm][:, :]
                    S.dma("sp", A("lr"), G["s5_lr"].ap()[layer, d], writes=[P["lr"]])
                    S.dma("sp", A("li"), G["s5_li"].ap()[layer, d], writes=[P["li"]])
                    S.dma("sp", A("stp"), G["s5_ldt"].ap()[layer, d], writes=[P["stp"]])
                    S.op("act", lambda e: e.activation(A("stp"), A("stp"), AF.Exp), reads=[P["stp"]], writes=[P["stp"]])
                    tt(A("rho"), A("lr"), A("stp"), ALU.mult, [P["lr"], P["stp"]], [P["rho"]])
                    S.op("act", lambda e: e.activation(A("rho"), A("rho"), AF.Exp), reads=[P["rho"]], writes=[P["rho"]])
                    tt(A("u"), A("li"), A("stp"), ALU.mult, [P["li"], P["stp"]], [P["u"]])
                    ts(A("u"), A("u"), 1.0 / TWO_PI, 0.5, ALU.mult, ALU.add, [P["u"]], [P["u"]])
                    S.op("dve", lambda e: e.tensor_copy(ki[:, :], A("u")), reads=[P["u"]], writes=[ki])
                    S.op("dve", lambda e: e.tensor_copy(A("f"), ki[:, :]), reads=[ki], writes=[P["f"]])
                    tt(A("f"), A("u"), A("f"), ALU.subtract, [P["u"], P["f"]], [P["f"]])
                    ts(A("msk"), A("f"), 0.5, None, ALU.is_ge, None, [P["f"]], [P["msk"]])
                    tt(A("f"), A("f"), A("msk"), ALU.subtract, [P["f"], P["msk"]], [P["f"]])
                    ts(A("f"), A("f"), -0.49999, 0.49999, ALU.max, ALU.min, [P["f"]], [P["f"]])
                    S.op("act", lambda e: e.activation(A("sphi"), A("f"), AF.Sin, scale=TWO_PI), reads=[P["f"]], writes=[P["sphi"]])
                    S.op("act", lambda e: e.activation(A("s2"), A("f"), AF.Sin, scale=math.pi), reads=[P["f"]], writes=[P["s2"]])
                    tt(A("c1"), A("s2"), A("s2"), ALU.mult, [P["s2"]], [P["c1"]])
                    ts(A("c1"), A("c1"), 2.0, -1.0, ALU.mult, ALU.add, [P["c1"]], [P["c1"]])
                    S.op("dve", lambda e: e.tensor_copy(Er[:, :, 0], A("c1")), reads=[P["c1"]], writes=[Er])
                    S.op("dve", lambda e: e.tensor_copy(Ei[:, :, 0], A("sphi")), reads=[P["sphi"]], writes=[Ei])
                    tt(A("ar"), A("rho"), A("c1"), ALU.mult, [P["rho"], P["c1"]], [P["ar"]])
                    tt(A("ai"), A("rho"), A("sphi"), ALU.mult, [P["rho"], P["sphi"]], [P["ai"]])
                    ts(A("ai"), A("ai"), -1.0, None, ALU.mult, None, [P["ai"]], [P["ai"]])
                    tt(A("den"), A("lr"), A("lr"), ALU.mult, [P["lr"]], [P["den"]])
                    tt(A("x1"), A("li"), A("li"), ALU.mult, [P["li"]], [P["x1"]])
                    tt(A("den"), A("den"), A("x1"), ALU.add, [P["den"], P["x1"]], [P["den"]])
                    S.op("dve", lambda e: e.reciprocal(A("den"), A("den")), reads=[P["den"]], writes=[P["den"]])
                    ts(A("am1"), A("ar"), -1.0, None, ALU.add, None, [P["ar"]], [P["am1"]])
                    tt(A("x1"), A("am1"), A("lr"), ALU.mult, [P["am1"], P["lr"]], [P["x1"]])
                    tt(A("x2"), A("ai"), A("li"), ALU.mult, [P["ai"], P["li"]], [P["x2"]])
                    tt(A("fr"), A("x1"), A("x2"), ALU.add, [P["x1"], P["x2"]], [P["fr"]])
                    tt(A("fr"), A("fr"), A("den"), ALU.mult, [P["fr"], P["den"]], [P["fr"]])
                    tt(A("x1"), A("ai"), A("lr"), ALU.mult, [P["ai"], P["lr"]], [P["x1"]])
                    tt(A("x2"), A("am1"), A("li"), ALU.mult, [P["am1"], P["li"]], [P["x2"]])
                    tt(A("fi"), A("x1"), A("x2"), ALU.subtract, [P["x1"], P["x2"]], [P["fi"]])
                    tt(A("fi"), A("fi"), A("den"), ALU.mult, [P["fi"], P["den"]], [P["fi"]])
                    n = 1
                    while n < TL:
                        cr = Er[:, :, n - 1:n].broadcast_to([128, 12, n])
                        ci = Ei[:, :, n - 1:n].broadcast_to([128, 12, n])
                        tt(t1[:, :, 0:n], Er[:, :, 0:n], cr, ALU.mult, [Er], [t1])
                        tt(t2[:, :, 0:n], Ei[:, :, 0:n], ci, ALU.mult, [Ei], [t2])
                        tt(Er[:, :, n:2 * n], t1[:, :, 0:n], t2[:, :, 0:n], ALU.subtract, [t1, t2], [Er])
                        tt(t1[:, :, 0:n], Er[:, :, 0:n], ci, ALU.mult, [Er, Ei], [t1])
                        tt(t2[:, :, 0:n], Ei[:, :, 0:n], cr, ALU.mult, [Ei, Er], [t2])
                        tt(Ei[:, :, n:2 * n], t1[:, :, 0:n], t2[:, :, 0:n], ALU.add, [t1, t2], [Ei])
                        n *= 2
                    S.op("dve", lambda e: e.tensor_copy(EW[:, :, 0], Er[:, :, TL - 1]), reads=[Er], writes=[EW])
                    S.op("dve", lambda e: e.tensor_copy(EW[:, :, 1], Ei[:, :, TL - 1]), reads=[Ei], writes=[EW])
                    frb = P["fr"][:, :].unsqueeze(2).broadcast_to([128, 12, TL])
                    fib = P["fi"][:, :].unsqueeze(2).broadcast_to([128, 12, TL])
                    tt(t1[:, :, :], Er[:, :, :], frb, ALU.mult, [Er, P["fr"]], [t1])
                    tt(t2[:, :, :], Ei[:, :, :], fib, ALU.mult, [Ei, P["fi"]], [t2])
                    tt(Fr[:, :, :], t1[:, :, :], t2[:, :, :], ALU.subtract, [t1, t2], [Fr])
                    tt(t1[:, :, :], Ei[:, :, :], frb, ALU.mult, [Ei, P["fr"]], [t1])
                    tt(t2[:, :, :], Er[:, :, :], fib, ALU.mult, [Er, P["fi"]], [t2])
                    tt(Fi[:, :, :], t1[:, :, :], t2[:, :, :], ALU.add, [t1, t2], [Fi])
                    if d == 1:
                        for tb in (Er, Ei, Fr, Fi):
                            for gp in range(12):
                                S.op("dve", lambda e: e.tensor_copy(t1[:, gp, :], _rev(tb[:, gp, :], TL)), reads=[tb], writes=[t1])
                            S.op("dve", lambda e: e.tensor_copy(tb[:, :, :], t1[:, :, :]), reads=[t1], writes=[tb])
                    S.barrier()

                own_set = set(G["out_tiles"]) | {TILES[0]}
                if d == 0:
                    last_own = max(i for i, tl in enumerate(TILES) if tl in own_set)
                    tiles = TILES[:last_own + 1]
                else:
                    tiles = [TILES[0]] + TILES[:0:-1]
                S.op("pool", lambda e: e.memset(hst[:], 0.0), writes=[hst])
                for ti, (t0, W) in enumerate(tiles):
                    nfr = W // TL
                    full = (t0, W) in set(G["out_tiles"]) or d == 0
                    u_, ub_ = ub[ti % 2], ubb[ti % 2]
                    S.dma("sp", u_[:, :, 0:W], uT.ap()[:, :, t0:t0 + W], writes=[u_])
                    S.op("act", lambda e: e.activation(ub_[:, :, 0:W], u_[:, :, 0:W], AF.Copy), reads=[u_], writes=[ub_])
                    if d == 1 and full:
                        S.dma("sp", y0[:, :, 0:W], ys5.ap()[:, :, t0:t0 + W], writes=[y0])
                    for gp in range(12):
                        uc = gp // 4
                        b = nn % NB; nn += 1
                        mm = m[b]
                        S.op("pe", lambda e: e.matmul(PS[0][:, 0:W], Bb[:, 0, gp, :], ub_[:, uc, 0:W], start=True, stop=True), reads=[Bb, ub_], writes=[PS[0]])
                        S.op("pe", lambda e: e.matmul(PS[1][:, 0:W], Bb[:, 1, gp, :], ub_[:, uc, 0:W], start=True, stop=True), reads=[Bb, ub_], writes=[PS[1]])
                        S.op("act", lambda e: e.activation(br[b][:, 0:W], PS[0][:, 0:W], AF.Copy), reads=[PS[0]], writes=[br[b]])
                        S.op("act", lambda e: e.activation(bi[b][:, 0:W], PS[1][:, 0:W], AF.Copy), reads=[PS[1]], writes=[bi[b]])

                        def bc(tb):
                            return tb[:, gp, :].unsqueeze(1).broadcast_to([128, nfr, TL])

                        def v3(t):
                            return t[:, 0:W].rearrange("p (c k) -> p c k", k=TL)
                        tt(v3(mm[0]), v3(br[b]), bc(Fr), ALU.mult, [br[b], Fr], [mm[0]], "dve")
                        tt(v3(mm[1]), v3(bi[b]), bc(Fi), ALU.mult, [bi[b], Fi], [mm[1]], "dve")
                        tt(v3(mm[2]), v3(bi[b]), bc(Fr), ALU.mult, [bi[b], Fr], [mm[2]], "dve")
                        tt(v3(mm[3]), v3(br[b]), bc(Fi), ALU.mult, [br[b], Fi], [mm[3]], "dve")
                        tt(mm[0][:, 0:W], mm[0][:, 0:W], mm[1][:, 0:W], ALU.subtract, [mm[0], mm[1]], [mm[0]], "dve")
                        tt(mm[2][:, 0:W], mm[2][:, 0:W], mm[3][:, 0:W], ALU.add, [mm[2], mm[3]], [mm[2]], "dve")
                        frs = list(range(nfr)) if d == 0 else list(range(nfr - 1, -1, -1))
                        for fk in frs:
                            cs = slice(fk * TL, (fk + 1) * TL)
                            for (gt, vt_, comp) in ((gr[b], mm[0], 0), (gi[b], mm[2], 1)):
                                o_ap, v_ap = gt[:, cs], vt_[:, cs]
                                if d == 1:
                                    o_ap, v_ap = _rev(o_ap, TL), _rev(v_ap, TL)
                                S.op("dve", lambda e: e.tensor_tensor_scan(o_ap, rho[:, gp:gp + 1].broadcast_to([128, TL]), v_ap,
                                                                           hst[:, gp, comp:comp + 1], ALU.mult, ALU.add),
                                     reads=[rho, vt_, hst], writes=[gt])
                            ie = fk * TL + (TL - 1 if d == 0 else 0)
                            gre, gie = gr[b][:, ie:ie + 1], gi[b][:, ie:ie + 1]
                            e_r, e_i = EW[:, gp, 0:1], EW[:, gp, 1:2]
                            tt(tn[:, 0:1], gie, e_i, ALU.mult, [gi[b], EW], [tn], "pool")
                            tt(tn[:, 1:2], gre, e_i, ALU.mult, [gr[b], EW], [tn], "pool")
                            tt(tn[:, 2:3], gre, e_r, ALU.mult, [gr[b], EW], [tn], "pool")
                            tt(tn[:, 3:4], gie, e_r, ALU.mult, [gi[b], EW], [tn], "pool")
                            tt(hst[:, gp, 0:1], tn[:, 2:3], tn[:, 0:1], ALU.add, [tn], [hst], "pool")
                            tt(hst[:, gp, 1:2], tn[:, 3:4], tn[:, 1:2], ALU.subtract, [tn], [hst], "pool")
                        if not full:
                            continue
                        tt(v3(mm[0]), v3(gr[b]), bc(Er), ALU.mult, [gr[b], Er], [mm[0]], "dve")
                        tt(v3(mm[2]), v3(gi[b]), bc(Ei), ALU.mult, [gi[b], Ei], [mm[2]], "dve")
                        tt(v3(br[b]), v3(gr[b]), bc(Ei), ALU.mult, [gr[b], Ei], [br[b]], "dve")
                        tt(v3(bi[b]), v3(gi[b]), bc(Er), ALU.mult, [gi[b], Er], [bi[b]], "dve")
                        tt(hrb[b][:, 0:W], mm[0][:, 0:W], mm[2][:, 0:W], ALU.add, [mm[0], mm[2]], [hrb[b]], "dve")
                        tt(hib[b][:, 0:W], br[b][:, 0:W], bi[b][:, 0:W], ALU.subtract, [br[b], bi[b]], [hib[b]], "dve")
                        py = PS[2 + uc]
                        S.op("pe", lambda e: e.matmul(py[:, 0:W], Cb[:, 0, gp, :], hrb[b][:, 0:W], start=(gp % 4 == 0), stop=False), reads=[Cb, hrb[b]], writes=[py])
                        S.op("pe", lambda e: e.matmul(py[:, 0:W], Cb[:, 1, gp, :], hib[b][:, 0:W], start=False, stop=(gp % 4 == 3)), reads=[Cb, hib[b]], writes=[py])
                    if not full:
                        continue
                    for uc in range(3):
                        py = PS[2 + uc]
                        ys_ = yst[uc % 2]
                        if d == 0:
                            S.op("dve", lambda e: e.scalar_tensor_tensor(ys_[:, 0:W], u_[:, uc, 0:W], v128[:, 106 + uc:107 + uc], py[:, 0:W], ALU.mult, ALU.add),
                                 reads=[u_, v128, py], writes=[ys_])
                            S.dma("pool", ys5.ap()[:, uc, t0:t0 + W], ys_[:, 0:W], reads=[ys_])
                        else:
                            tt(ys_[:, 0:W], y0[:, uc, 0:W], py[:, 0:W], ALU.add, [y0, py], [ys_], "dve")
                            S.op("act", lambda e: e.activation(gy[:, uc, 0:W], ys_[:, 0:W], AF.Gelu_apprx_tanh), reads=[ys_], writes=[gy])
                    if d == 1:
                        for j in range(6):
                            pg = PS[5 + j % 2]
                            for kc in range(3):
                                S.op("pe", lambda e: e.matmul(pg[:, 0:W], gw[:, kc, j * 128:(j + 1) * 128], gy[:, kc, 0:W], start=(kc == 0), stop=(kc == 2)),
                                     reads=[gw, gy], writes=[pg])
                            if j < 3:
                                S.op("act", lambda e: e.activation(vl[j][:, 0:W], pg[:, 0:W], AF.Identity, bias=v128[:, 109 + j:110 + j], scale=1.0),
                                     reads=[pg, v128], writes=[vl[j]])
                            else:
                                sg_ = sgt[j % 2]
                                yo_ = ydo[j % 2]
                                S.op("act", lambda e: e.activation(sg_[:, 0:W], pg[:, 0:W], AF.Sigmoid, bias=v128[:, 109 + j:110 + j], scale=1.0),
                                     reads=[pg, v128], writes=[sg_])
                                tt(yo_[:, 0:W], vl[j - 3][:, 0:W], sg_[:, 0:W], ALU.mult, [vl[j - 3], sg_], [yo_], "dve")
                                S.dma("pool", G["ydT"].ap()[:, j - 3, t0:t0 + W], yo_[:, 0:W], reads=[yo_])
                S.barrier()


def phase5(nc, S, G):
    PS, layer, v128, modv, A2 = G["PS"], G["layer"], G["v128"], G["modv"], G["A2"]
    cF, cB = G["cF"], G["cB"]
    h_src, h_dst, last = G["h_src"], G["h_dst"], G["last"]
    BRK = G["BRK"]
    ysrc = [G["yaT"], G["ybT"], G["ycT"], G["ydT"]]
    with contextlib.ExitStack() as st:
        xn = S.sb(st, "m_xn", [128, KC, 512], BF16)
        ys = [S.sb(st, "m_y%d" % i, [128, BRK[i], 512], BF16) for i in range(4)]
        ht = S.sb(st, "m_h", [128, KC, 512], F32)
        acc = S.sb(st, "m_acc", [128, KC, 512], F32)
        mb = S.sb(st, "m_mb", [128, KC, 512], BF16)
        h1 = S.sb(st, "m_h1", [128, KC, 512], F32)
        sq = S.sb(st, "m_sq", [128, KC, 512], BF16)
        rstd = S.sb(st, "m_rstd", [128, 512], F32)
        xf = S.sb(st, "m_xf", [128, KC, 512], BF16)
        hid = S.sb(st, "m_hid", [128, 22, 512], BF16)
        h2 = [S.sb(st, "m_h2%d" % i, [128, 512], F32) for i in range(2)]
        sg = [S.sb(st, "m_sg%d" % i, [128, 512], F32) for i in range(2)]
        tm = [S.sb(st, "m_tm%d" % i, [128, 512], F32) for i in range(2)]
        wg = [S.sb(st, "m_wg%d" % i, [128, 4, KC, 128], BF16) for i in range(2)]
        wb = [S.sb(st, "m_wb%d" % i, [128, 15, 128], BF16) for i in range(2)]
        wo = [S.sb(st, "m_wo%d" % i, [128, KC, 128], BF16) for i in range(2)]
        wgu = [S.sb(st, "m_wgu%d" % i, [128, 2, KC, 128], BF16) for i in range(2)]
        wdn = [S.sb(st, "m_wdn%d" % i, [128, 22, 128], BF16) for i in range(2)]
        BOFF = [0, 4, 8, 12]
        npp = 0
        for (t0, W) in G["out_tiles"]:
            s = 1 if t0 < NCTX else 0
            S.dma("sp", xn[:, :, 0:W], G["xnT"].ap()[:, :, t0:t0 + W], writes=[xn])
            for i in range(4):
                S.dma("sp", ys[i][:, :, 0:W], ysrc[i].ap()[:, :, t0:t0 + W], writes=[ys[i]])
            S.dma("sp", ht[:, :, 0:W], h_src.ap().rearrange("(kc p) t -> p kc t", p=128)[:, :, t0:t0 + W], writes=[ht])
            for j in range(8):
                wgj, wbj = wg[j % 2], wb[j % 2]
                for i in range(4):
                    S.dma("sp", wgj[:, i, :, :], G["wg_bf"].ap()[i * 8 + j], writes=[wgj])
                    S.dma("sp", wbj[:, BOFF[i]:BOFF[i] + BRK[i], :], G["wb_bf"][i].ap()[j], writes=[wbj])
                for i in range(4):
                    pg, pb = PS[npp % 2], PS[2 + npp % 2]
                    sgi, tmi = sg[npp % 2], tm[npp % 2]
                    npp += 1
                    for kc in range(KC):
                        S.op("pe", lambda e: e.matmul(pg[:, 0:W], wgj[:, i, kc, :], xn[:, kc, 0:W], start=(kc == 0), stop=(kc == KC - 1)),
                             reads=[wgj, xn], writes=[pg])
                    for kc in range(BRK[i]):
                        S.op("pe", lambda e: e.matmul(pb[:, 0:W], wbj[:, BOFF[i] + kc, :], ys[i][:, kc, 0:W], start=(kc == 0), stop=(kc == BRK[i] - 1)),
                             reads=[wbj, ys[i]], writes=[pb])
                    S.op("act", lambda e: e.activation(sgi[:, 0:W], pg[:, 0:W], AF.Sigmoid), reads=[pg], writes=[sgi])
                    if i == 0:
                        S.op("dve", lambda e: e.tensor_tensor(acc[:, j, 0:W], sgi[:, 0:W], pb[:, 0:W], ALU.mult), reads=[sgi, pb], writes=[acc])
                    else:
                        S.op("dve", lambda e: e.tensor_tensor(tmi[:, 0:W], sgi[:, 0:W], pb[:, 0:W], ALU.mult), reads=[sgi, pb], writes=[tmi])
                        S.op("pool", lambda e: e.tensor_tensor(acc[:, j, 0:W], acc[:, j, 0:W], tmi[:, 0:W], ALU.add), reads=[acc, tmi], writes=[acc])
                S.op("act", lambda e: e.activation(mb[:, j, 0:W], acc[:, j, 0:W], AF.Copy), reads=[acc], writes=[mb])
            for j in range(8):
                woj = wo[j % 2]
                S.dma("sp", woj[:], G["wo_bf"].ap()[j], writes=[woj])
                po = PS[4 + j % 2]
                for kc in range(KC):
                    S.op("pe", lambda e: e.matmul(po[:, 0:W], woj[:, kc, :], mb[:, kc, 0:W], start=(kc == 0), stop=(kc == KC - 1)),
                         reads=[woj, mb], writes=[po])
                S.op("dve", lambda e: e.scalar_tensor_tensor(h1[:, j, 0:W], po[:, 0:W], modv[:, 16 + j, s:s + 1], ht[:, j, 0:W], ALU.mult, ALU.add),
                     reads=[po, modv, ht], writes=[h1])
            S.op("act", lambda e: e.activation(sq[:, :, 0:W], h1[:, :, 0:W], AF.Square), reads=[h1], writes=[sq])
            for kc in range(KC):
                S.op("pe", lambda e: e.matmul(PS[6][:, 0:W], cB(CI_ONES), sq[:, kc, 0:W], start=(kc == 0), stop=(kc == KC - 1)),
                     reads=[sq], writes=[PS[6]])
            S.op("act", lambda e: e.activation(rstd[:, 0:W], PS[6][:, 0:W], AF.Sqrt, bias=EPS, scale=1.0 / D), reads=[PS[6]], writes=[rstd])
            S.op("dve", lambda e: e.reciprocal(rstd[:, 0:W], rstd[:, 0:W]), reads=[rstd], writes=[rstd])
            for kc in range(KC):
                tmi = tm[kc % 2]
                S.op("dve", lambda e: e.tensor_tensor(tmi[:, 0:W], h1[:, kc, 0:W], rstd[:, 0:W], ALU.mult), reads=[h1, rstd], writes=[tmi])
                S.op("act", lambda e: e.activation(xf[:, kc, 0:W], tmi[:, 0:W], AF.Identity, bias=modv[:, 24 + kc, s:s + 1], scale=A2[:, kc, s:s + 1]),
                     reads=[tmi, modv, A2], writes=[xf])
            for jj in range(22):
                w = wgu[jj % 2]
                S.dma("sp", w[:, 0, :, :], G["wgu_bf"].ap()[jj], writes=[w])
                S.dma("sp", w[:, 1, :, :], G["wgu_bf"].ap()[22 + jj], writes=[w])
                pg, pu = PS[jj % 2], PS[2 + jj % 2]
                sgi = sg[jj % 2]
                for kc in range(KC):
                    S.op("pe", lambda e: e.matmul(pg[:, 0:W], w[:, 0, kc, :], xf[:, kc, 0:W], start=(kc == 0), stop=(kc == KC - 1)),
                         reads=[w, xf], writes=[pg])
                for kc in range(KC):
                    S.op("pe", lambda e: e.matmul(pu[:, 0:W], w[:, 1, kc, :], xf[:, kc, 0:W], start=(kc == 0), stop=(kc == KC - 1)),
                         reads=[w, xf], writes=[pu])
                S.op("act", lambda e: e.activation(sgi[:, 0:W], pg[:, 0:W], AF.Silu), reads=[pg], writes=[sgi])
                S.op("dve", lambda e: e.tensor_tensor(hid[:, jj, 0:W], sgi[:, 0:W], pu[:, 0:W], ALU.mult), reads=[sgi, pu], writes=[hid])
            for j in range(8):
                w = wdn[j % 2]
                S.dma("sp", w[:], G["wdn_bf"].ap()[j], writes=[w])
                po = PS[4 + j % 2]
                for kc in range(22):
                    S.op("pe", lambda e: e.matmul(po[:, 0:W], w[:, kc, :], hid[:, kc, 0:W], start=(kc == 0), stop=(kc == 21)),
                         reads=[w, hid], writes=[po])
                o = h2[j % 2]
                S.op("dve", lambda e: e.scalar_tensor_tensor(o[:, 0:W], po[:, 0:W], modv[:, 40 + j, s:s + 1], h1[:, j, 0:W], ALU.mult, ALU.add),
                     reads=[po, modv, h1], writes=[o])
                if last:
                    S.dma("pool", h_dst.ap()[j * 128:(j + 1) * 128, t0 - NCTX:t0 - NCTX + W], o[:, 0:W], reads=[o])
                else:
                    S.dma("pool", h_dst.ap()[j * 128:(j + 1) * 128, t0:t0 + W], o[:, 0:W], reads=[o])


def _prep_shared(inp):
    out = {}
    f32 = np.float32
    for half in (0, 1):
        d = {}
        dsel = [0, 1] if half == 0 else [1, 0]
        for k in ("w_mod", "w_gate", "w_br_ssd", "w_br_diff", "w_br_gqa", "w_br_s5", "w_out", "ffn_w_gate_up",
                  "ffn_w_down", "s5_glu_w"):
            d[k] = np.ascontiguousarray(inp[k], dtype=f32)
        w_in = np.array(inp["w_in"], dtype=f32)
        if half == 1:
            tmp = w_in[:, :, C_DT:C_DT + 8].copy()
            w_in[:, :, C_DT:C_DT + 8] = w_in[:, :, C_DT + 8:C_DT + 16]
            w_in[:, :, C_DT + 8:C_DT + 16] = tmp
        d["w_in"] = w_in
        v = np.zeros((2, 128, NV), f32)
        r = np.zeros((2, NR), f32)
        for l in range(2):
            v[l, :, 0:8] = inp["norm1_g"][l].reshape(8, 128).T
            v[l, :, 8:16] = inp["norm2_g"][l].reshape(8, 128).T
            v[l, :, 16:64] = inp["b_mod"][l].reshape(48, 128).T
            v[l, :, 64] = np.tile(inp["diff_qn_g"][l], 2)
            v[l, :, 65] = np.tile(inp["diff_kn_g"][l], 2)
            v[l, :, 66] = np.tile(inp["gqa_qn_g"][l], 2)
            v[l, :, 67] = np.tile(inp["gqa_kn_g"][l], 2)
            v[l, :, 68] = np.tile(inp["diff_subln_g"][l][:64], 2)
            v[l, :, 69] = np.tile(inp["diff_subln_g"][l][64:], 2)
            v[l, :, 70:76] = inp["ssd_conv_b"][l].reshape(6, 128).T
            cw = inp["ssd_conv_w"][l]
            if half == 1:
                cw = cw[::-1]
            v[l, :, 76:106] = cw.reshape(5, 6, 128).transpose(2, 1, 0).reshape(128, 30)
            v[l, :, 106:109] = inp["s5_d"][l].reshape(3, 128).T
            v[l, :, 109:115] = inp["s5_glu_b"][l].reshape(6, 128).T
            r[l, 0:16] = inp["ssd_a_log"][l][dsel].reshape(16)
            r[l, 16:32] = inp["ssd_dt_bias"][l][dsel].reshape(16)
            r[l, 32:40] = inp["ssd_d"][l]
            r[l, 40:552] = inp["ssd_norm_g"][l]
            r[l, 552:616] = inp["diff_lam_q1"][l]
            r[l, 616:680] = inp["diff_lam_k1"][l]
            r[l, 680:744] = inp["diff_lam_q2"][l]
            r[l, 744:808] = inp["diff_lam_k2"][l]
        d["vec128"] = v
        d["rowvecs"] = r

        def s5lay(a):
            a = np.asarray(a, f32)[:, dsel]
            return np.ascontiguousarray(a.reshape(2, 2, 12, 2, 64).transpose(0, 1, 3, 4, 2).reshape(2, 2, 128, 12))
        d["s5_lr"] = s5lay(inp["s5_lam_re"])
        d["s5_li"] = s5lay(inp["s5_lam_im"])
        ldt = np.asarray(inp["s5_log_dt"], f32)
        d["s5_ldt"] = s5lay(np.broadcast_to(ldt[..., None], (2, 2, 24, 64)))
        Bb = np.zeros((2, 2, 12, 128, 128), f32)
        Cb = np.zeros((2, 2, 12, 128, 128), f32)
        for ri, (bk, ck) in enumerate((("s5_b_re", "s5_c_re"), ("s5_b_im", "s5_c_im"))):
            b = np.asarray(inp[bk], f32)
            c = np.asarray(inp[ck], f32)
            for gp in range(12):
                for two in range(2):
                    g = 2 * gp + two
                    r0 = 32 * (gp % 4) + 16 * two
                    Bb[:, ri, gp, r0:r0 + 16, 64 * two:64 * two + 64] = b[:, g].transpose(0, 2, 1)
                    Cb[:, ri, gp, 64 * two:64 * two + 64, r0:r0 + 16] = c[:, g].transpose(0, 2, 1)
        d["s5_Bblk"] = Bb
        d["s5_Cblk"] = Cb
        ct, sn = rope_tables(half == 1)
        d["rope_cos"] = ct
        d["rope_sin"] = sn
        d["consts"] = make_consts()
        out[half] = d
    return out


def make_in_map(inp, shared, core):
    b, half = core // 2, core % 2
    x = np.asarray(inp["x"][b], np.float32)
    cx = np.asarray(inp["ctx"][b], np.float32)
    if half == 1:
        x = x[::-1]
        cx = cx[::-1]
    m = dict(shared[half])
    m["hT0"] = np.ascontiguousarray(np.concatenate([cx, x], 0).T)
    c2 = np.stack([np.asarray(inp["c"][b], np.float32), np.asarray(inp["c_ctx"], np.float32)], -1)
    m["c2"] = np.ascontiguousarray(c2.reshape(8, 128, 2).transpose(1, 0, 2))
    return m


_NC_CACHE = {}


def kernel(**inputs):
    if "nc" not in _NC_CACHE:
        _NC_CACHE["nc"] = build()
    nc = _NC_CACHE["nc"]
    shared = _prep_shared(inputs)
    in_maps = [make_in_map(inputs, shared, core) for core in range(8)]
    res = run_bass_kernel_spmd(nc, in_maps, core_ids=list(range(8)))
    out = np.zeros((4, NLAT, D), np.float32)
    n = N_OUT_TILES_LAST * 512
    for core in range(8):
        b, half = core // 2, core % 2
        y = np.asarray(res.results[core]["y"]).T
        if half == 0:
            out[b, :n] = y
        else:
            out[b, NLAT - n:] = y[::-1]
    return out
```

```python
import contextlib
import math
import numpy as np
import concourse.bass as bass
import concourse.mybir as mybir
from concourse.bass_utils import run_bass_kernel_spmd

F32 = mybir.dt.float32
BF16 = mybir.dt.bfloat16
AF = mybir.ActivationFunctionType
ALU = mybir.AluOpType
AX = mybir.AxisListType


class Tile:
    def __init__(self, S, h, name, psum=False):
        self.S = S
        self.h = h
        self.name = name
        self.psum = psum
        self.lw = None
        self.rd = {}
        self.dsem = None

    def __getitem__(self, k):
        return self.h[k]


class Sched:
    SAME_ENGINE_SYNC = ("dve", "act", "pool")

    def __init__(self, nc):
        self.nc = nc
        self.E = {"pe": nc.tensor, "dve": nc.vector, "act": nc.scalar, "pool": nc.gpsimd, "sp": nc.sync}
        self.sems = {}
        for k in self.E:
            self.sems[k] = [nc.alloc_semaphore("sem_" + k), 0]
        self.waited = {k: {} for k in self.E}
        self.free_dsems = []
        self.n_dsem = 0
        self.stack = []
        self.ninstr = 0
        self.inflight = {k: [] for k in self.E}
        self.MAXOUT = 6

    def sb(self, stack, name, shape, dtype):
        self.nalloc = getattr(self, "nalloc", 0) + 1
        name = "%s_%d" % (name, self.nalloc)
        h = stack.enter_context(self.nc.sbuf_tensor(name, list(shape), dtype))
        t = Tile(self, h, name)
        if not hasattr(self, "tiles"):
            self.tiles = []
        self.tiles.append(t)
        return t

    def mark(self):
        return len(getattr(self, "tiles", []))

    def release_since(self, mk):
        self.release(self.tiles[mk:])
        del self.tiles[mk:]

    def ps(self, stack, name, shape, dtype=F32):
        h = stack.enter_context(self.nc.psum_tensor(name, list(shape), dtype))
        return Tile(self, h, name, psum=True)

    def _dsem(self, t):
        if t.dsem is None:
            if self.free_dsems:
                t.dsem = self.free_dsems.pop()
            else:
                key = "d%d" % self.n_dsem
                self.n_dsem += 1
                self.sems[key] = [self.nc.alloc_semaphore("sem_" + key), 0]
                t.dsem = key
        return t.dsem

    def release(self, tiles):
        for t in tiles:
            if t.dsem is not None:
                self.free_dsems.append(t.dsem)
                t.dsem = None

    def _wait(self, eng, key, val):
        if val <= 0:
            return
        w = self.waited[eng]
        if w.get(key, 0) >= val:
            return
        self.E[eng].wait_ge(self.sems[key][0], val)
        w[key] = val
        self.ninstr += 1

    def _deps(self, eng, reads, writes):
        deps = {}
        def add(d):
            if d is None:
                return
            k, v = d
            if k == eng and eng not in self.SAME_ENGINE_SYNC:
                return
            if deps.get(k, 0) < v:
                deps[k] = v
        for t in reads:
            add(t.lw)
        for t in writes:
            add(t.lw)
            for k, v in t.rd.items():
                add((k, v))
        for k, v in deps.items():
            self._wait(eng, k, v)

    def op(self, eng, fn, reads=(), writes=()):
        self._deps(eng, reads, writes)
        ins = fn(self.E[eng])
        s = self.sems[eng]
        s[1] += 1
        ins.then_inc(s[0], 1)
        self.ninstr += 1
        me = (eng, s[1])
        for t in reads:
            if t.rd.get(eng, 0) < s[1]:
                t.rd[eng] = s[1]
        for t in writes:
            t.lw = me
            t.rd = {}
        return ins

    def dma(self, q, out, in_, reads=(), writes=(), **kw):
        self._deps(q, reads, writes)
        tl = (list(writes) + list(reads))
        assert tl, "dma needs an sbuf tile for its semaphore"
        key = self._dsem(tl[0])
        self._throttle(q)
        ins = self.E[q].dma_start(out=out, in_=in_, **kw)
        s = self.sems[key]
        s[1] += 16
        ins.then_inc(s[0], 16)
        self.ninstr += 1
        me = (key, s[1])
        self.inflight[q].append(me)
        for t in reads:
            if t.rd.get(key, 0) < s[1]:
                t.rd[key] = s[1]
        for t in writes:
            t.lw = me
            t.rd = {}
        return ins

    def _throttle(self, q):
        fl = self.inflight[q]
        while len(fl) >= self.MAXOUT:
            k, v = fl.pop(0)
            self._wait(q, k, v)

    def dma_dram(self, q, out, in_, **kw):
        key = "dd_" + q
        if key not in self.sems:
            self.sems[key] = [self.nc.alloc_semaphore("sem_" + key), 0]
        self._throttle(q)
        ins = self.E[q].dma_start(out=out, in_=in_, **kw)
        s = self.sems[key]
        s[1] += 16
        ins.then_inc(s[0], 16)
        self.ninstr += 1
        self.inflight[q].append((key, s[1]))
        return ins

    def barrier(self):
        for eng in self.E:
            for key, (h, cnt) in self.sems.items():
                if key == eng and eng not in self.SAME_ENGINE_SYNC and eng != "sp":
                    continue
                self._wait(eng, key, cnt)

D = 1024
NCTX = 256
NLAT = 8192
T = NCTX + NLAT
KC = 8
TILES = [(0, 256)] + [(256 + 512 * i, 512) for i in range(16)]
N_OUT_TILES_LAST = 8
EPS = 1e-6
INC = 3984
C_Z, C_XBC, C_DT, C_DQ, C_DK, C_DV, C_GQ, C_GK, C_GV, C_U = 0, 512, 1280, 1296, 1808, 2320, 2832, 3344, 3472, 3600
FH = 2816
NV = 115
NR = 808
CI_ID, CI_BLK64, CI_ROT, CI_TRI0, CI_TRI1, CI_MN0, CI_MN1, CI_SEL, CI_ONES = range(9)
NCONST = 9


def make_consts():
    c = np.zeros((NCONST, 128, 128), np.float32)
    c[CI_ID] = np.eye(128)
    c[CI_BLK64, :64, :64] = 1.0
    c[CI_BLK64, 64:, 64:] = 1.0
    for hb in (0, 64):
        for q0 in (0, 32):
            for i in range(16):
                c[CI_ROT, hb + q0 + 16 + i, hb + q0 + i] = -1.0
                c[CI_ROT, hb + q0 + i, hb + q0 + 16 + i] = 1.0
    k = np.arange(128)
    c[CI_TRI0] = (k[:, None] <= k[None, :])
    c[CI_TRI1] = (k[:, None] >= k[None, :])
    c[CI_MN0] = np.where(k[:, None] <= k[None, :], 0.0, -1e30)
    c[CI_MN1] = np.where(k[:, None] >= k[None, :], 0.0, -1e30)
    c[CI_SEL, 64, :] = 1.0
    c[CI_ONES] = 1.0
    return c


def rope_tables(flip):
    n_rows = NLAT // 64
    rows = np.repeat(np.arange(n_rows, dtype=np.float32), 64)
    cols = np.tile(np.arange(64, dtype=np.float32), n_rows)
    inv = (10000.0 ** (-np.arange(16, dtype=np.float32) / 16)).astype(np.float32)
    ar = rows[:, None] * inv
    ac = cols[:, None] * inv
    ang = np.concatenate([ar, ar, ac, ac], -1)
    cos = np.cos(ang).astype(np.float32)
    sin = np.sin(ang).astype(np.float32)
    if flip:
        cos = cos[::-1]
        sin = sin[::-1]
    ct = np.ones((128, T), np.float32)
    st = np.zeros((128, T), np.float32)
    ct[:64, NCTX:] = cos.T
    ct[64:, NCTX:] = cos.T
    st[:64, NCTX:] = sin.T
    st[64:, NCTX:] = sin.T
    return ct, st


def dview(t, pattern, **kw):
    return t.ap().rearrange(pattern, **kw)


def build(n_layers=2, phases=None, dbg=(), last_tiles=N_OUT_TILES_LAST, force_full=False, dbg_in=()):
    nc = bass.Bass("TRN2", target_bir_lowering=False)
    S = Sched(nc)
    dbg = set(dbg)
    allph = phases is None

    def want(p):
        return allph or p in phases

    def dram(name, shape, dt, kind=None):
        if kind is None:
            kind = "ExternalOutput" if name in dbg else ("ExternalInput" if name in dbg_in else "Internal")
        return nc.dram_tensor(name, list(shape), dt, kind=kind)

    def din(name, shape, dt=F32):
        return nc.dram_tensor(name, list(shape), dt, kind="ExternalInput")

    hT_in = din("hT0", [D, T])
    c2 = din("c2", [128, KC, 2])
    w_mod = din("w_mod", [2, D, 6 * D])
    w_in = din("w_in", [2, D, INC])
    w_gate = din("w_gate", [2, 4, D, D])
    w_br = [din("w_br_ssd", [2, 512, D]), din("w_br_diff", [2, 512, D]), din("w_br_gqa", [2, 512, D]),
            din("w_br_s5", [2, 384, D])]
    BRK = [4, 4, 4, 3]
    w_out = din("w_out", [2, D, D])
    w_gu = din("ffn_w_gate_up", [2, D, 2 * FH])
    w_dn = din("ffn_w_down", [2, FH, D])
    glu_w = din("s5_glu_w", [2, 384, 768])
    vec128 = din("vec128", [2, 128, NV])
    rowvecs = din("rowvecs", [2, NR])
    rope_c = din("rope_cos", [128, T])
    rope_s = din("rope_sin", [128, T])
    consts = din("consts", [NCONST, 128, 128])
    s5_lr = din("s5_lr", [2, 2, 128, 12])
    s5_li = din("s5_li", [2, 2, 128, 12])
    s5_ldt = din("s5_ldt", [2, 2, 128, 12])
    s5_B = din("s5_Bblk", [2, 2, 12, 128, 128])
    s5_C = din("s5_Cblk", [2, 2, 12, 128, 128])
    y_out = nc.dram_tensor("y", [D, last_tiles * 512], F32, kind="ExternalOutput")

    hT = [hT_in, dram("hT1", [D, T], F32), dram("hT2", [D, T], F32)]
    xnT = dram("xnT", [128, KC, T], BF16)
    xbcT = dram("xbcT", [128, 6, T], F32)
    qkT = dram("qkT", [128, 13, T], BF16)
    uT = dram("uT", [128, 3, T], F32)
    zt = dram("zt", [T, 512], F32)
    vaug = dram("vaug", [T, 10, 65], BF16)
    dtt = dram("dtt", [T, 16], F32)
    yaT = dram("yaT", [128, 4, T], BF16)
    ybT = dram("ybT", [128, 4, T], BF16)
    ycT = dram("ycT", [128, 4, T], BF16)
    ydT = dram("ydT", [128, 3, T], BF16)
    xtok = dram("xtok", [T, 512], BF16)
    btok = dram("btok", [T, 128], BF16)
    bcT = dram("bcT", [2, 2, 64, T], BF16)
    yssd = dram("yssd", [T, 512], F32)
    yssd1 = dram("yssd1", [T, 512], F32)
    ys5 = dram("ys5", [128, 3, T], F32)
    modv_d = dram("modv_d", [2, 128, 48, 2], F32)
    wg_bf = dram("wg_bf", [4 * 8, 128, 8, 128], BF16)
    wb_bf = [dram("wb_bf%d" % i, [8, 128, BRK[i], 128], BF16) for i in range(4)]
    wo_bf = dram("wo_bf", [8, 128, 8, 128], BF16)
    wgu_bf = dram("wgu_bf", [44, 128, 8, 128], BF16)
    wdn_bf = dram("wdn_bf", [8, 128, 22, 128], BF16)

    with contextlib.ExitStack() as gst:
        cst_f = S.sb(gst, "cst_f", [128, NCONST, 128], F32)
        cst_b = S.sb(gst, "cst_b", [128, NCONST, 128], BF16)
        S.dma("sp", cst_f[:], consts.ap().rearrange("c p m -> p c m"), writes=[cst_f])
        S.op("dve", lambda e: e.tensor_copy(cst_b[:], cst_f[:]), reads=[cst_f], writes=[cst_b])
        psall = gst.enter_context(nc.psum_tensor("psall", [128, 4096], F32))
        PS = [Tile(S, psall[:, i * 512:(i + 1) * 512], "ps%d" % i, psum=True) for i in range(8)]

        def cF(i):
            return cst_f[:, i, :]

        def cB(i):
            return cst_b[:, i, :]

        for layer in range(n_layers):
            last = (layer == n_layers - 1) and not force_full
            out_tiles = TILES[1:1 + last_tiles] if last else TILES
            h_src = hT[layer]
            h_dst = y_out if last else hT[layer + 1]
            lam_init = 0.8 - 0.6 * math.exp(-0.3 * layer)

            if want("W"):
                qs = ["pool"]
                n = 0
                for i in range(4):
                    for j in range(8):
                        S.dma_dram("pool", wg_bf.ap()[i * 8 + j],
                                   w_gate.ap()[layer, i, :, j * 128:(j + 1) * 128].rearrange("(kc p) m -> p kc m", p=128))
                    for j in range(8):
                        S.dma_dram("pool", wb_bf[i].ap()[j],
                                   w_br[i].ap()[layer, :, j * 128:(j + 1) * 128].rearrange("(kc p) m -> p kc m", p=128))
                for j in range(8):
                    S.dma_dram("pool", wo_bf.ap()[j],
                               w_out.ap()[layer, :, j * 128:(j + 1) * 128].rearrange("(kc p) m -> p kc m", p=128))
                    S.dma_dram("pool", wdn_bf.ap()[j],
                               w_dn.ap()[layer, :, j * 128:(j + 1) * 128].rearrange("(kc p) m -> p kc m", p=128))
                for j in range(44):
                    S.dma_dram("pool", wgu_bf.ap()[j],
                               w_gu.ap()[layer, :, j * 128:(j + 1) * 128].rearrange("(kc p) m -> p kc m", p=128))

            lmk = S.mark()
            with contextlib.ExitStack() as lst:
                v128 = S.sb(lst, "v128", [128, NV], F32)
                rows = S.sb(lst, "rows", [128, NR], F32)
                modv = S.sb(lst, "modv", [128, 48, 2], F32)
                A1 = S.sb(lst, "A1", [128, 8, 2], F32)
                A2 = S.sb(lst, "A2", [128, 8, 2], F32)
                lamc = S.sb(lst, "lamc", [128, 4], F32)
                subg = S.sb(lst, "subg", [128, 2], F32)
                aneg = S.sb(lst, "aneg", [128, 16], F32)
                S.dma("sp", v128[:], vec128.ap()[layer], writes=[v128])
                S.dma("sp", rows[:], rowvecs.ap()[layer].partition_broadcast(128), writes=[rows])
                if want("P"):
                    with contextlib.ExitStack() as st:
                        sc = S.sb(st, "sc", [128, KC, 2], F32)
                        S.dma("sp", sc[:], c2.ap(), writes=[sc])
                        S.op("act", lambda e: e.activation(sc[:], sc[:], AF.Silu), reads=[sc], writes=[sc])
                        wm = [S.sb(st, "wm%d" % i, [128, KC, 512], F32) for i in range(2)]
                        pm = PS[0]
                        for blk in range(12):
                            w = wm[blk % 2]
                            S.dma("sp", w[:], w_mod.ap()[layer, :, blk * 512:(blk + 1) * 512].rearrange("(kc p) n -> p kc n", p=128),
                                  writes=[w])
                            for jj in range(4):
                                j = blk * 4 + jj
                                for kc in range(KC):
                                    S.op("pe", lambda e, j=j, jj=jj, kc=kc, w=w: e.matmul(
                                        pm[:, j * 2:(j + 1) * 2], w[:, kc, jj * 128:(jj + 1) * 128], sc[:, kc, :],
                                        start=(kc == 0), stop=(kc == KC - 1)), reads=[w, sc], writes=[pm])
                        S.op("dve", lambda e: e.tensor_tensor(
                            modv[:], pm[:, 0:96].rearrange("p (j s) -> p j s", s=2),
                            v128[:, 16:64].unsqueeze(2).broadcast_to([128, 48, 2]), ALU.add),
                            reads=[pm, v128], writes=[modv])
                        S.dma("pool", modv_d.ap()[layer], modv[:], reads=[modv])
                else:
                    S.dma("sp", modv[:], modv_d.ap()[layer], writes=[modv])
                for (A, gcol, scj) in ((A1, 0, 8), (A2, 8, 32)):
                    S.op("dve", lambda e, A=A, scj=scj: e.tensor_scalar(A[:], modv[:, scj:scj + 8, :], 1.0, None, ALU.add),
                         reads=[modv], writes=[A])
                    S.op("dve", lambda e, A=A, gcol=gcol: e.tensor_tensor(
                        A[:], A[:], v128[:, gcol:gcol + 8].unsqueeze(2).broadcast_to([128, 8, 2]), ALU.mult),
                        reads=[A, v128], writes=[A])
                with contextlib.ExitStack() as st:
                    tmp = S.sb(st, "lamtmp", [128, 128], F32)
                    red = S.sb(st, "lamred", [128, 2], F32)
                    R0 = 552
                    S.op("dve", lambda e: e.tensor_tensor(tmp[:, 0:64], rows[:, R0:R0 + 64], rows[:, R0 + 64:R0 + 128], ALU.mult),
                         reads=[rows], writes=[tmp])
                    S.op("dve", lambda e: e.tensor_tensor(tmp[:, 64:128], rows[:, R0 + 128:R0 + 192], rows[:, R0 + 192:R0 + 256], ALU.mult),
                         reads=[rows, tmp], writes=[tmp])
                    S.op("dve", lambda e: e.reduce_sum(red[:], tmp[:].rearrange("p (a b) -> p a b", a=2), AX.X),
                         reads=[tmp], writes=[red])
                    S.op("act", lambda e: e.activation(red[:], red[:], AF.Exp), reads=[red], writes=[red])
                    S.op("dve", lambda e: e.tensor_tensor(lamc[:, 0:1], red[:, 1:2], red[:, 0:1], ALU.subtract),
                         reads=[red], writes=[lamc])
                    S.op("dve", lambda e: e.tensor_scalar(lamc[:, 0:1], lamc[:, 0:1], -lam_init, None, ALU.add),
                         reads=[lamc], writes=[lamc])
                    S.op("dve", lambda e: e.tensor_scalar(subg[:], v128[:, 68:70], 1.0 - lam_init, None, ALU.mult),
                         reads=[v128], writes=[subg])
                    S.op("act", lambda e: e.activation(aneg[:], rows[:, 0:16], AF.Exp), reads=[rows], writes=[aneg])
                    S.op("dve", lambda e: e.tensor_scalar(aneg[:], aneg[:], -1.0, None, ALU.mult), reads=[aneg], writes=[aneg])
                    S.barrier()

                for pname, pfn in (("1", phase1), ("2", phase2), ("3", phase3), ("4", phase4), ("5", phase5)):
                    if want(pname):
                        mk = S.mark()
                        pfn(nc, S, locals())
                        S.barrier()
                        S.release_since(mk)
                S.barrier()
            S.release_since(lmk)
        S.barrier()
    print("ninstr", S.ninstr, "dsems", S.n_dsem)
    return nc


def phase1(nc, S, G):
    PS, layer, v128, modv, A1 = G["PS"], G["layer"], G["v128"], G["modv"], G["A1"]
    cF, cB = G["cF"], G["cB"]
    h_src = G["h_src"]
    with contextlib.ExitStack() as st:
        win = S.sb(st, "win", [128, KC, INC], BF16)
        for c0 in range(0, INC, 512):
            c1 = min(INC, c0 + 512)
            S.dma("pool", win[:, :, c0:c1],
                  G["w_in"].ap()[layer, :, c0:c1].rearrange("(kc p) n -> p kc n", p=128), writes=[win])
        hts = [S.sb(st, "ht%d" % i, [128, KC, 512], F32) for i in range(2)]
        sq = S.sb(st, "sq", [128, KC, 512], BF16)
        rstd = S.sb(st, "rstd", [128, 512], F32)
        xns = [S.sb(st, "xn%d" % i, [128, KC, 512], BF16) for i in range(2)]
        cos_t = [S.sb(st, "cos%d" % i, [128, 512], F32) for i in range(2)]
        sin_t = [S.sb(st, "sin%d" % i, [128, 512], F32) for i in range(2)]
        stg = [S.sb(st, "stg%d" % i, [128, 512], F32) for i in range(4)]
        qsq = [S.sb(st, "qsq%d" % i, [128, 512], BF16) for i in range(2)]
        qy = [S.sb(st, "qy%d" % i, [128, 512], BF16) for i in range(2)]
        qr = [S.sb(st, "qr%d" % i, [128, 512], F32) for i in range(2)]
        qt1 = [S.sb(st, "qt1%d" % i, [128, 512], F32) for i in range(2)]
        qt2 = [S.sb(st, "qt2%d" % i, [128, 512], F32) for i in range(2)]
        qo = [S.sb(st, "qo%d" % i, [128, 512], BF16) for i in range(2)]
        vst = [S.sb(st, "vst%d" % i, [128, 10, 65], BF16) for i in range(2)]
        dst = [S.sb(st, "dst%d" % i, [128, 16], F32) for i in range(2)]
        for v in vst:
            S.op("pool", lambda e, v=v: e.memset(v[:], 1.0), writes=[v])
        nstg = 0
        nq = 0
        own_set = set(G["out_tiles"])
        for ti, (t0, W) in enumerate(TILES):
            s = 1 if t0 < NCTX else 0
            own = (t0, W) in own_set
            ht = hts[ti % 2]
            xn = xns[ti % 2]
            ct, sn = cos_t[ti % 2], sin_t[ti % 2]
            S.dma("sp", ht[:, :, 0:W], h_src.ap().rearrange("(kc p) t -> p kc t", p=128)[:, :, t0:t0 + W], writes=[ht])
            S.dma("sp", ct[:, 0:W], G["rope_c"].ap()[:, t0:t0 + W], writes=[ct])
            S.dma("sp", sn[:, 0:W], G["rope_s"].ap()[:, t0:t0 + W], writes=[sn])
            S.op("act", lambda e: e.activation(sq[:, :, 0:W], ht[:, :, 0:W], AF.Square), reads=[ht], writes=[sq])
            for kc in range(KC):
                S.op("pe", lambda e, kc=kc: e.matmul(PS[0][:, 0:W], cB(CI_ONES), sq[:, kc, 0:W], start=(kc == 0), stop=(kc == KC - 1)),
                     reads=[sq], writes=[PS[0]])
            S.op("act", lambda e: e.activation(rstd[:, 0:W], PS[0][:, 0:W], AF.Sqrt, bias=EPS, scale=1.0 / D), reads=[PS[0]], writes=[rstd])
            S.op("dve", lambda e: e.reciprocal(rstd[:, 0:W], rstd[:, 0:W]), reads=[rstd], writes=[rstd])
            S.op("dve", lambda e: e.tensor_tensor(ht[:, :, 0:W], ht[:, :, 0:W], rstd[:, 0:W].unsqueeze(1).broadcast_to([128, KC, W]), ALU.mult),
                 reads=[ht, rstd], writes=[ht])
            for kc in range(KC):
                S.op("act", lambda e, kc=kc: e.activation(xn[:, kc, 0:W], ht[:, kc, 0:W], AF.Identity,
                                                          bias=modv[:, kc, s:s + 1], scale=A1[:, kc, s:s + 1]),
                     reads=[ht, modv, A1], writes=[xn])
            if own:
                S.dma("pool", G["xnT"].ap()[:, :, t0:t0 + W], xn[:, :, 0:W], reads=[xn])

            def fm_group(col0, pst):
                for kc in range(KC):
                    S.op("pe", lambda e, kc=kc: e.matmul(pst[:, 0:W], win[:, kc, col0:col0 + 128], xn[:, kc, 0:W],
                                                         start=(kc == 0), stop=(kc == KC - 1)), reads=[win, xn], writes=[pst])
            ng = 0
            for c in range(6):
                pst = PS[1 + ng % 2]; ng += 1
                fm_group(C_XBC + 128 * c, pst)
                sg = stg[nstg % 4]; nstg += 1
                S.op("act", lambda e, sg=sg, pst=pst: e.activation(sg[:, 0:W], pst[:, 0:W], AF.Copy), reads=[pst], writes=[sg])
                S.dma("pool", G["xbcT"].ap()[:, c, t0:t0 + W], sg[:, 0:W], reads=[sg])
            for c in range(3):
                pst = PS[1 + ng % 2]; ng += 1
                fm_group(C_U + 128 * c, pst)
                sg = stg[nstg % 4]; nstg += 1
                S.op("dve", lambda e, sg=sg, pst=pst: e.tensor_copy(sg[:, 0:W], pst[:, 0:W]), reads=[pst], writes=[sg])
                S.dma("pool", G["uT"].ap()[:, c, t0:t0 + W], sg[:, 0:W], reads=[sg])
            for c in range(13):
                if not own and (c < 4 or 8 <= c < 12):
                    continue
                if c < 4:
                    col0, gcol = C_DQ + 128 * c, 64
                elif c < 8:
                    col0, gcol = C_DK + 128 * (c - 4), 65
                elif c < 12:
                    col0, gcol = C_GQ + 128 * (c - 8), 66
                else:
                    col0, gcol = C_GK, 67
                pst = PS[1 + ng % 2]; ng += 1
                fm_group(col0, pst)
                b = nq % 2; nq += 1
                a_sq, a_y, a_r, a_t1, a_t2, a_o = qsq[b], qy[b], qr[b], qt1[b], qt2[b], qo[b]
                S.op("act", lambda e: e.activation(a_sq[:, 0:W], pst[:, 0:W], AF.Square), reads=[pst], writes=[a_sq])
                S.op("act", lambda e: e.activation(a_y[:, 0:W], pst[:, 0:W], AF.Identity, scale=v128[:, gcol:gcol + 1]),
                     reads=[pst, v128], writes=[a_y])
                S.op("pe", lambda e: e.matmul(PS[3][:, 0:W], cB(CI_BLK64), a_sq[:, 0:W], start=True, stop=True), reads=[a_sq], writes=[PS[3]])
                S.op("pe", lambda e: e.matmul(PS[4][:, 0:W], cB(CI_ROT), a_y[:, 0:W], start=True, stop=True), reads=[a_y], writes=[PS[4]])
                S.op("act", lambda e: e.activation(a_r[:, 0:W], PS[3][:, 0:W], AF.Sqrt, bias=EPS, scale=1.0 / 64), reads=[PS[3]], writes=[a_r])
                S.op("dve", lambda e: e.reciprocal(a_r[:, 0:W], a_r[:, 0:W]), reads=[a_r], writes=[a_r])
                S.op("dve", lambda e: e.tensor_tensor(a_t1[:, 0:W], a_y[:, 0:W], ct[:, 0:W], ALU.mult), reads=[a_y, ct], writes=[a_t1])
                S.op("dve", lambda e: e.tensor_tensor(a_t2[:, 0:W], PS[4][:, 0:W], sn[:, 0:W], ALU.mult), reads=[PS[4], sn], writes=[a_t2])
                S.op("dve", lambda e: e.tensor_tensor(a_t1[:, 0:W], a_t1[:, 0:W], a_t2[:, 0:W], ALU.add), reads=[a_t1, a_t2], writes=[a_t1])
                S.op("dve", lambda e: e.tensor_tensor(a_o[:, 0:W], a_t1[:, 0:W], a_r[:, 0:W], ALU.mult), reads=[a_t1, a_r], writes=[a_o])
                S.dma("pool", G["qkT"].ap()[:, c, t0:t0 + W], a_o[:, 0:W], reads=[a_o])

            for blk in range(W // 128):
                bs = slice(blk * 128, (blk + 1) * 128)
                r0 = t0 + blk * 128
                vs_, ds_ = vst[blk % 2], dst[blk % 2]
                for kc in range(KC):
                    fl = dict(start=(kc == 0), stop=(kc == KC - 1))
                    if own:
                        S.op("pe", lambda e: e.matmul(PS[5][:, 0:512], xn[:, kc, bs], win[:, kc, C_Z:C_Z + 512], **fl),
                             reads=[win, xn], writes=[PS[5]])
                    S.op("pe", lambda e: e.matmul(PS[6][:, 0:512], xn[:, kc, bs], win[:, kc, C_DV:C_DV + 512], **fl),
                         reads=[win, xn], writes=[PS[6]])
                    S.op("pe", lambda e: e.matmul(PS[7][:, 0:128], xn[:, kc, bs], win[:, kc, C_GV:C_GV + 128], **fl),
                         reads=[win, xn], writes=[PS[7]])
                    S.op("pe", lambda e: e.matmul(PS[0][:, 0:16], xn[:, kc, bs], win[:, kc, C_DT:C_DT + 16], **fl),
                         reads=[win, xn], writes=[PS[0]])
                if own:
                    sg = stg[nstg % 4]; nstg += 1
                    S.op("act", lambda e, sg=sg: e.activation(sg[:, :], PS[5][:, :], AF.Copy), reads=[PS[5]], writes=[sg])
                    S.dma("pool", G["zt"].ap()[r0:r0 + 128, :], sg[:, :], reads=[sg])
                S.op("dve", lambda e: e.tensor_copy(vs_[:, 0:8, 0:64], PS[6][:, :].rearrange("p (g c) -> p g c", c=64)),
                     reads=[PS[6]], writes=[vs_])
                S.op("dve", lambda e: e.tensor_copy(vs_[:, 8:10, 0:64], PS[7][:, 0:128].rearrange("p (g c) -> p g c", c=64)),
                     reads=[PS[7]], writes=[vs_])
                S.op("act", lambda e: e.activation(ds_[:, :], PS[0][:, 0:16], AF.Copy), reads=[PS[0]], writes=[ds_])
                S.dma("pool", G["vaug"].ap()[r0:r0 + 128], vs_[:], reads=[vs_])
                S.dma("pool", G["dtt"].ap()[r0:r0 + 128, :], ds_[:, :], reads=[ds_])


def phase2(nc, S, G):
    PS, layer, v128 = G["PS"], G["layer"], G["v128"]
    cF, cB, lamc, subg = G["cF"], G["cB"], G["lamc"], G["subg"]
    out_tiles = G["out_tiles"]
    qkT, vaug = G["qkT"], G["vaug"]
    NKB = T // 128
    with contextlib.ExitStack() as st:
        kTs = [S.sb(st, "kT%d" % i, [128, T], BF16) for i in range(2)]
        vts = [S.sb(st, "vt%d" % i, [128, NKB, 2, 65], BF16) for i in range(2)]
        qts = [S.sb(st, "qt%d" % i, [128, 2, 512], BF16) for i in range(2)]
        pTw = [S.sb(st, "pTw%d" % i, [128, 2, 512], BF16) for i in range(2)]
        psall = G["psall"]
        xlo = [S.sb(st, "xlo%d" % i, [65, 512], F32) for i in range(2)]
        rinv = [S.sb(st, "rinv%d" % i, [64, 512], F32) for i in range(2)]
        olo = [S.sb(st, "olo%d" % i, [64, 512], F32) for i in range(2)]
        ohi = [S.sb(st, "ohi%d" % i, [64, 512], F32) for i in range(2)]
        dlo = S.sb(st, "dlo", [64, 512], F32)
        dhi = S.sb(st, "dhi", [64, 512], F32)
        sql = S.sb(st, "sql", [64, 512], BF16)
        sqh = S.sb(st, "sqh", [64, 512], BF16)
        rs = S.sb(st, "rs", [64, 512], F32)
        yo = [S.sb(st, "yo%d" % i, [64, 512], BF16) for i in range(4)]
        nrot = 0
        nyo = 0
        nq = 0
        for grp in range(6):
            kT, vt = kTs[grp % 2], vts[grp % 2]
            is_diff = grp < 4
            if is_diff:
                h = grp
                S.dma("sp", kT[:, :], qkT.ap()[:, 4 + h, :], writes=[kT])
                for k0 in range(0, NKB, 11):
                    S.dma("sp", vt[:, k0:k0 + 11, :, :],
                          vaug.ap()[k0 * 128:(k0 + 11) * 128, 2 * h:2 * h + 2, :].rearrange("(kb p) g c -> p kb g c", p=128), writes=[vt])
                units = [(0, 0), (0, 1)]
            else:
                n = grp - 4
                S.dma("sp", kT[0:64, :], qkT.ap()[64 * n:64 * n + 64, 12, :], writes=[kT])
                S.dma("sp", kT[64:128, :], qkT.ap()[64 * n:64 * n + 64, 12, :], writes=[kT])
                for k0 in range(0, NKB, 11):
                    S.dma("sp", vt[:, k0:k0 + 11, 0:1, :],
                          vaug.ap()[k0 * 128:(k0 + 11) * 128, 8 + n:9 + n, :].rearrange("(kb p) g c -> p kb g c", p=128), writes=[vt])
                units = [(0, 0), (0, 1), (1, 0), (1, 1)]
            for (t0, W) in out_tiles:
                qt = qts[nq % 2]; nq += 1
                if is_diff:
                    S.dma("sp", qt[:, 0, 0:W], qkT.ap()[:, h, t0:t0 + W], writes=[qt])
                else:
                    S.dma("sp", qt[:, :, 0:W], qkT.ap()[:, 8 + 2 * n:10 + 2 * n, t0:t0 + W], writes=[qt])
                kbs = [0, 1] if t0 < NCTX else list(range(NKB))
                subs = []
                for kb in kbs:
                    for u0 in range(0, len(units), 2):
                        subs.append((kb, [u0, u0 + 1]))

                def emit_s(i):
                    kb, us = subs[i]
                    r = i % 2
                    pw = pTw[r]
                    for k, ui in enumerate(us):
                        qc, half = units[ui]
                        hs = slice(64 * half, 64 * half + 64)
                        pS = PS[2 * r + k]
                        S.op("pe", lambda e: e.matmul(pS[:, 0:W], kT[hs, kb * 128:(kb + 1) * 128], qt[hs, qc, 0:W], start=True, stop=True),
                             reads=[kT, qt], writes=[pS])
                    if W == 512:
                        S.op("act", lambda e: e.activation(pw[:, :, :].rearrange("p a w -> p (a w)"),
                                                           psall[:, 2 * r * 512:(2 * r + 2) * 512], AF.Exp, scale=0.125),
                             reads=[PS[2 * r], PS[2 * r + 1]], writes=[pw])
                    else:
                        for k in range(2):
                            S.op("act", lambda e: e.activation(pw[:, k, 0:W], PS[2 * r + k][:, 0:W], AF.Exp, scale=0.125),
                                 reads=[PS[2 * r + k]], writes=[pw])

                def emit_pv(i):
                    kb, us = subs[i]
                    pw = pTw[i % 2]
                    fl = dict(start=(kb == kbs[0]), stop=(kb == kbs[-1]))
                    for k, ui in enumerate(us):
                        if is_diff:
                            a0, a1 = PS[4 + 2 * ui], PS[5 + 2 * ui]
                            S.op("pe", lambda e: e.matmul(a0[0:65, 0:W], vt[:, kb, 0, 0:65], pw[:, k, 0:W], **fl), reads=[vt, pw], writes=[a0])
                            S.op("pe", lambda e: e.matmul(a1[0:64, 0:W], vt[:, kb, 1, 0:64], pw[:, k, 0:W], **fl), reads=[vt, pw], writes=[a1])
                        else:
                            a0 = PS[4 + ui]
                            S.op("pe", lambda e: e.matmul(a0[0:65, 0:W], vt[:, kb, 0, 0:65], pw[:, k, 0:W], **fl), reads=[vt, pw], writes=[a0])

                for i in range(len(subs) + 1):
                    if i < len(subs):
                        emit_s(i)
                    if i >= 1:
                        emit_pv(i - 1)
                if is_diff:
                    for m in range(2):
                        a0, a1 = PS[4 + 2 * m], PS[5 + 2 * m]
                        S.op("act", lambda e: e.activation(xlo[m][0:65, 0:W], a0[0:65, 0:W], AF.Copy), reads=[a0], writes=[xlo[m]])
                        S.op("pe", lambda e: e.matmul(PS[0][0:64, 0:W], cF(CI_SEL)[0:65, 0:64], xlo[m][0:65, 0:W], start=True, stop=True),
                             reads=[xlo[m]], writes=[PS[0]])
                        S.op("dve", lambda e: e.reciprocal(rinv[m][:, 0:W], PS[0][0:64, 0:W]), reads=[PS[0]], writes=[rinv[m]])
                        S.op("dve", lambda e: e.tensor_tensor(olo[m][:, 0:W], xlo[m][0:64, 0:W], rinv[m][:, 0:W], ALU.mult),
                             reads=[xlo[m], rinv[m]], writes=[olo[m]])
                        S.op("dve", lambda e: e.tensor_tensor(ohi[m][:, 0:W], a1[0:64, 0:W], rinv[m][:, 0:W], ALU.mult),
                             reads=[a1, rinv[m]], writes=[ohi[m]])
                    S.op("dve", lambda e: e.scalar_tensor_tensor(dlo[:, 0:W], olo[1][:, 0:W], lamc[0:64, 0:1], olo[0][:, 0:W], ALU.mult, ALU.add),
                         reads=[olo[0], olo[1], lamc], writes=[dlo])
                    S.op("dve", lambda e: e.scalar_tensor_tensor(dhi[:, 0:W], ohi[1][:, 0:W], lamc[0:64, 0:1], ohi[0][:, 0:W], ALU.mult, ALU.add),
                         reads=[ohi[0], ohi[1], lamc], writes=[dhi])
                    S.op("act", lambda e: e.activation(sql[:, 0:W], dlo[:, 0:W], AF.Square), reads=[dlo], writes=[sql])
                    S.op("act", lambda e: e.activation(sqh[:, 0:W], dhi[:, 0:W], AF.Square), reads=[dhi], writes=[sqh])
                    S.op("pe", lambda e: e.matmul(PS[0][0:64, 0:W], cB(CI_ONES)[0:64, 0:64], sql[:, 0:W], start=True, stop=False),
                         reads=[sql], writes=[PS[0]])
                    S.op("pe", lambda e: e.matmul(PS[0][0:64, 0:W], cB(CI_ONES)[0:64, 0:64], sqh[:, 0:W], start=False, stop=True),
                         reads=[sqh], writes=[PS[0]])
                    S.op("act", lambda e: e.activation(rs[:, 0:W], PS[0][0:64, 0:W], AF.Sqrt, bias=EPS, scale=1.0 / 128), reads=[PS[0]], writes=[rs])
                    S.op("dve", lambda e: e.reciprocal(rs[:, 0:W], rs[:, 0:W]), reads=[rs], writes=[rs])
                    for (dd, col, p0) in ((dlo, 0, 0), (dhi, 1, 64)):
                        y = yo[nyo % 4]; nyo += 1
                        S.op("dve", lambda e: e.scalar_tensor_tensor(y[:, 0:W], dd[:, 0:W], subg[0:64, col:col + 1], rs[:, 0:W], ALU.mult, ALU.mult),
                             reads=[dd, subg, rs], writes=[y])
                        S.dma("pool", G["ybT"].ap()[p0:p0 + 64, h, t0:t0 + W], y[:, 0:W], reads=[y])
                else:
                    for j in range(4):
                        head = 4 * n + j
                        a0 = PS[4 + j]
                        m = j % 2
                        S.op("act", lambda e: e.activation(xlo[m][0:65, 0:W], a0[0:65, 0:W], AF.Copy), reads=[a0], writes=[xlo[m]])
                        S.op("pe", lambda e: e.matmul(PS[0][0:64, 0:W], cF(CI_SEL)[0:65, 0:64], xlo[m][0:65, 0:W], start=True, stop=True),
                             reads=[xlo[m]], writes=[PS[0]])
                        S.op("dve", lambda e: e.reciprocal(rinv[m][:, 0:W], PS[0][0:64, 0:W]), reads=[PS[0]], writes=[rinv[m]])
                        y = yo[nyo % 4]; nyo += 1
                        S.op("dve", lambda e: e.tensor_tensor(y[:, 0:W], xlo[m][0:64, 0:W], rinv[m][:, 0:W], ALU.mult),
                             reads=[xlo[m], rinv[m]], writes=[y])
                        p0 = 64 * (head % 2)
                        S.dma("pool", G["ycT"].ap()[p0:p0 + 64, head // 2, t0:t0 + W], y[:, 0:W], reads=[y])


def phase3(nc, S, G):
    PS, layer, v128, rows, aneg = G["PS"], G["layer"], G["v128"], G["rows"], G["aneg"]
    cF, cB = G["cF"], G["cB"]
    xbcT, bcT, xtok, btok, yssd = G["xbcT"], G["bcT"], G["xtok"], G["btok"], G["yssd"]
    NCH = T // 128
    with contextlib.ExitStack() as st:
        xin = [S.sb(st, "c_xin%d" % i, [128, 6, 516], F32) for i in range(2)]
        cv = S.sb(st, "c_cv", [128, 6, 512], F32)
        sl = [S.sb(st, "c_sl%d" % i, [128, 6, 512], BF16) for i in range(2)]
        xtk = [S.sb(st, "c_xtk%d" % i, [128, 640], BF16) for i in range(2)]
        psT = PS[7][:, :].bitcast(BF16)
        nb = 0
        for ti, (t0, W) in enumerate(TILES):
            seg0, seg1 = (0, NCTX) if t0 < NCTX else (NCTX, T)
            xi, sli = xin[ti % 2], sl[ti % 2]
            S.op("pool", lambda e: e.memset(xi[:, :, 0:2], 0.0), writes=[xi])
            S.op("pool", lambda e: e.memset(xi[:, :, W + 2:W + 4], 0.0), writes=[xi])
            lo = t0 - 2 if t0 > seg0 else t0
            hi = t0 + W + 2 if t0 + W < seg1 else t0 + W
            S.dma("sp", xi[:, :, lo - (t0 - 2):hi - (t0 - 2)], xbcT.ap()[:, :, lo:hi], writes=[xi])
            for c in range(6):
                wc = 76 + c * 5
                S.op("dve", lambda e: e.tensor_scalar(cv[:, c, 0:W], xi[:, c, 0:W], v128[:, wc:wc + 1], v128[:, 70 + c:71 + c], ALU.mult, ALU.add),
                     reads=[xi, v128], writes=[cv])
                for k in range(1, 5):
                    S.op("dve", lambda e: e.scalar_tensor_tensor(cv[:, c, 0:W], xi[:, c, k:k + W], v128[:, wc + k:wc + k + 1], cv[:, c, 0:W], ALU.mult, ALU.add),
                         reads=[xi, v128, cv], writes=[cv])
            S.op("act", lambda e: e.activation(sli[:, :, 0:W], cv[:, :, 0:W], AF.Silu), reads=[cv], writes=[sli])
            S.dma("pool", bcT.ap()[0].rearrange("g n t -> (g n) t")[:, t0:t0 + W], sli[:, 4, 0:W], reads=[sli])
            S.dma("pool", bcT.ap()[1].rearrange("g n t -> (g n) t")[:, t0:t0 + W], sli[:, 5, 0:W], reads=[sli])
            for blk in range(W // 128):
                r0 = t0 + blk * 128
                xt_ = xtk[nb % 2]; nb += 1
                for c in range(5):
                    S.op("pe", lambda e: e.transpose(psT[:, c * 128:(c + 1) * 128], sli[:, c, blk * 128:(blk + 1) * 128], cB(CI_ID)),
                         reads=[sli], writes=[PS[7]])
                S.op("act", lambda e: e.activation(xt_[:, :], psT[:, 0:640], AF.Copy), reads=[PS[7]], writes=[xt_])
                S.dma("pool", xtok.ap()[r0:r0 + 128, :], xt_[:, 0:512], reads=[xt_])
                S.dma("pool", btok.ap()[r0:r0 + 128, :], xt_[:, 512:640], reads=[xt_])
    S.barrier()
    yssd1 = G["yssd1"]
    with contextlib.ExitStack() as st:
        own_ck = set()
        for (t0_, W_) in G["out_tiles"]:
            own_ck.update(range(t0_ // 128, (t0_ + W_) // 128))
        last_own = max(own_ck)
        Dd = []
        for d in range(2):
            B = {}
            B["xk"] = [S.sb(st, "s_xk%d_%d" % (d, i), [128, 512], BF16) for i in range(2)]
            B["bk"] = [S.sb(st, "s_bk%d_%d" % (d, i), [128, 128], BF16) for i in range(2)]
            B["bct"] = [S.sb(st, "s_bct%d_%d" % (d, i), [64, 4, 128], BF16) for i in range(2)]
            B["dr"] = [S.sb(st, "s_dr%d_%d" % (d, i), [128, 16], F32) for i in range(2)]
            B["yv"] = [S.sb(st, "s_yv%d_%d" % (d, i), [128, 512], F32) for i in range(2)]
            for nm, shp, dt_ in (("av", [128, 16], F32), ("ab", [128, 8, 128], F32), ("ac", [128, 8], F32), ("ea", [128, 8], F32),
                                 ("cdec", [128, 8], F32), ("arg", [128, 8, 128], F32), ("dec", [128, 8, 128], F32),
                                 ("MT", [128, 8, 128], BF16), ("Bw", [128, 8, 64], BF16), ("xdt", [128, 8, 64], BF16),
                                 ("tmp", [128, 512], F32), ("h32", [64, 8, 64], F32), ("hb", [64, 8, 64], BF16), ("dte8", [128, 16], F32)):
                B[nm] = S.sb(st, "s_%s%d" % (nm, d), shp, dt_)
            B["n"] = 0
            B["pA"], B["pB"], B["pC"], B["pD"] = PS[4 * d], PS[4 * d + 1], PS[4 * d + 2], PS[4 * d + 3]
            S.op("pool", lambda e: e.memset(B["h32"][:], 0.0), writes=[B["h32"]])
            S.op("pool", lambda e: e.memset(B["hb"][:], 0.0), writes=[B["hb"]])
            Dd.append(B)
        orders = [[0, 1] + list(range(2, last_own + 1)), [1, 0] + list(range(NCH - 1, 1, -1))]

        def emit_chunk(d, ck):
            B = Dd[d]
            last_i = 127 if d == 0 else 0
            tri = cF(CI_TRI0 if d == 0 else CI_TRI1)
            mn = cF(CI_MN0 if d == 0 else CI_MN1)
            cols = slice(d * 8, d * 8 + 8)
            pA, pB, pC, pD = B["pA"], B["pB"], B["pC"], B["pD"]
            av, ab, ac, ea, cdec, arg, dec = B["av"], B["ab"], B["ac"], B["ea"], B["cdec"], B["arg"], B["dec"]
            MT, Bw, xdt, tmp, h32, hb, dte8 = B["MT"], B["Bw"], B["xdt"], B["tmp"], B["h32"], B["hb"], B["dte8"]
            c0 = ck * 128
            b = B["n"] % 2
            B["n"] += 1
            xk, bk, bct, dt = B["xk"][b], B["bk"][b], B["bct"][b], B["dr"][b]
            full = (d == 0) or (ck in own_ck)
            S.dma("sp", xk[:, :], xtok.ap()[c0:c0 + 128, :], writes=[xk])
            S.dma("sp", bk[:, :], btok.ap()[c0:c0 + 128, :], writes=[bk])
            if full:
                S.dma("sp", bct[:, :, :], bcT.ap()[:, :, :, c0:c0 + 128].rearrange("k g n t -> n (k g) t"), writes=[bct])
            S.dma("sp", dt[:, :], G["dtt"].ap()[c0:c0 + 128, :], writes=[dt])
            S.op("dve", lambda e: e.tensor_tensor(dt[:, :], dt[:, :], rows[:, 16:32], ALU.add), reads=[dt, rows], writes=[dt])
            S.op("act", lambda e: e.activation(dt[:, :], dt[:, :], AF.Exp), reads=[dt], writes=[dt])
            S.op("act", lambda e: e.activation(dt[:, :], dt[:, :], AF.Ln, bias=1.0, scale=1.0), reads=[dt], writes=[dt])
            S.op("dve", lambda e: e.tensor_tensor(av[:, :], dt[:, :], aneg[:, :], ALU.mult), reads=[dt, aneg], writes=[av])
            yield
            if full:
                S.op("pe", lambda e: e.matmul(pC[:, 0:8], tri, av[:, cols], start=True, stop=True), reads=[av], writes=[pC])
                S.op("dve", lambda e: e.tensor_copy(ab[:, :, :], av[:, cols].unsqueeze(2).broadcast_to([128, 8, 128])), reads=[av], writes=[ab])
                for h in range(8):
                    pr = pA if h < 4 else pB
                    S.op("pe", lambda e: e.matmul(pr[:, (h % 4) * 128:(h % 4 + 1) * 128], ab[:, h, :], tri, start=True, stop=True),
                         reads=[ab], writes=[pr])
                yield
                S.op("act", lambda e: e.activation(ac[:, :], pC[:, 0:8], AF.Copy), reads=[pC], writes=[ac])
                for h in range(8):
                    pr = pA if h < 4 else pB
                    S.op("dve", lambda e: e.scalar_tensor_tensor(arg[:, h, :], pr[:, (h % 4) * 128:(h % 4 + 1) * 128], ac[:, h:h + 1], mn,
                                                                 ALU.subtract, ALU.add), reads=[pr, ac], writes=[arg])
                yield
                S.op("act", lambda e: e.activation(dec[:, :, :], arg[:, :, :], AF.Exp), reads=[arg], writes=[dec])
                S.op("act", lambda e: e.activation(ea[:, :], ac[:, :], AF.Exp), reads=[ac], writes=[ea])
                for hh, pr in enumerate((pA, pB)):
                    S.op("act", lambda e: e.activation(cdec[:, hh * 4:hh * 4 + 4], pr[:, :].rearrange("p (h t) -> p h t", t=128)[:, :, last_i],
                                                       AF.Exp), reads=[pr], writes=[cdec])
                for g in range(2):
                    S.op("pe", lambda e: e.matmul(pC[:, 16 + g * 128:16 + (g + 1) * 128], bct[:, g, :], bct[:, 2 + g, :], start=True, stop=True),
                         reads=[bct], writes=[pC])
                yield
                for g in range(2):
                    S.op("dve", lambda e: e.tensor_tensor(MT[:, 4 * g:4 * g + 4, :], dec[:, 4 * g:4 * g + 4, :],
                                                          pC[:, 16 + g * 128:16 + (g + 1) * 128].unsqueeze(1).broadcast_to([128, 4, 128]), ALU.mult),
                         reads=[dec, pC], writes=[MT])
                    S.op("dve", lambda e: e.tensor_tensor(Bw[:, 4 * g:4 * g + 4, :],
                                                           bk[:, g * 64:(g + 1) * 64].unsqueeze(1).broadcast_to([128, 4, 64]),
                                                           dec[:, 4 * g:4 * g + 4, last_i:last_i + 1].broadcast_to([128, 4, 64]), ALU.mult),
                         reads=[bk, dec], writes=[Bw])
            else:
                S.op("pe", lambda e: e.matmul(pC[:, 0:8], tri, av[:, cols], start=True, stop=True), reads=[av], writes=[pC])
                S.op("pe", lambda e: e.matmul(pC[:, 8:16], cF(CI_ONES), av[:, cols], start=True, stop=True), reads=[av], writes=[pC])
                S.op("act", lambda e: e.activation(dte8[:, :], pC[:, 0:16], AF.Copy), reads=[pC], writes=[dte8])
                S.op("act", lambda e: e.activation(cdec[:, :], dte8[:, 8:16], AF.Exp), reads=[dte8], writes=[cdec])
                S.op("dve", lambda e: e.tensor_tensor(dte8[:, 0:8], dte8[:, 8:16], dte8[:, 0:8], ALU.subtract), reads=[dte8], writes=[dte8])
                S.op("act", lambda e: e.activation(dte8[:, 0:8], dte8[:, 0:8], AF.Exp), reads=[dte8], writes=[dte8])
                for g in range(2):
                    S.op("dve", lambda e: e.tensor_tensor(Bw[:, 4 * g:4 * g + 4, :],
                                                           bk[:, g * 64:(g + 1) * 64].unsqueeze(1).broadcast_to([128, 4, 64]),
                                                           dte8[:, 4 * g:4 * g + 4].unsqueeze(2).broadcast_to([128, 4, 64]), ALU.mult),
                         reads=[bk, dte8], writes=[Bw])
            S.op("dve", lambda e: e.tensor_tensor(xdt[:, :, :], xk[:, :].rearrange("p (h q) -> p h q", q=64),
                                                   dt[:, cols].unsqueeze(2).broadcast_to([128, 8, 64]), ALU.mult),
                 reads=[xk, dt], writes=[xdt])
            yield
            for h in range(8):
                g = h // 4
                hs = slice(h * 64, (h + 1) * 64)
                if full:
                    S.op("pe", lambda e: e.matmul(pA[:, hs], MT[:, h, :], xdt[:, h, :], start=True, stop=True), reads=[MT, xdt], writes=[pA])
                    S.op("pe", lambda e: e.matmul(pB[:, hs], bct[:, 2 + g, :], hb[:, h, :], start=True, stop=True), reads=[bct, hb], writes=[pB])
                S.op("pe", lambda e: e.matmul(pD[0:64, hs], Bw[:, h, :], xdt[:, h, :], start=True, stop=True), reads=[Bw, xdt], writes=[pD])
            yield
            y = B["yv"][b]
            if full:
                S.op("dve", lambda e: e.tensor_tensor(y[:, :].rearrange("p (h q) -> p h q", q=64), pB[:, :].rearrange("p (h q) -> p h q", q=64),
                                                      ea[:, :].unsqueeze(2).broadcast_to([128, 8, 64]), ALU.mult), reads=[pB, ea], writes=[y])
                S.op("dve", lambda e: e.tensor_tensor(y[:, :], y[:, :], pA[:, :], ALU.add), reads=[y, pA], writes=[y])
            S.op("dve", lambda e: e.tensor_tensor(h32[:, :, :], h32[:, :, :], cdec[0:64, :].unsqueeze(2).broadcast_to([64, 8, 64]), ALU.mult),
                 reads=[h32, cdec], writes=[h32])
            S.op("dve", lambda e: e.tensor_tensor(h32[:, :, :], h32[:, :, :], pD[0:64, :].rearrange("p (h q) -> p h q", q=64), ALU.add),
                 reads=[h32, pD], writes=[h32])
            S.op("act", lambda e: e.activation(hb[:, :, :], h32[:, :, :], AF.Copy), reads=[h32], writes=[hb])
            yield
            if not full:
                return
            if d == 0:
                S.op("dve", lambda e: e.tensor_tensor(tmp[:, :].rearrange("p (h q) -> p h q", q=64), xk[:, :].rearrange("p (h q) -> p h q", q=64),
                                                       rows[:, 32:40].unsqueeze(2).broadcast_to([128, 8, 64]), ALU.mult),
                     reads=[xk, rows], writes=[tmp])
                S.op("dve", lambda e: e.tensor_tensor(y[:, :], y[:, :], tmp[:, :], ALU.add), reads=[y, tmp], writes=[y])
                S.dma("pool", yssd.ap()[c0:c0 + 128, :], y[:, :], reads=[y])
            else:
                S.dma("pool", yssd1.ap()[c0:c0 + 128, :], y[:, :], reads=[y])

        for i in range(max(len(orders[0]), len(orders[1]))):
            active = [emit_chunk(d, orders[d][i]) for d in range(2) if i < len(orders[d])]
            while active:
                for g_ in list(active):
                    try:
                        next(g_)
                    except StopIteration:
                        active.remove(g_)
        S.barrier()
    with contextlib.ExitStack() as st:
        y0 = [S.sb(st, "f_y0%d" % i, [128, 512], F32) for i in range(2)]
        y1 = [S.sb(st, "f_y1%d" % i, [128, 512], F32) for i in range(2)]
        zk = [S.sb(st, "f_zk%d" % i, [128, 512], F32) for i in range(2)]
        tmp = S.sb(st, "f_tmp", [128, 512], F32)
        ss = [S.sb(st, "f_ss%d" % i, [128, 2], F32) for i in range(2)]
        yn = [S.sb(st, "f_yn%d" % i, [128, 512], BF16) for i in range(2)]
        yaS = [S.sb(st, "f_ya%d" % i, [128, 4, 128], BF16) for i in range(2)]
        for i, ck in enumerate(sorted(own_ck)):
            c0 = ck * 128
            b = i % 2
            psT = PS[6 + b][:, :].bitcast(BF16)
            S.dma("sp", y0[b][:, :], yssd.ap()[c0:c0 + 128, :], writes=[y0[b]])
            S.dma("sp", y1[b][:, :], yssd1.ap()[c0:c0 + 128, :], writes=[y1[b]])
            S.dma("sp", zk[b][:, :], G["zt"].ap()[c0:c0 + 128, :], writes=[zk[b]])
            y = y0[b]
            S.op("dve", lambda e: e.tensor_tensor(y[:, :], y[:, :], y1[b][:, :], ALU.add), reads=[y, y1[b]], writes=[y])
            S.op("act", lambda e: e.activation(zk[b][:, :], zk[b][:, :], AF.Silu), reads=[zk[b]], writes=[zk[b]])
            S.op("dve", lambda e: e.tensor_tensor(y[:, :], y[:, :], zk[b][:, :], ALU.mult), reads=[y, zk[b]], writes=[y])
            S.op("act", lambda e: e.activation(tmp[:, :], y[:, :], AF.Square, accum_out=ss[b][:, 0:1]), reads=[y], writes=[tmp, ss[b]])
            S.op("act", lambda e: e.activation(ss[b][:, 1:2], ss[b][:, 0:1], AF.Sqrt, bias=EPS, scale=1.0 / 512), reads=[ss[b]], writes=[ss[b]])
            S.op("dve", lambda e: e.reciprocal(ss[b][:, 1:2], ss[b][:, 1:2]), reads=[ss[b]], writes=[ss[b]])
            S.op("dve", lambda e: e.scalar_tensor_tensor(yn[b][:, :], y[:, :], ss[b][:, 1:2], rows[:, 40:552], ALU.mult, ALU.mult),
                 reads=[y, ss[b], rows], writes=[yn[b]])
            for c in range(4):
                S.op("pe", lambda e: e.transpose(psT[:, c * 128:(c + 1) * 128], yn[b][:, c * 128:(c + 1) * 128], cB(CI_ID)),
                     reads=[yn[b]], writes=[PS[6 + b]])
            ya = yaS[b]
            S.op("act", lambda e: e.activation(ya[:, :, :], psT[:, 0:512].rearrange("p (c t) -> p c t", t=128), AF.Copy),
                 reads=[PS[6 + b]], writes=[ya])
            S.dma("pool", G["yaT"].ap()[:, :, c0:c0 + 128], ya[:, :, :], reads=[ya])
        S.barrier()


def _rev(a, n):
    return bass.AP(a.tensor, a.offset + n - 1, [list(a.ap[0]), [-1, n]])


def phase4(nc, S, G):
    PS, layer, v128 = G["PS"], G["layer"], G["v128"]
    cF, cB = G["cF"], G["cB"]
    uT, ys5 = G["uT"], G["ys5"]
    I32 = mybir.dt.int32
    TWO_PI = 2.0 * math.pi
    with contextlib.ExitStack() as st:
        Bb = S.sb(st, "z_Bb", [128, 2, 12, 128], BF16)
        Cb = S.sb(st, "z_Cb", [128, 2, 12, 128], BF16)
        gw = S.sb(st, "z_gw", [128, 3, 768], BF16)
        for ri in range(2):
            S.dma("pool", Bb[:, ri, :, :], G["s5_B"].ap()[layer, ri].rearrange("gp k m -> k gp m"), writes=[Bb])
            S.dma("pool", Cb[:, ri, :, :], G["s5_C"].ap()[layer, ri].rearrange("gp k m -> k gp m"), writes=[Cb])
        S.dma("pool", gw[:, :, :], G["glu_w"].ap()[layer].rearrange("(kc p) n -> p kc n", p=128), writes=[gw])
        def tt(out, a, b, op, tiles_r, tiles_w, eng="dve"):
            S.op(eng, lambda e: e.tensor_tensor(out, a, b, op), reads=tiles_r, writes=tiles_w)

        def ts(out, a, s1, s2, op0, op1, tiles_r, tiles_w):
            if op1 is None:
                S.op("dve", lambda e: e.tensor_scalar(out, a, s1, None, op0), reads=tiles_r, writes=tiles_w)
            else:
                S.op("dve", lambda e: e.tensor_scalar(out, a, s1, s2, op0, op1), reads=tiles_r, writes=tiles_w)

        TL = 256
        ub = [S.sb(st, "z_ub%d" % i, [128, 3, 512], F32) for i in range(2)]
        ubb = [S.sb(st, "z_ubb%d" % i, [128, 3, 512], BF16) for i in range(2)]
        NB = 3
        br = [S.sb(st, "z_br%d" % i, [128, 512], F32) for i in range(NB)]
        bi = [S.sb(st, "z_bi%d" % i, [128, 512], F32) for i in range(NB)]
        m = [[S.sb(st, "z_m%d_%d" % (i, j), [128, 512], F32) for j in range(4)] for i in range(NB)]
        gr = [m[i][1] for i in range(NB)]
        gi = [m[i][3] for i in range(NB)]
        hrb = [S.sb(st, "z_hrb%d" % i, [128, 512], BF16) for i in range(NB)]
        hib = [S.sb(st, "z_hib%d" % i, [128, 512], BF16) for i in range(NB)]
        hst = S.sb(st, "z_hst", [128, 12, 2], F32)
        tn = S.sb(st, "z_tn", [128, 4], F32)
        yst = [S.sb(st, "z_yst%d" % i, [128, 512], F32) for i in range(2)]
        y0 = S.sb(st, "z_y0", [128, 3, 512], F32)
        gy = S.sb(st, "z_gy", [128, 3, 512], BF16)
        vl = [S.sb(st, "z_vl%d" % i, [128, 512], F32) for i in range(3)]
        sgt = [S.sb(st, "z_sg%d" % i, [128, 512], F32) for i in range(2)]
        ydo = [S.sb(st, "z_ydo%d" % i, [128, 512], BF16) for i in range(2)]
        nn = 0
        for d in range(2):
            with contextlib.ExitStack() as sd:
                Er = S.sb(sd, "z_Er", [128, 12, TL], F32)
                Ei = S.sb(sd, "z_Ei", [128, 12, TL], F32)
                Fr = S.sb(sd, "z_Fr", [128, 12, TL], F32)
                Fi = S.sb(sd, "z_Fi", [128, 12, TL], F32)
                rho = S.sb(sd, "z_rho", [128, 12], F32)
                EW = S.sb(sd, "z_EW", [128, 12, 2], F32)
                with contextlib.ExitStack() as st2:
                    t1 = S.sb(st2, "z_t1", [128, 12, TL], F32)
                    t2 = S.sb(st2, "z_t2", [128, 12, TL], F32)
                    names = ["lr", "li", "stp", "u", "f", "sphi", "s2", "c1", "ar", "ai", "den", "am1", "fr", "fi", "x1", "x2", "msk"]
                    P = {nm: S.sb(st2, "zp_%s" % nm, [128, 12], F32) for nm in names}
                    P["rho"] = rho
                    ki = S.sb(st2, "zp_ki", [128, 12], I32)
                    A = lambda nm: P[nm][:, :]
                    S.dma("sp", A("lr"), G["s5_lr"].ap()[layer, d], writes=[P["lr"]])
                    S.dma("sp", A("li"), G["s5_li"].ap()[layer, d], writes=[P["li"]])
                    S.dma("sp", A("stp"), G["s5_ldt"].ap()[layer, d], writes=[P["stp"]])
                    S.op("act", lambda e: e.activation(A("stp"), A("stp"), AF.Exp), reads=[P["stp"]], writes=[P["stp"]])
                    tt(A("rho"), A("lr"), A("stp"), ALU.mult, [P["lr"], P["stp"]], [P["rho"]])
                    S.op("act", lambda e: e.activation(A("rho"), A("rho"), AF.Exp), reads=[P["rho"]], writes=[P["rho"]])
                    tt(A("u"), A("li"), A("stp"), ALU.mult, [P["li"], P["stp"]], [P["u"]])
                    ts(A("u"), A("u"), 1.0 / TWO_PI, 0.5, ALU.mult, ALU.add, [P["u"]], [P["u"]])
                    S.op("dve", lambda e: e.tensor_copy(ki[:, :], A("u")), reads=[P["u"]], writes=[ki])
                    S.op("dve", lambda e: e.tensor_copy(A("f"), ki[:, :]), reads=[ki], writes=[P["f"]])
                    tt(A("f"), A("u"), A("f"), ALU.subtract, [P["u"], P["f"]], [P["f"]])
                    ts(A("msk"), A("f"), 0.5, None, ALU.is_ge, None, [P["f"]], [P["msk"]])
                    tt(A("f"), A("f"), A("msk"), ALU.subtract, [P["f"], P["msk"]], [P["f"]])
                    ts(A("f"), A("f"), -0.49999, 0.49999, ALU.max, ALU.min, [P["f"]], [P["f"]])
                    S.op("act", lambda e: e.activation(A("sphi"), A("f"), AF.Sin, scale=TWO_PI), reads=[P["f"]], writes=[P["sphi"]])
                    S.op("act", lambda e: e.activation(A("s2"), A("f"), AF.Sin, scale=math.pi), reads=[P["f"]], writes=[P["s2"]])
                    tt(A("c1"), A("s2"), A("s2"), ALU.mult, [P["s2"]], [P["c1"]])
                    ts(A("c1"), A("c1"), 2.0, -1.0, ALU.mult, ALU.add, [P["c1"]], [P["c1"]])
                    S.op("dve", lambda e: e.tensor_copy(Er[:, :, 0], A("c1")), reads=[P["c1"]], writes=[Er])
                    S.op("dve", lambda e: e.tensor_copy(Ei[:, :, 0], A("sphi")), reads=[P["sphi"]], writes=[Ei])
                    tt(A("ar"), A("rho"), A("c1"), ALU.mult, [P["rho"], P["c1"]], [P["ar"]])
                    tt(A("ai"), A("rho"), A("sphi"), ALU.mult, [P["rho"], P["sphi"]], [P["ai"]])
                    ts(A("ai"), A("ai"), -1.0, None, ALU.mult, None, [P["ai"]], [P["ai"]])
                    tt(A("den"), A("lr"), A("lr"), ALU.mult, [P["lr"]], [P["den"]])
                    tt(A("x1"), A("li"), A("li"), ALU.mult, [P["li"]], [P["x1"]])
                    tt(A("den"), A("den"), A("x1"), ALU.add, [P["den"], P["x1"]], [P["den"]])
                    S.op("dve", lambda e: e.reciprocal(A("den"), A("den")), reads=[P["den"]], writes=[P["den"]])
                    ts(A("am1"), A("ar"), -1.0, None, ALU.add, None, [P["ar"]], [P["am1"]])
                    tt(A("x1"), A("am1"), A("lr"), ALU.mult, [P["am1"], P["lr"]], [P["x1"]])
                    tt(A("x2"), A("ai"), A("li"), ALU.mult, [P["ai"], P["li"]], [P["x2"]])
                    tt(A("fr"), A("x1"), A("x2"), ALU.add, [P["x1"], P["x2"]], [P["fr"]])
                    tt(A("fr"), A("fr"), A("den"), ALU.mult, [P["fr"], P["den"]], [P["fr"]])
                    tt(A("x1"), A("ai"), A("lr"), ALU.mult, [P["ai"], P["lr"]], [P["x1"]])
                    tt(A("x2"), A("am1"), A("li"), ALU.mult, [P["am1"], P["li"]], [P["x2"]])
                    tt(A("fi"), A("x1"), A("x2"), ALU.subtract, [P["x1"], P["x2"]], [P["fi"]])
                    tt(A("fi"), A("fi"), A("den"), ALU.mult, [P["fi"], P["den"]], [P["fi"]])
                    n = 1
                    while n < TL:
                        cr = Er[:, :, n - 1:n].broadcast_to([128, 12, n])
                        ci = Ei[:, :, n - 1:n].broadcast_to([128, 12, n])
                        tt(t1[:, :, 0:n], Er[:, :, 0:n], cr, ALU.mult, [Er], [t1])
                        tt(t2[:, :, 0:n], Ei[:, :, 0:n], ci, ALU.mult, [Ei], [t2])
                        tt(Er[:, :, n:2 * n], t1[:, :, 0:n], t2[:, :, 0:n], ALU.subtract, [t1, t2], [Er])
                        tt(t1[:, :, 0:n], Er[:, :, 0:n], ci, ALU.mult, [Er, Ei], [t1])
                        tt(t2[:, :, 0:n], Ei[:, :, 0:n], cr, ALU.mult, [Ei, Er], [t2])
                        tt(Ei[:, :, n:2 * n], t1[:, :, 0:n], t2[:, :, 0:n], ALU.add, [t1, t2], [Ei])
                        n *= 2
                    S.op("dve", lambda e: e.tensor_copy(EW[:, :, 0], Er[:, :, TL - 1]), reads=[Er], writes=[EW])
                    S.op("dve", lambda e: e.tensor_copy(EW[:, :, 1], Ei[:, :, TL - 1]), reads=[Ei], writes=[EW])
                    frb = P["fr"][:, :].unsqueeze(2).broadcast_to([128, 12, TL])
                    fib = P["fi"][:, :].unsqueeze(2).broadcast_to([128, 12, TL])
                    tt(t1[:, :, :], Er[:, :, :], frb, ALU.mult, [Er, P["fr"]], [t1])
                    tt(t2[:, :, :], Ei[:, :, :], fib, ALU.mult, [Ei, P["fi"]], [t2])
                    tt(Fr[:, :, :], t1[:, :, :], t2[:, :, :], ALU.subtract, [t1, t2], [Fr])
                    tt(t1[:, :, :], Ei[:, :, :], frb, ALU.mult, [Ei, P["fr"]], [t1])
                    tt(t2[:, :, :], Er[:, :, :], fib, ALU.mult, [Er, P["fi"]], [t2])
                    tt(Fi[:, :, :], t1[:, :, :], t2[:, :, :], ALU.add, [t1, t2], [Fi])
                    if d == 1:
                        for tb in (Er, Ei, Fr, Fi):
                            for gp in range(12):
                                S.op("dve", lambda e: e.tensor_copy(t1[:, gp, :], _rev(tb[:, gp, :], TL)), reads=[tb], writes=[t1])
                            S.op("dve", lambda e: e.tensor_copy(tb[:, :, :], t1[:, :, :]), reads=[t1], writes=[tb])
                    S.barrier()

                own_set = set(G["out_tiles"]) | {TILES[0]}
                if d == 0:
                    last_own = max(i for i, tl in enumerate(TILES) if tl in own_set)
                    tiles = TILES[:last_own + 1]
                else:
                    tiles = [TILES[0]] + TILES[:0:-1]
                S.op("pool", lambda e: e.memset(hst[:], 0.0), writes=[hst])
                for ti, (t0, W) in enumerate(tiles):
                    nfr = W // TL
                    full = (t0, W) in set(G["out_tiles"]) or d == 0
                    u_, ub_ = ub[ti % 2], ubb[ti % 2]
                    S.dma("sp", u_[:, :, 0:W], uT.ap()[:, :, t0:t0 + W], writes=[u_])
                    S.op("act", lambda e: e.activation(ub_[:, :, 0:W], u_[:, :, 0:W], AF.Copy), reads=[u_], writes=[ub_])
                    if d == 1 and full:
                        S.dma("sp", y0[:, :, 0:W], ys5.ap()[:, :, t0:t0 + W], writes=[y0])
                    for gp in range(12):
                        uc = gp // 4
                        b = nn % NB; nn += 1
                        mm = m[b]
                        S.op("pe", lambda e: e.matmul(PS[0][:, 0:W], Bb[:, 0, gp, :], ub_[:, uc, 0:W], start=True, stop=True), reads=[Bb, ub_], writes=[PS[0]])
                        S.op("pe", lambda e: e.matmul(PS[1][:, 0:W], Bb[:, 1, gp, :], ub_[:, uc, 0:W], start=True, stop=True), reads=[Bb, ub_], writes=[PS[1]])
                        S.op("act", lambda e: e.activation(br[b][:, 0:W], PS[0][:, 0:W], AF.Copy), reads=[PS[0]], writes=[br[b]])
                        S.op("act", lambda e: e.activation(bi[b][:, 0:W], PS[1][:, 0:W], AF.Copy), reads=[PS[1]], writes=[bi[b]])

                        def bc(tb):
                            return tb[:, gp, :].unsqueeze(1).broadcast_to([128, nfr, TL])

                        def v3(t):
                            return t[:, 0:W].rearrange("p (c k) -> p c k", k=TL)
                        tt(v3(mm[0]), v3(br[b]), bc(Fr), ALU.mult, [br[b], Fr], [mm[0]], "dve")
                        tt(v3(mm[1]), v3(bi[b]), bc(Fi), ALU.mult, [bi[b], Fi], [mm[1]], "dve")
                        tt(v3(mm[2]), v3(bi[b]), bc(Fr), ALU.mult, [bi[b], Fr], [mm[2]], "dve")
                        tt(v3(mm[3]), v3(br[b]), bc(Fi), ALU.mult, [br[b], Fi], [mm[3]], "dve")
                        tt(mm[0][:, 0:W], mm[0][:, 0:W], mm[1][:, 0:W], ALU.subtract, [mm[0], mm[1]], [mm[0]], "dve")
                        tt(mm[2][:, 0:W], mm[2][:, 0:W], mm[3][:, 0:W], ALU.add, [mm[2], mm[3]], [mm[2]], "dve")
                        frs = list(range(nfr)) if d == 0 else list(range(nfr - 1, -1, -1))
                        for fk in frs:
                            cs = slice(fk * TL, (fk + 1) * TL)
                            for (gt, vt_, comp) in ((gr[b], mm[0], 0), (gi[b], mm[2], 1)):
                                o_ap, v_ap = gt[:, cs], vt_[:, cs]
                                if d == 1:
                                    o_ap, v_ap = _rev(o_ap, TL), _rev(v_ap, TL)
                                S.op("dve", lambda e: e.tensor_tensor_scan(o_ap, rho[:, gp:gp + 1].broadcast_to([128, TL]), v_ap,
                                                                           hst[:, gp, comp:comp + 1], ALU.mult, ALU.add),
                                     reads=[rho, vt_, hst], writes=[gt])
                            ie = fk * TL + (TL - 1 if d == 0 else 0)
                            gre, gie = gr[b][:, ie:ie + 1], gi[b][:, ie:ie + 1]
                            e_r, e_i = EW[:, gp, 0:1], EW[:, gp, 1:2]
                            tt(tn[:, 0:1], gie, e_i, ALU.mult, [gi[b], EW], [tn], "pool")
                            tt(tn[:, 1:2], gre, e_i, ALU.mult, [gr[b], EW], [tn], "pool")
                            tt(tn[:, 2:3], gre, e_r, ALU.mult, [gr[b], EW], [tn], "pool")
                            tt(tn[:, 3:4], gie, e_r, ALU.mult, [gi[b], EW], [tn], "pool")
                            tt(hst[:, gp, 0:1], tn[:, 2:3], tn[:, 0:1], ALU.add, [tn], [hst], "pool")
                            tt(hst[:, gp, 1:2], tn[:, 3:4], tn[:, 1:2], ALU.subtract, [tn], [hst], "pool")
                        if not full:
                            continue
                        tt(v3(mm[0]), v3(gr[b]), bc(Er), ALU.mult, [gr[b], Er], [mm[0]], "dve")
                        tt(v3(mm[2]), v3(gi[b]), bc(Ei), ALU.mult, [gi[b], Ei], [mm[2]], "dve")
                        tt(v3(br[b]), v3(gr[b]), bc(Ei), ALU.mult, [gr[b], Ei], [br[b]], "dve")
                        tt(v3(bi[b]), v3(gi[b]), bc(Er), ALU.mult, [gi[b], Er], [bi[b]], "dve")
                        tt(hrb[b][:, 0:W], mm[0][:, 0:W], mm[2][:, 0:W], ALU.add, [mm[0], mm[2]], [hrb[b]], "dve")
                        tt(hib[b][:, 0:W], br[b][:, 0:W], bi[b][:, 0:W], ALU.subtract, [br[b], bi[b]], [hib[b]], "dve")
                        py = PS[2 + uc]
                        S.op("pe", lambda e: e.matmul(py[:, 0:W], Cb[:, 0, gp, :], hrb[b][:, 0:W], start=(gp % 4 == 0), stop=False), reads=[Cb, hrb[b]], writes=[py])
                        S.op("pe", lambda e: e.matmul(py[:, 0:W], Cb[:, 1, gp, :], hib[b][:, 0:W], start=False, stop=(gp % 4 == 3)), reads=[Cb, hib[b]], writes=[py])
                    if not full:
                        continue
                    for uc in range(3):
                        py = PS[2 + uc]
                        ys_ = yst[uc % 2]
                        if d == 0:
                            S.op("dve", lambda e: e.scalar_tensor_tensor(ys_[:, 0:W], u_[:, uc, 0:W], v128[:, 106 + uc:107 + uc], py[:, 0:W], ALU.mult, ALU.add),
                                 reads=[u_, v128, py], writes=[ys_])
                            S.dma("pool", ys5.ap()[:, uc, t0:t0 + W], ys_[:, 0:W], reads=[ys_])
                        else:
                            tt(ys_[:, 0:W], y0[:, uc, 0:W], py[:, 0:W], ALU.add, [y0, py], [ys_], "dve")
                            S.op("act", lambda e: e.activation(gy[:, uc, 0:W], ys_[:, 0:W], AF.Gelu_apprx_tanh), reads=[ys_], writes=[gy])
                    if d == 1:
                        for j in range(6):
                            pg = PS[5 + j % 2]
                            for kc in range(3):
                                S.op("pe", lambda e: e.matmul(pg[:, 0:W], gw[:, kc, j * 128:(j + 1) * 128], gy[:, kc, 0:W], start=(kc == 0), stop=(kc == 2)),
                                     reads=[gw, gy], writes=[pg])
                            if j < 3:
                                S.op("act", lambda e: e.activation(vl[j][:, 0:W], pg[:, 0:W], AF.Identity, bias=v128[:, 109 + j:110 + j], scale=1.0),
                                     reads=[pg, v128], writes=[vl[j]])
                            else:
                                sg_ = sgt[j % 2]
                                yo_ = ydo[j % 2]
                                S.op("act", lambda e: e.activation(sg_[:, 0:W], pg[:, 0:W], AF.Sigmoid, bias=v128[:, 109 + j:110 + j], scale=1.0),
                                     reads=[pg, v128], writes=[sg_])
                                tt(yo_[:, 0:W], vl[j - 3][:, 0:W], sg_[:, 0:W], ALU.mult, [vl[j - 3], sg_], [yo_], "dve")
                                S.dma("pool", G["ydT"].ap()[:, j - 3, t0:t0 + W], yo_[:, 0:W], reads=[yo_])
                S.barrier()


def phase5(nc, S, G):
    PS, layer, v128, modv, A2 = G["PS"], G["layer"], G["v128"], G["modv"], G["A2"]
    cF, cB = G["cF"], G["cB"]
    h_src, h_dst, last = G["h_src"], G["h_dst"], G["last"]
    BRK = G["BRK"]
    ysrc = [G["yaT"], G["ybT"], G["ycT"], G["ydT"]]
    with contextlib.ExitStack() as st:
        xn = S.sb(st, "m_xn", [128, KC, 512], BF16)
        ys = [S.sb(st, "m_y%d" % i, [128, BRK[i], 512], BF16) for i in range(4)]
        ht = S.sb(st, "m_h", [128, KC, 512], F32)
        acc = S.sb(st, "m_acc", [128, KC, 512], F32)
        mb = S.sb(st, "m_mb", [128, KC, 512], BF16)
        h1 = S.sb(st, "m_h1", [128, KC, 512], F32)
        sq = S.sb(st, "m_sq", [128, KC, 512], BF16)
        rstd = S.sb(st, "m_rstd", [128, 512], F32)
        xf = S.sb(st, "m_xf", [128, KC, 512], BF16)
        hid = S.sb(st, "m_hid", [128, 22, 512], BF16)
        h2 = [S.sb(st, "m_h2%d" % i, [128, 512], F32) for i in range(2)]
        sg = [S.sb(st, "m_sg%d" % i, [128, 512], F32) for i in range(2)]
        tm = [S.sb(st, "m_tm%d" % i, [128, 512], F32) for i in range(2)]
        wg = [S.sb(st, "m_wg%d" % i, [128, 4, KC, 128], BF16) for i in range(2)]
        wb = [S.sb(st, "m_wb%d" % i, [128, 15, 128], BF16) for i in range(2)]
        wo = [S.sb(st, "m_wo%d" % i, [128, KC, 128], BF16) for i in range(2)]
        wgu = [S.sb(st, "m_wgu%d" % i, [128, 2, KC, 128], BF16) for i in range(2)]
        wdn = [S.sb(st, "m_wdn%d" % i, [128, 22, 128], BF16) for i in range(2)]
        BOFF = [0, 4, 8, 12]
        npp = 0
        for (t0, W) in G["out_tiles"]:
            s = 1 if t0 < NCTX else 0
            S.dma("sp", xn[:, :, 0:W], G["xnT"].ap()[:, :, t0:t0 + W], writes=[xn])
            for i in range(4):
                S.dma("sp", ys[i][:, :, 0:W], ysrc[i].ap()[:, :, t0:t0 + W], writes=[ys[i]])
            S.dma("sp", ht[:, :, 0:W], h_src.ap().rearrange("(kc p) t -> p kc t", p=128)[:, :, t0:t0 + W], writes=[ht])
            for j in range(8):
                wgj, wbj = wg[j % 2], wb[j % 2]
                for i in range(4):
                    S.dma("sp", wgj[:, i, :, :], G["wg_bf"].ap()[i * 8 + j], writes=[wgj])
                    S.dma("sp", wbj[:, BOFF[i]:BOFF[i] + BRK[i], :], G["wb_bf"][i].ap()[j], writes=[wbj])
                for i in range(4):
                    pg, pb = PS[npp % 2], PS[2 + npp % 2]
                    sgi, tmi = sg[npp % 2], tm[npp % 2]
                    npp += 1
                    for kc in range(KC):
                        S.op("pe", lambda e: e.matmul(pg[:, 0:W], wgj[:, i, kc, :], xn[:, kc, 0:W], start=(kc == 0), stop=(kc == KC - 1)),
                             reads=[wgj, xn], writes=[pg])
                    for kc in range(BRK[i]):
                        S.op("pe", lambda e: e.matmul(pb[:, 0:W], wbj[:, BOFF[i] + kc, :], ys[i][:, kc, 0:W], start=(kc == 0), stop=(kc == BRK[i] - 1)),
                             reads=[wbj, ys[i]], writes=[pb])
                    S.op("act", lambda e: e.activation(sgi[:, 0:W], pg[:, 0:W], AF.Sigmoid), reads=[pg], writes=[sgi])
                    if i == 0:
                        S.op("dve", lambda e: e.tensor_tensor(acc[:, j, 0:W], sgi[:, 0:W], pb[:, 0:W], ALU.mult), reads=[sgi, pb], writes=[acc])
                    else:
                        S.op("dve", lambda e: e.tensor_tensor(tmi[:, 0:W], sgi[:, 0:W], pb[:, 0:W], ALU.mult), reads=[sgi, pb], writes=[tmi])
                        S.op("dve", lambda e: e.tensor_tensor(acc[:, j, 0:W], acc[:, j, 0:W], tmi[:, 0:W], ALU.add), reads=[acc, tmi], writes=[acc])
                S.op("act", lambda e: e.activation(mb[:, j, 0:W], acc[:, j, 0:W], AF.Copy), reads=[acc], writes=[mb])
            for j in range(8):
                woj = wo[j % 2]
                S.dma("sp", woj[:], G["wo_bf"].ap()[j], writes=[woj])
                po = PS[4 + j % 2]
                for kc in range(KC):
                    S.op("pe", lambda e: e.matmul(po[:, 0:W], woj[:, kc, :], mb[:, kc, 0:W], start=(kc == 0), stop=(kc == KC - 1)),
                         reads=[woj, mb], writes=[po])
                S.op("dve", lambda e: e.scalar_tensor_tensor(h1[:, j, 0:W], po[:, 0:W], modv[:, 16 + j, s:s + 1], ht[:, j, 0:W], ALU.mult, ALU.add),
                     reads=[po, modv, ht], writes=[h1])
            S.op("act", lambda e: e.activation(sq[:, :, 0:W], h1[:, :, 0:W], AF.Square), reads=[h1], writes=[sq])
            for kc in range(KC):
                S.op("pe", lambda e: e.matmul(PS[6][:, 0:W], cB(CI_ONES), sq[:, kc, 0:W], start=(kc == 0), stop=(kc == KC - 1)),
                     reads=[sq], writes=[PS[6]])
            S.op("act", lambda e: e.activation(rstd[:, 0:W], PS[6][:, 0:W], AF.Sqrt, bias=EPS, scale=1.0 / D), reads=[PS[6]], writes=[rstd])
            S.op("dve", lambda e: e.reciprocal(rstd[:, 0:W], rstd[:, 0:W]), reads=[rstd], writes=[rstd])
            for kc in range(KC):
                tmi = tm[kc % 2]
                S.op("dve", lambda e: e.tensor_tensor(tmi[:, 0:W], h1[:, kc, 0:W], rstd[:, 0:W], ALU.mult), reads=[h1, rstd], writes=[tmi])
                S.op("act", lambda e: e.activation(xf[:, kc, 0:W], tmi[:, 0:W], AF.Identity, bias=modv[:, 24 + kc, s:s + 1], scale=A2[:, kc, s:s + 1]),
                     reads=[tmi, modv, A2], writes=[xf])
            for jj in range(22):
                w = wgu[jj % 2]
                S.dma("sp", w[:, 0, :, :], G["wgu_bf"].ap()[jj], writes=[w])
                S.dma("sp", w[:, 1, :, :], G["wgu_bf"].ap()[22 + jj], writes=[w])
                pg, pu = PS[jj % 2], PS[2 + jj % 2]
                sgi = sg[jj % 2]
                for kc in range(KC):
                    S.op("pe", lambda e: e.matmul(pg[:, 0:W], w[:, 0, kc, :], xf[:, kc, 0:W], start=(kc == 0), stop=(kc == KC - 1)),
                         reads=[w, xf], writes=[pg])
                for kc in range(KC):
                    S.op("pe", lambda e: e.matmul(pu[:, 0:W], w[:, 1, kc, :], xf[:, kc, 0:W], start=(kc == 0), stop=(kc == KC - 1)),
                         reads=[w, xf], writes=[pu])
                S.op("act", lambda e: e.activation(sgi[:, 0:W], pg[:, 0:W], AF.Silu), reads=[pg], writes=[sgi])
                S.op("dve", lambda e: e.tensor_tensor(hid[:, jj, 0:W], sgi[:, 0:W], pu[:, 0:W], ALU.mult), reads=[sgi, pu], writes=[hid])
            for j in range(8):
                w = wdn[j % 2]
                S.dma("sp", w[:], G["wdn_bf"].ap()[j], writes=[w])
                po = PS[4 + j % 2]
                for kc in range(22):
                    S.op("pe", lambda e: e.matmul(po[:, 0:W], w[:, kc, :], hid[:, kc, 0:W], start=(kc == 0), stop=(kc == 21)),
                         reads=[w, hid], writes=[po])
                o = h2[j % 2]
                S.op("dve", lambda e: e.scalar_tensor_tensor(o[:, 0:W], po[:, 0:W], modv[:, 40 + j, s:s + 1], h1[:, j, 0:W], ALU.mult, ALU.add),
                     reads=[po, modv, h1], writes=[o])
                if last:
                    S.dma("pool", h_dst.ap()[j * 128:(j + 1) * 128, t0 - NCTX:t0 - NCTX + W], o[:, 0:W], reads=[o])
                else:
                    S.dma("pool", h_dst.ap()[j * 128:(j + 1) * 128, t0:t0 + W], o[:, 0:W], reads=[o])


def _prep_shared(inp):
    out = {}
    f32 = np.float32
    for half in (0, 1):
        d = {}
        dsel = [0, 1] if half == 0 else [1, 0]
        for k in ("w_mod", "w_gate", "w_br_ssd", "w_br_diff", "w_br_gqa", "w_br_s5", "w_out", "ffn_w_gate_up",
                  "ffn_w_down", "s5_glu_w"):
            d[k] = np.ascontiguousarray(inp[k], dtype=f32)
        w_in = np.array(inp["w_in"], dtype=f32)
        if half == 1:
            tmp = w_in[:, :, C_DT:C_DT + 8].copy()
            w_in[:, :, C_DT:C_DT + 8] = w_in[:, :, C_DT + 8:C_DT + 16]
            w_in[:, :, C_DT + 8:C_DT + 16] = tmp
        d["w_in"] = w_in
        v = np.zeros((2, 128, NV), f32)
        r = np.zeros((2, NR), f32)
        for l in range(2):
            v[l, :, 0:8] = inp["norm1_g"][l].reshape(8, 128).T
            v[l, :, 8:16] = inp["norm2_g"][l].reshape(8, 128).T
            v[l, :, 16:64] = inp["b_mod"][l].reshape(48, 128).T
            v[l, :, 64] = np.tile(inp["diff_qn_g"][l], 2)
            v[l, :, 65] = np.tile(inp["diff_kn_g"][l], 2)
            v[l, :, 66] = np.tile(inp["gqa_qn_g"][l], 2)
            v[l, :, 67] = np.tile(inp["gqa_kn_g"][l], 2)
            v[l, :, 68] = np.tile(inp["diff_subln_g"][l][:64], 2)
            v[l, :, 69] = np.tile(inp["diff_subln_g"][l][64:], 2)
            v[l, :, 70:76] = inp["ssd_conv_b"][l].reshape(6, 128).T
            cw = inp["ssd_conv_w"][l]
            if half == 1:
                cw = cw[::-1]
            v[l, :, 76:106] = cw.reshape(5, 6, 128).transpose(2, 1, 0).reshape(128, 30)
            v[l, :, 106:109] = inp["s5_d"][l].reshape(3, 128).T
            v[l, :, 109:115] = inp["s5_glu_b"][l].reshape(6, 128).T
            r[l, 0:16] = inp["ssd_a_log"][l][dsel].reshape(16)
            r[l, 16:32] = inp["ssd_dt_bias"][l][dsel].reshape(16)
            r[l, 32:40] = inp["ssd_d"][l]
            r[l, 40:552] = inp["ssd_norm_g"][l]
            r[l, 552:616] = inp["diff_lam_q1"][l]
            r[l, 616:680] = inp["diff_lam_k1"][l]
            r[l, 680:744] = inp["diff_lam_q2"][l]
            r[l, 744:808] = inp["diff_lam_k2"][l]
        d["vec128"] = v
        d["rowvecs"] = r

        def s5lay(a):
            a = np.asarray(a, f32)[:, dsel]
            return np.ascontiguousarray(a.reshape(2, 2, 12, 2, 64).transpose(0, 1, 3, 4, 2).reshape(2, 2, 128, 12))
        d["s5_lr"] = s5lay(inp["s5_lam_re"])
        d["s5_li"] = s5lay(inp["s5_lam_im"])
        ldt = np.asarray(inp["s5_log_dt"], f32)
        d["s5_ldt"] = s5lay(np.broadcast_to(ldt[..., None], (2, 2, 24, 64)))
        Bb = np.zeros((2, 2, 12, 128, 128), f32)
        Cb = np.zeros((2, 2, 12, 128, 128), f32)
        for ri, (bk, ck) in enumerate((("s5_b_re", "s5_c_re"), ("s5_b_im", "s5_c_im"))):
            b = np.asarray(inp[bk], f32)
            c = np.asarray(inp[ck], f32)
            for gp in range(12):
                for two in range(2):
                    g = 2 * gp + two
                    r0 = 32 * (gp % 4) + 16 * two
                    Bb[:, ri, gp, r0:r0 + 16, 64 * two:64 * two + 64] = b[:, g].transpose(0, 2, 1)
                    Cb[:, ri, gp, 64 * two:64 * two + 64, r0:r0 + 16] = c[:, g].transpose(0, 2, 1)
        d["s5_Bblk"] = Bb
        d["s5_Cblk"] = Cb
        ct, sn = rope_tables(half == 1)
        d["rope_cos"] = ct
        d["rope_sin"] = sn
        d["consts"] = make_consts()
        out[half] = d
    return out


def make_in_map(inp, shared, core):
    b, half = core // 2, core % 2
    x = np.asarray(inp["x"][b], np.float32)
    cx = np.asarray(inp["ctx"][b], np.float32)
    if half == 1:
        x = x[::-1]
        cx = cx[::-1]
    m = dict(shared[half])
    m["hT0"] = np.ascontiguousarray(np.concatenate([cx, x], 0).T)
    c2 = np.stack([np.asarray(inp["c"][b], np.float32), np.asarray(inp["c_ctx"], np.float32)], -1)
    m["c2"] = np.ascontiguousarray(c2.reshape(8, 128, 2).transpose(1, 0, 2))
    return m


_NC_CACHE = {}


def kernel(**inputs):
    if "nc" not in _NC_CACHE:
        _NC_CACHE["nc"] = build()
    nc = _NC_CACHE["nc"]
    shared = _prep_shared(inputs)
    in_maps = [make_in_map(inputs, shared, core) for core in range(8)]
    res = run_bass_kernel_spmd(nc, in_maps, core_ids=list(range(8)))
    out = np.zeros((4, NLAT, D), np.float32)
    n = N_OUT_TILES_LAST * 512
    for core in range(8):
        b, half = core // 2, core % 2
        y = np.asarray(res.results[core]["y"]).T
        if half == 0:
            out[b, :n] = y
        else:
            out[b, NLAT - n:] = y[::-1]
    return out
```

```python
import contextlib
import math
import numpy as np
import concourse.bass as bass
import concourse.mybir as mybir
from concourse.bass_utils import run_bass_kernel_spmd

F32 = mybir.dt.float32
BF16 = mybir.dt.bfloat16
AF = mybir.ActivationFunctionType
ALU = mybir.AluOpType
AX = mybir.AxisListType


class Tile:
    def __init__(self, S, h, name, psum=False):
        self.S = S
        self.h = h
        self.name = name
        self.psum = psum
        self.lw = None
        self.rd = {}
        self.dsem = None

    def __getitem__(self, k):
        return self.h[k]


class Sched:
    SAME_ENGINE_SYNC = ("dve", "act", "pool")

    def __init__(self, nc):
        self.nc = nc
        self.E = {"pe": nc.tensor, "dve": nc.vector, "act": nc.scalar, "pool": nc.gpsimd, "sp": nc.sync}
        self.sems = {}
        for k in self.E:
            self.sems[k] = [nc.alloc_semaphore("sem_" + k), 0]
        self.waited = {k: {} for k in self.E}
        self.free_dsems = []
        self.n_dsem = 0
        self.stack = []
        self.ninstr = 0
        self.inflight = {k: [] for k in self.E}
        self.MAXOUT = 16

    def sb(self, stack, name, shape, dtype):
        self.nalloc = getattr(self, "nalloc", 0) + 1
        name = "%s_%d" % (name, self.nalloc)
        h = stack.enter_context(self.nc.sbuf_tensor(name, list(shape), dtype))
        t = Tile(self, h, name)
        if not hasattr(self, "tiles"):
            self.tiles = []
        self.tiles.append(t)
        return t

    def mark(self):
        return len(getattr(self, "tiles", []))

    def release_since(self, mk):
        self.release(self.tiles[mk:])
        del self.tiles[mk:]

    def ps(self, stack, name, shape, dtype=F32):
        h = stack.enter_context(self.nc.psum_tensor(name, list(shape), dtype))
        return Tile(self, h, name, psum=True)

    def _dsem(self, t):
        if t.dsem is None:
            if self.free_dsems:
                t.dsem = self.free_dsems.pop()
            else:
                key = "d%d" % self.n_dsem
                self.n_dsem += 1
                self.sems[key] = [self.nc.alloc_semaphore("sem_" + key), 0]
                t.dsem = key
        return t.dsem

    def release(self, tiles):
        for t in tiles:
            if t.dsem is not None:
                self.free_dsems.append(t.dsem)
                t.dsem = None

    def _wait(self, eng, key, val):
        if val <= 0:
            return
        w = self.waited[eng]
        if w.get(key, 0) >= val:
            return
        self.E[eng].wait_ge(self.sems[key][0], val)
        w[key] = val
        self.ninstr += 1

    def _deps(self, eng, reads, writes):
        deps = {}
        def add(d):
            if d is None:
                return
            k, v = d
            if k == eng and eng not in self.SAME_ENGINE_SYNC:
                return
            if deps.get(k, 0) < v:
                deps[k] = v
        for t in reads:
            add(t.lw)
        for t in writes:
            add(t.lw)
            for k, v in t.rd.items():
                add((k, v))
        for k, v in deps.items():
            self._wait(eng, k, v)

    def op(self, eng, fn, reads=(), writes=()):
        self._deps(eng, reads, writes)
        ins = fn(self.E[eng])
        s = self.sems[eng]
        s[1] += 1
        ins.then_inc(s[0], 1)
        self.ninstr += 1
        me = (eng, s[1])
        for t in reads:
            if t.rd.get(eng, 0) < s[1]:
                t.rd[eng] = s[1]
        for t in writes:
            t.lw = me
            t.rd = {}
        return ins

    def dma(self, q, out, in_, reads=(), writes=(), **kw):
        self._deps(q, reads, writes)
        tl = (list(writes) + list(reads))
        assert tl, "dma needs an sbuf tile for its semaphore"
        key = self._dsem(tl[0])
        self._throttle(q)
        ins = self.E[q].dma_start(out=out, in_=in_, **kw)
        s = self.sems[key]
        s[1] += 16
        ins.then_inc(s[0], 16)
        self.ninstr += 1
        me = (key, s[1])
        self.inflight[q].append(me)
        for t in reads:
            if t.rd.get(key, 0) < s[1]:
                t.rd[key] = s[1]
        for t in writes:
            t.lw = me
            t.rd = {}
        return ins

    def _throttle(self, q):
        fl = self.inflight[q]
        while len(fl) >= self.MAXOUT:
            k, v = fl.pop(0)
            self._wait(q, k, v)

    def dma_dram(self, q, out, in_, **kw):
        key = "dd_" + q
        if key not in self.sems:
            self.sems[key] = [self.nc.alloc_semaphore("sem_" + key), 0]
        self._throttle(q)
        ins = self.E[q].dma_start(out=out, in_=in_, **kw)
        s = self.sems[key]
        s[1] += 16
        ins.then_inc(s[0], 16)
        self.ninstr += 1
        self.inflight[q].append((key, s[1]))
        return ins

    def barrier(self):
        for eng in self.E:
            for key, (h, cnt) in self.sems.items():
                if key == eng and eng not in self.SAME_ENGINE_SYNC and eng != "sp":
                    continue
                self._wait(eng, key, cnt)

D = 1024
NCTX = 256
NLAT = 8192
T = NCTX + NLAT
KC = 8
TILES = [(0, 256)] + [(256 + 512 * i, 512) for i in range(16)]
N_OUT_TILES_LAST = 8
EPS = 1e-6
INC = 3984
C_Z, C_XBC, C_DT, C_DQ, C_DK, C_DV, C_GQ, C_GK, C_GV, C_U = 0, 512, 1280, 1296, 1808, 2320, 2832, 3344, 3472, 3600
FH = 2816
NV = 115
NR = 808
CI_ID, CI_BLK64, CI_ROT, CI_TRI0, CI_TRI1, CI_MN0, CI_MN1, CI_SEL, CI_ONES = range(9)
NCONST = 9


def make_consts():
    c = np.zeros((NCONST, 128, 128), np.float32)
    c[CI_ID] = np.eye(128)
    c[CI_BLK64, :64, :64] = 1.0
    c[CI_BLK64, 64:, 64:] = 1.0
    for hb in (0, 64):
        for q0 in (0, 32):
            for i in range(16):
                c[CI_ROT, hb + q0 + 16 + i, hb + q0 + i] = -1.0
                c[CI_ROT, hb + q0 + i, hb + q0 + 16 + i] = 1.0
    k = np.arange(128)
    c[CI_TRI0] = (k[:, None] <= k[None, :])
    c[CI_TRI1] = (k[:, None] >= k[None, :])
    c[CI_MN0] = np.where(k[:, None] <= k[None, :], 0.0, -1e30)
    c[CI_MN1] = np.where(k[:, None] >= k[None, :], 0.0, -1e30)
    c[CI_SEL, 64, :] = 1.0
    c[CI_ONES] = 1.0
    return c


def rope_tables(flip):
    n_rows = NLAT // 64
    rows = np.repeat(np.arange(n_rows, dtype=np.float32), 64)
    cols = np.tile(np.arange(64, dtype=np.float32), n_rows)
    inv = (10000.0 ** (-np.arange(16, dtype=np.float32) / 16)).astype(np.float32)
    ar = rows[:, None] * inv
    ac = cols[:, None] * inv
    ang = np.concatenate([ar, ar, ac, ac], -1)
    cos = np.cos(ang).astype(np.float32)
    sin = np.sin(ang).astype(np.float32)
    if flip:
        cos = cos[::-1]
        sin = sin[::-1]
    ct = np.ones((128, T), np.float32)
    st = np.zeros((128, T), np.float32)
    ct[:64, NCTX:] = cos.T
    ct[64:, NCTX:] = cos.T
    st[:64, NCTX:] = sin.T
    st[64:, NCTX:] = sin.T
    return ct, st


def dview(t, pattern, **kw):
    return t.ap().rearrange(pattern, **kw)


def build(n_layers=2, phases=None, dbg=(), last_tiles=N_OUT_TILES_LAST, force_full=False, dbg_in=()):
    nc = bass.Bass("TRN2", target_bir_lowering=False)
    S = Sched(nc)
    dbg = set(dbg)
    allph = phases is None

    def want(p):
        return allph or p in phases

    def dram(name, shape, dt, kind=None):
        if kind is None:
            kind = "ExternalOutput" if name in dbg else ("ExternalInput" if name in dbg_in else "Internal")
        return nc.dram_tensor(name, list(shape), dt, kind=kind)

    def din(name, shape, dt=F32):
        return nc.dram_tensor(name, list(shape), dt, kind="ExternalInput")

    hT_in = din("hT0", [D, T])
    c2 = din("c2", [128, KC, 2])
    w_mod = din("w_mod", [2, D, 6 * D])
    w_in = din("w_in", [2, D, INC])
    w_gate = din("w_gate", [2, 4, D, D])
    w_br = [din("w_br_ssd", [2, 512, D]), din("w_br_diff", [2, 512, D]), din("w_br_gqa", [2, 512, D]),
            din("w_br_s5", [2, 384, D])]
    BRK = [4, 4, 4, 3]
    w_out = din("w_out", [2, D, D])
    w_gu = din("ffn_w_gate_up", [2, D, 2 * FH])
    w_dn = din("ffn_w_down", [2, FH, D])
    glu_w = din("s5_glu_w", [2, 384, 768])
    vec128 = din("vec128", [2, 128, NV])
    rowvecs = din("rowvecs", [2, NR])
    rope_c = din("rope_cos", [128, T])
    rope_s = din("rope_sin", [128, T])
    consts = din("consts", [NCONST, 128, 128])
    s5_lr = din("s5_lr", [2, 2, 128, 12])
    s5_li = din("s5_li", [2, 2, 128, 12])
    s5_ldt = din("s5_ldt", [2, 2, 128, 12])
    s5_B = din("s5_Bblk", [2, 2, 12, 128, 128])
    s5_C = din("s5_Cblk", [2, 2, 12, 128, 128])
    y_out = nc.dram_tensor("y", [D, last_tiles * 512], F32, kind="ExternalOutput")

    hT = [hT_in, dram("hT1", [D, T], F32), dram("hT2", [D, T], F32)]
    xnT = dram("xnT", [128, KC, T], BF16)
    xbcT = dram("xbcT", [128, 6, T], F32)
    qkT = dram("qkT", [128, 13, T], BF16)
    uT = dram("uT", [128, 3, T], F32)
    zt = dram("zt", [T, 512], F32)
    vaug = dram("vaug", [T, 10, 65], BF16)
    dtt = dram("dtt", [T, 16], F32)
    yaT = dram("yaT", [128, 4, T], BF16)
    ybT = dram("ybT", [128, 4, T], BF16)
    ycT = dram("ycT", [128, 4, T], BF16)
    ydT = dram("ydT", [128, 3, T], BF16)
    xtok = dram("xtok", [T, 512], BF16)
    btok = dram("btok", [T, 128], BF16)
    bcT = dram("bcT", [2, 2, 64, T], BF16)
    yssd = dram("yssd", [T, 512], F32)
    yssd1 = dram("yssd1", [T, 512], F32)
    ys5 = dram("ys5", [128, 3, T], F32)
    modv_d = dram("modv_d", [2, 128, 48, 2], F32)
    wg_bf = dram("wg_bf", [4 * 8, 128, 8, 128], BF16)
    wb_bf = [dram("wb_bf%d" % i, [8, 128, BRK[i], 128], BF16) for i in range(4)]
    wo_bf = dram("wo_bf", [8, 128, 8, 128], BF16)
    wgu_bf = dram("wgu_bf", [44, 128, 8, 128], BF16)
    wdn_bf = dram("wdn_bf", [8, 128, 22, 128], BF16)

    with contextlib.ExitStack() as gst:
        cst_f = S.sb(gst, "cst_f", [128, NCONST, 128], F32)
        cst_b = S.sb(gst, "cst_b", [128, NCONST, 128], BF16)
        S.dma("sp", cst_f[:], consts.ap().rearrange("c p m -> p c m"), writes=[cst_f])
        S.op("dve", lambda e: e.tensor_copy(cst_b[:], cst_f[:]), reads=[cst_f], writes=[cst_b])
        psall = gst.enter_context(nc.psum_tensor("psall", [128, 4096], F32))
        PS = [Tile(S, psall[:, i * 512:(i + 1) * 512], "ps%d" % i, psum=True) for i in range(8)]

        def cF(i):
            return cst_f[:, i, :]

        def cB(i):
            return cst_b[:, i, :]

        for layer in range(n_layers):
            last = (layer == n_layers - 1) and not force_full
            out_tiles = TILES[1:1 + last_tiles] if last else TILES
            h_src = hT[layer]
            h_dst = y_out if last else hT[layer + 1]
            lam_init = 0.8 - 0.6 * math.exp(-0.3 * layer)

            if want("W"):
                qs = ["pool"]
                n = 0
                for i in range(4):
                    for j in range(8):
                        S.dma_dram("pool", wg_bf.ap()[i * 8 + j],
                                   w_gate.ap()[layer, i, :, j * 128:(j + 1) * 128].rearrange("(kc p) m -> p kc m", p=128))
                    for j in range(8):
                        S.dma_dram("pool", wb_bf[i].ap()[j],
                                   w_br[i].ap()[layer, :, j * 128:(j + 1) * 128].rearrange("(kc p) m -> p kc m", p=128))
                for j in range(8):
                    S.dma_dram("pool", wo_bf.ap()[j],
                               w_out.ap()[layer, :, j * 128:(j + 1) * 128].rearrange("(kc p) m -> p kc m", p=128))
                    S.dma_dram("pool", wdn_bf.ap()[j],
                               w_dn.ap()[layer, :, j * 128:(j + 1) * 128].rearrange("(kc p) m -> p kc m", p=128))
                for j in range(44):
                    S.dma_dram("pool", wgu_bf.ap()[j],
                               w_gu.ap()[layer, :, j * 128:(j + 1) * 128].rearrange("(kc p) m -> p kc m", p=128))

            lmk = S.mark()
            with contextlib.ExitStack() as lst:
                v128 = S.sb(lst, "v128", [128, NV], F32)
                rows = S.sb(lst, "rows", [128, NR], F32)
                modv = S.sb(lst, "modv", [128, 48, 2], F32)
                A1 = S.sb(lst, "A1", [128, 8, 2], F32)
                A2 = S.sb(lst, "A2", [128, 8, 2], F32)
                lamc = S.sb(lst, "lamc", [128, 4], F32)
                subg = S.sb(lst, "subg", [128, 2], F32)
                aneg = S.sb(lst, "aneg", [128, 16], F32)
                S.dma("sp", v128[:], vec128.ap()[layer], writes=[v128])
                S.dma("sp", rows[:], rowvecs.ap()[layer].partition_broadcast(128), writes=[rows])
                if want("P"):
                    with contextlib.ExitStack() as st:
                        sc = S.sb(st, "sc", [128, KC, 2], F32)
                        S.dma("sp", sc[:], c2.ap(), writes=[sc])
                        S.op("act", lambda e: e.activation(sc[:], sc[:], AF.Silu), reads=[sc], writes=[sc])
                        wm = [S.sb(st, "wm%d" % i, [128, KC, 512], F32) for i in range(2)]
                        pm = PS[0]
                        for blk in range(12):
                            w = wm[blk % 2]
                            S.dma("sp", w[:], w_mod.ap()[layer, :, blk * 512:(blk + 1) * 512].rearrange("(kc p) n -> p kc n", p=128),
                                  writes=[w])
                            for jj in range(4):
                                j = blk * 4 + jj
                                for kc in range(KC):
                                    S.op("pe", lambda e, j=j, jj=jj, kc=kc, w=w: e.matmul(
                                        pm[:, j * 2:(j + 1) * 2], w[:, kc, jj * 128:(jj + 1) * 128], sc[:, kc, :],
                                        start=(kc == 0), stop=(kc == KC - 1)), reads=[w, sc], writes=[pm])
                        S.op("dve", lambda e: e.tensor_tensor(
                            modv[:], pm[:, 0:96].rearrange("p (j s) -> p j s", s=2),
                            v128[:, 16:64].unsqueeze(2).broadcast_to([128, 48, 2]), ALU.add),
                            reads=[pm, v128], writes=[modv])
                        S.dma("pool", modv_d.ap()[layer], modv[:], reads=[modv])
                else:
                    S.dma("sp", modv[:], modv_d.ap()[layer], writes=[modv])
                for (A, gcol, scj) in ((A1, 0, 8), (A2, 8, 32)):
                    S.op("dve", lambda e, A=A, scj=scj: e.tensor_scalar(A[:], modv[:, scj:scj + 8, :], 1.0, None, ALU.add),
                         reads=[modv], writes=[A])
                    S.op("dve", lambda e, A=A, gcol=gcol: e.tensor_tensor(
                        A[:], A[:], v128[:, gcol:gcol + 8].unsqueeze(2).broadcast_to([128, 8, 2]), ALU.mult),
                        reads=[A, v128], writes=[A])
                with contextlib.ExitStack() as st:
                    tmp = S.sb(st, "lamtmp", [128, 128], F32)
                    red = S.sb(st, "lamred", [128, 2], F32)
                    R0 = 552
                    S.op("dve", lambda e: e.tensor_tensor(tmp[:, 0:64], rows[:, R0:R0 + 64], rows[:, R0 + 64:R0 + 128], ALU.mult),
                         reads=[rows], writes=[tmp])
                    S.op("dve", lambda e: e.tensor_tensor(tmp[:, 64:128], rows[:, R0 + 128:R0 + 192], rows[:, R0 + 192:R0 + 256], ALU.mult),
                         reads=[rows, tmp], writes=[tmp])
                    S.op("dve", lambda e: e.reduce_sum(red[:], tmp[:].rearrange("p (a b) -> p a b", a=2), AX.X),
                         reads=[tmp], writes=[red])
                    S.op("act", lambda e: e.activation(red[:], red[:], AF.Exp), reads=[red], writes=[red])
                    S.op("dve", lambda e: e.tensor_tensor(lamc[:, 0:1], red[:, 1:2], red[:, 0:1], ALU.subtract),
                         reads=[red], writes=[lamc])
                    S.op("dve", lambda e: e.tensor_scalar(lamc[:, 0:1], lamc[:, 0:1], -lam_init, None, ALU.add),
                         reads=[lamc], writes=[lamc])
                    S.op("dve", lambda e: e.tensor_scalar(subg[:], v128[:, 68:70], 1.0 - lam_init, None, ALU.mult),
                         reads=[v128], writes=[subg])
                    S.op("act", lambda e: e.activation(aneg[:], rows[:, 0:16], AF.Exp), reads=[rows], writes=[aneg])
                    S.op("dve", lambda e: e.tensor_scalar(aneg[:], aneg[:], -1.0, None, ALU.mult), reads=[aneg], writes=[aneg])
                    S.barrier()

                for pname, pfn in (("1", phase1), ("2", phase2), ("3", phase3), ("4", phase4), ("5", phase5)):
                    if want(pname):
                        mk = S.mark()
                        pfn(nc, S, locals())
                        S.barrier()
                        S.release_since(mk)
                S.barrier()
            S.release_since(lmk)
        S.barrier()
    print("ninstr", S.ninstr, "dsems", S.n_dsem)
    return nc


def phase1(nc, S, G):
    PS, layer, v128, modv, A1 = G["PS"], G["layer"], G["v128"], G["modv"], G["A1"]
    cF, cB = G["cF"], G["cB"]
    h_src = G["h_src"]
    with contextlib.ExitStack() as st:
        win = S.sb(st, "win", [128, KC, INC], BF16)
        for c0 in range(0, INC, 512):
            c1 = min(INC, c0 + 512)
            S.dma("pool", win[:, :, c0:c1],
                  G["w_in"].ap()[layer, :, c0:c1].rearrange("(kc p) n -> p kc n", p=128), writes=[win])
        hts = [S.sb(st, "ht%d" % i, [128, KC, 512], F32) for i in range(2)]
        sq = S.sb(st, "sq", [128, KC, 512], BF16)
        rstd = S.sb(st, "rstd", [128, 512], F32)
        xns = [S.sb(st, "xn%d" % i, [128, KC, 512], BF16) for i in range(2)]
        cos_t = [S.sb(st, "cos%d" % i, [128, 512], F32) for i in range(2)]
        sin_t = [S.sb(st, "sin%d" % i, [128, 512], F32) for i in range(2)]
        stg = [S.sb(st, "stg%d" % i, [128, 512], F32) for i in range(4)]
        qsq = [S.sb(st, "qsq%d" % i, [128, 512], BF16) for i in range(2)]
        qy = [S.sb(st, "qy%d" % i, [128, 512], BF16) for i in range(2)]
        qr = [S.sb(st, "qr%d" % i, [128, 512], F32) for i in range(2)]
        qt1 = [S.sb(st, "qt1%d" % i, [128, 512], F32) for i in range(2)]
        qt2 = [S.sb(st, "qt2%d" % i, [128, 512], F32) for i in range(2)]
        qo = [S.sb(st, "qo%d" % i, [128, 512], BF16) for i in range(2)]
        vst = [S.sb(st, "vst%d" % i, [128, 10, 65], BF16) for i in range(2)]
        dst = [S.sb(st, "dst%d" % i, [128, 16], F32) for i in range(2)]
        for v in vst:
            S.op("pool", lambda e, v=v: e.memset(v[:], 1.0), writes=[v])
        nstg = 0
        nq = 0
        own_set = set(G["out_tiles"])
        for ti, (t0, W) in enumerate(TILES):
            s = 1 if t0 < NCTX else 0
            own = (t0, W) in own_set
            ht = hts[ti % 2]
            xn = xns[ti % 2]
            ct, sn = cos_t[ti % 2], sin_t[ti % 2]
            S.dma("sp", ht[:, :, 0:W], h_src.ap().rearrange("(kc p) t -> p kc t", p=128)[:, :, t0:t0 + W], writes=[ht])
            S.dma("sp", ct[:, 0:W], G["rope_c"].ap()[:, t0:t0 + W], writes=[ct])
            S.dma("sp", sn[:, 0:W], G["rope_s"].ap()[:, t0:t0 + W], writes=[sn])
            S.op("act", lambda e: e.activation(sq[:, :, 0:W], ht[:, :, 0:W], AF.Square), reads=[ht], writes=[sq])
            for kc in range(KC):
                S.op("pe", lambda e, kc=kc: e.matmul(PS[0][:, 0:W], cB(CI_ONES), sq[:, kc, 0:W], start=(kc == 0), stop=(kc == KC - 1)),
                     reads=[sq], writes=[PS[0]])
            S.op("act", lambda e: e.activation(rstd[:, 0:W], PS[0][:, 0:W], AF.Sqrt, bias=EPS, scale=1.0 / D), reads=[PS[0]], writes=[rstd])
            S.op("dve", lambda e: e.reciprocal(rstd[:, 0:W], rstd[:, 0:W]), reads=[rstd], writes=[rstd])
            S.op("dve", lambda e: e.tensor_tensor(ht[:, :, 0:W], ht[:, :, 0:W], rstd[:, 0:W].unsqueeze(1).broadcast_to([128, KC, W]), ALU.mult),
                 reads=[ht, rstd], writes=[ht])
            for kc in range(KC):
                S.op("act", lambda e, kc=kc: e.activation(xn[:, kc, 0:W], ht[:, kc, 0:W], AF.Identity,
                                                          bias=modv[:, kc, s:s + 1], scale=A1[:, kc, s:s + 1]),
                     reads=[ht, modv, A1], writes=[xn])
            if own:
                S.dma("pool", G["xnT"].ap()[:, :, t0:t0 + W], xn[:, :, 0:W], reads=[xn])

            def fm_group(col0, pst):
                for kc in range(KC):
                    S.op("pe", lambda e, kc=kc: e.matmul(pst[:, 0:W], win[:, kc, col0:col0 + 128], xn[:, kc, 0:W],
                                                         start=(kc == 0), stop=(kc == KC - 1)), reads=[win, xn], writes=[pst])
            ng = 0
            for c in range(6):
                pst = PS[1 + ng % 2]; ng += 1
                fm_group(C_XBC + 128 * c, pst)
                sg = stg[nstg % 4]; nstg += 1
                S.op("act", lambda e, sg=sg, pst=pst: e.activation(sg[:, 0:W], pst[:, 0:W], AF.Copy), reads=[pst], writes=[sg])
                S.dma("pool", G["xbcT"].ap()[:, c, t0:t0 + W], sg[:, 0:W], reads=[sg])
            for c in range(3):
                pst = PS[1 + ng % 2]; ng += 1
                fm_group(C_U + 128 * c, pst)
                sg = stg[nstg % 4]; nstg += 1
                S.op("dve", lambda e, sg=sg, pst=pst: e.tensor_copy(sg[:, 0:W], pst[:, 0:W]), reads=[pst], writes=[sg])
                S.dma("pool", G["uT"].ap()[:, c, t0:t0 + W], sg[:, 0:W], reads=[sg])
            for c in range(13):
                if not own and (c < 4 or 8 <= c < 12):
                    continue
                if c < 4:
                    col0, gcol = C_DQ + 128 * c, 64
                elif c < 8:
                    col0, gcol = C_DK + 128 * (c - 4), 65
                elif c < 12:
                    col0, gcol = C_GQ + 128 * (c - 8), 66
                else:
                    col0, gcol = C_GK, 67
                pst = PS[1 + ng % 2]; ng += 1
                fm_group(col0, pst)
                b = nq % 2; nq += 1
                a_sq, a_y, a_r, a_t1, a_t2, a_o = qsq[b], qy[b], qr[b], qt1[b], qt2[b], qo[b]
                S.op("act", lambda e: e.activation(a_sq[:, 0:W], pst[:, 0:W], AF.Square), reads=[pst], writes=[a_sq])
                S.op("act", lambda e: e.activation(a_y[:, 0:W], pst[:, 0:W], AF.Identity, scale=v128[:, gcol:gcol + 1]),
                     reads=[pst, v128], writes=[a_y])
                S.op("pe", lambda e: e.matmul(PS[3][:, 0:W], cB(CI_BLK64), a_sq[:, 0:W], start=True, stop=True), reads=[a_sq], writes=[PS[3]])
                S.op("pe", lambda e: e.matmul(PS[4][:, 0:W], cB(CI_ROT), a_y[:, 0:W], start=True, stop=True), reads=[a_y], writes=[PS[4]])
                S.op("act", lambda e: e.activation(a_r[:, 0:W], PS[3][:, 0:W], AF.Sqrt, bias=EPS, scale=1.0 / 64), reads=[PS[3]], writes=[a_r])
                S.op("dve", lambda e: e.reciprocal(a_r[:, 0:W], a_r[:, 0:W]), reads=[a_r], writes=[a_r])
                S.op("dve", lambda e: e.tensor_tensor(a_t1[:, 0:W], a_y[:, 0:W], ct[:, 0:W], ALU.mult), reads=[a_y, ct], writes=[a_t1])
                S.op("dve", lambda e: e.tensor_tensor(a_t2[:, 0:W], PS[4][:, 0:W], sn[:, 0:W], ALU.mult), reads=[PS[4], sn], writes=[a_t2])
                S.op("dve", lambda e: e.tensor_tensor(a_t1[:, 0:W], a_t1[:, 0:W], a_t2[:, 0:W], ALU.add), reads=[a_t1, a_t2], writes=[a_t1])
                S.op("dve", lambda e: e.tensor_tensor(a_o[:, 0:W], a_t1[:, 0:W], a_r[:, 0:W], ALU.mult), reads=[a_t1, a_r], writes=[a_o])
                S.dma("pool", G["qkT"].ap()[:, c, t0:t0 + W], a_o[:, 0:W], reads=[a_o])

            for blk in range(W // 128):
                bs = slice(blk * 128, (blk + 1) * 128)
                r0 = t0 + blk * 128
                vs_, ds_ = vst[blk % 2], dst[blk % 2]
                for kc in range(KC):
                    fl = dict(start=(kc == 0), stop=(kc == KC - 1))
                    if own:
                        S.op("pe", lambda e: e.matmul(PS[5][:, 0:512], xn[:, kc, bs], win[:, kc, C_Z:C_Z + 512], **fl),
                             reads=[win, xn], writes=[PS[5]])
                    S.op("pe", lambda e: e.matmul(PS[6][:, 0:512], xn[:, kc, bs], win[:, kc, C_DV:C_DV + 512], **fl),
                         reads=[win, xn], writes=[PS[6]])
                    S.op("pe", lambda e: e.matmul(PS[7][:, 0:128], xn[:, kc, bs], win[:, kc, C_GV:C_GV + 128], **fl),
                         reads=[win, xn], writes=[PS[7]])
                    S.op("pe", lambda e: e.matmul(PS[0][:, 0:16], xn[:, kc, bs], win[:, kc, C_DT:C_DT + 16], **fl),
                         reads=[win, xn], writes=[PS[0]])
                if own:
                    sg = stg[nstg % 4]; nstg += 1
                    S.op("act", lambda e, sg=sg: e.activation(sg[:, :], PS[5][:, :], AF.Copy), reads=[PS[5]], writes=[sg])
                    S.dma("pool", G["zt"].ap()[r0:r0 + 128, :], sg[:, :], reads=[sg])
                S.op("dve", lambda e: e.tensor_copy(vs_[:, 0:8, 0:64], PS[6][:, :].rearrange("p (g c) -> p g c", c=64)),
                     reads=[PS[6]], writes=[vs_])
                S.op("dve", lambda e: e.tensor_copy(vs_[:, 8:10, 0:64], PS[7][:, 0:128].rearrange("p (g c) -> p g c", c=64)),
                     reads=[PS[7]], writes=[vs_])
                S.op("act", lambda e: e.activation(ds_[:, :], PS[0][:, 0:16], AF.Copy), reads=[PS[0]], writes=[ds_])
                S.dma("pool", G["vaug"].ap()[r0:r0 + 128], vs_[:], reads=[vs_])
                S.dma("pool", G["dtt"].ap()[r0:r0 + 128, :], ds_[:, :], reads=[ds_])


def phase2(nc, S, G):
    PS, layer, v128 = G["PS"], G["layer"], G["v128"]
    cF, cB, lamc, subg = G["cF"], G["cB"], G["lamc"], G["subg"]
    out_tiles = G["out_tiles"]
    qkT, vaug = G["qkT"], G["vaug"]
    NKB = T // 128
    with contextlib.ExitStack() as st:
        kTs = [S.sb(st, "kT%d" % i, [128, T], BF16) for i in range(2)]
        vts = [S.sb(st, "vt%d" % i, [128, NKB, 2, 65], BF16) for i in range(2)]
        qts = [S.sb(st, "qt%d" % i, [128, 2, 512], BF16) for i in range(2)]
        pTw = [S.sb(st, "pTw%d" % i, [128, 2, 512], BF16) for i in range(2)]
        psall = G["psall"]
        xlo = [S.sb(st, "xlo%d" % i, [65, 512], F32) for i in range(2)]
        rinv = [S.sb(st, "rinv%d" % i, [64, 512], F32) for i in range(2)]
        olo = [S.sb(st, "olo%d" % i, [64, 512], F32) for i in range(2)]
        ohi = [S.sb(st, "ohi%d" % i, [64, 512], F32) for i in range(2)]
        dlo = S.sb(st, "dlo", [64, 512], F32)
        dhi = S.sb(st, "dhi", [64, 512], F32)
        sql = S.sb(st, "sql", [64, 512], BF16)
        sqh = S.sb(st, "sqh", [64, 512], BF16)
        rs = S.sb(st, "rs", [64, 512], F32)
        yo = [S.sb(st, "yo%d" % i, [64, 512], BF16) for i in range(4)]
        nrot = 0
        nyo = 0
        nq = 0
        for grp in range(6):
            kT, vt = kTs[grp % 2], vts[grp % 2]
            is_diff = grp < 4
            if is_diff:
                h = grp
                S.dma("sp", kT[:, :], qkT.ap()[:, 4 + h, :], writes=[kT])
                for k0 in range(0, NKB, 11):
                    S.dma("sp", vt[:, k0:k0 + 11, :, :],
                          vaug.ap()[k0 * 128:(k0 + 11) * 128, 2 * h:2 * h + 2, :].rearrange("(kb p) g c -> p kb g c", p=128), writes=[vt])
                units = [(0, 0), (0, 1)]
            else:
                n = grp - 4
                S.dma("sp", kT[0:64, :], qkT.ap()[64 * n:64 * n + 64, 12, :], writes=[kT])
                S.dma("sp", kT[64:128, :], qkT.ap()[64 * n:64 * n + 64, 12, :], writes=[kT])
                for k0 in range(0, NKB, 11):
                    S.dma("sp", vt[:, k0:k0 + 11, 0:1, :],
                          vaug.ap()[k0 * 128:(k0 + 11) * 128, 8 + n:9 + n, :].rearrange("(kb p) g c -> p kb g c", p=128), writes=[vt])
                units = [(0, 0), (0, 1), (1, 0), (1, 1)]
            for (t0, W) in out_tiles:
                qt = qts[nq % 2]; nq += 1
                if is_diff:
                    S.dma("sp", qt[:, 0, 0:W], qkT.ap()[:, h, t0:t0 + W], writes=[qt])
                else:
                    S.dma("sp", qt[:, :, 0:W], qkT.ap()[:, 8 + 2 * n:10 + 2 * n, t0:t0 + W], writes=[qt])
                kbs = [0, 1] if t0 < NCTX else list(range(NKB))
                subs = []
                for kb in kbs:
                    for u0 in range(0, len(units), 2):
                        subs.append((kb, [u0, u0 + 1]))

                def emit_s(i):
                    kb, us = subs[i]
                    r = i % 2
                    pw = pTw[r]
                    for k, ui in enumerate(us):
                        qc, half = units[ui]
                        hs = slice(64 * half, 64 * half + 64)
                        pS = PS[2 * r + k]
                        S.op("pe", lambda e: e.matmul(pS[:, 0:W], kT[hs, kb * 128:(kb + 1) * 128], qt[hs, qc, 0:W], start=True, stop=True),
                             reads=[kT, qt], writes=[pS])
                    if W == 512:
                        S.op("act", lambda e: e.activation(pw[:, :, :].rearrange("p a w -> p (a w)"),
                                                           psall[:, 2 * r * 512:(2 * r + 2) * 512], AF.Exp, scale=0.125),
                             reads=[PS[2 * r], PS[2 * r + 1]], writes=[pw])
                    else:
                        for k in range(2):
                            S.op("act", lambda e: e.activation(pw[:, k, 0:W], PS[2 * r + k][:, 0:W], AF.Exp, scale=0.125),
                                 reads=[PS[2 * r + k]], writes=[pw])

                def emit_pv(i):
                    kb, us = subs[i]
                    pw = pTw[i % 2]
                    fl = dict(start=(kb == kbs[0]), stop=(kb == kbs[-1]))
                    for k, ui in enumerate(us):
                        if is_diff:
                            a0, a1 = PS[4 + 2 * ui], PS[5 + 2 * ui]
                            S.op("pe", lambda e: e.matmul(a0[0:65, 0:W], vt[:, kb, 0, 0:65], pw[:, k, 0:W], **fl), reads=[vt, pw], writes=[a0])
                            S.op("pe", lambda e: e.matmul(a1[0:64, 0:W], vt[:, kb, 1, 0:64], pw[:, k, 0:W], **fl), reads=[vt, pw], writes=[a1])
                        else:
                            a0 = PS[4 + ui]
                            S.op("pe", lambda e: e.matmul(a0[0:65, 0:W], vt[:, kb, 0, 0:65], pw[:, k, 0:W], **fl), reads=[vt, pw], writes=[a0])

                for i in range(len(subs) + 1):
                    if i < len(subs):
                        emit_s(i)
                    if i >= 1:
                        emit_pv(i - 1)
                if is_diff:
                    for m in range(2):
                        a0, a1 = PS[4 + 2 * m], PS[5 + 2 * m]
                        S.op("act", lambda e: e.activation(xlo[m][0:65, 0:W], a0[0:65, 0:W], AF.Copy), reads=[a0], writes=[xlo[m]])
                        S.op("pe", lambda e: e.matmul(PS[0][0:64, 0:W], cF(CI_SEL)[0:65, 0:64], xlo[m][0:65, 0:W], start=True, stop=True),
                             reads=[xlo[m]], writes=[PS[0]])
                        S.op("dve", lambda e: e.reciprocal(rinv[m][:, 0:W], PS[0][0:64, 0:W]), reads=[PS[0]], writes=[rinv[m]])
                        S.op("dve", lambda e: e.tensor_tensor(olo[m][:, 0:W], xlo[m][0:64, 0:W], rinv[m][:, 0:W], ALU.mult),
                             reads=[xlo[m], rinv[m]], writes=[olo[m]])
                        S.op("dve", lambda e: e.tensor_tensor(ohi[m][:, 0:W], a1[0:64, 0:W], rinv[m][:, 0:W], ALU.mult),
                             reads=[a1, rinv[m]], writes=[ohi[m]])
                    S.op("dve", lambda e: e.scalar_tensor_tensor(dlo[:, 0:W], olo[1][:, 0:W], lamc[0:64, 0:1], olo[0][:, 0:W], ALU.mult, ALU.add),
                         reads=[olo[0], olo[1], lamc], writes=[dlo])
                    S.op("dve", lambda e: e.scalar_tensor_tensor(dhi[:, 0:W], ohi[1][:, 0:W], lamc[0:64, 0:1], ohi[0][:, 0:W], ALU.mult, ALU.add),
                         reads=[ohi[0], ohi[1], lamc], writes=[dhi])
                    S.op("act", lambda e: e.activation(sql[:, 0:W], dlo[:, 0:W], AF.Square), reads=[dlo], writes=[sql])
                    S.op("act", lambda e: e.activation(sqh[:, 0:W], dhi[:, 0:W], AF.Square), reads=[dhi], writes=[sqh])
                    S.op("pe", lambda e: e.matmul(PS[0][0:64, 0:W], cB(CI_ONES)[0:64, 0:64], sql[:, 0:W], start=True, stop=False),
                         reads=[sql], writes=[PS[0]])
                    S.op("pe", lambda e: e.matmul(PS[0][0:64, 0:W], cB(CI_ONES)[0:64, 0:64], sqh[:, 0:W], start=False, stop=True),
                         reads=[sqh], writes=[PS[0]])
                    S.op("act", lambda e: e.activation(rs[:, 0:W], PS[0][0:64, 0:W], AF.Sqrt, bias=EPS, scale=1.0 / 128), reads=[PS[0]], writes=[rs])
                    S.op("dve", lambda e: e.reciprocal(rs[:, 0:W], rs[:, 0:W]), reads=[rs], writes=[rs])
                    for (dd, col, p0) in ((dlo, 0, 0), (dhi, 1, 64)):
                        y = yo[nyo % 4]; nyo += 1
                        S.op("dve", lambda e: e.scalar_tensor_tensor(y[:, 0:W], dd[:, 0:W], subg[0:64, col:col + 1], rs[:, 0:W], ALU.mult, ALU.mult),
                             reads=[dd, subg, rs], writes=[y])
                        S.dma("pool", G["ybT"].ap()[p0:p0 + 64, h, t0:t0 + W], y[:, 0:W], reads=[y])
                else:
                    for j in range(4):
                        head = 4 * n + j
                        a0 = PS[4 + j]
                        m = j % 2
                        S.op("act", lambda e: e.activation(xlo[m][0:65, 0:W], a0[0:65, 0:W], AF.Copy), reads=[a0], writes=[xlo[m]])
                        S.op("pe", lambda e: e.matmul(PS[0][0:64, 0:W], cF(CI_SEL)[0:65, 0:64], xlo[m][0:65, 0:W], start=True, stop=True),
                             reads=[xlo[m]], writes=[PS[0]])
                        S.op("dve", lambda e: e.reciprocal(rinv[m][:, 0:W], PS[0][0:64, 0:W]), reads=[PS[0]], writes=[rinv[m]])
                        y = yo[nyo % 4]; nyo += 1
                        S.op("dve", lambda e: e.tensor_tensor(y[:, 0:W], xlo[m][0:64, 0:W], rinv[m][:, 0:W], ALU.mult),
                             reads=[xlo[m], rinv[m]], writes=[y])
                        p0 = 64 * (head % 2)
                        S.dma("pool", G["ycT"].ap()[p0:p0 + 64, head // 2, t0:t0 + W], y[:, 0:W], reads=[y])


def phase3(nc, S, G):
    PS, layer, v128, rows, aneg = G["PS"], G["layer"], G["v128"], G["rows"], G["aneg"]
    cF, cB = G["cF"], G["cB"]
    xbcT, bcT, xtok, btok, yssd = G["xbcT"], G["bcT"], G["xtok"], G["btok"], G["yssd"]
    NCH = T // 128
    with contextlib.ExitStack() as st:
        xin = [S.sb(st, "c_xin%d" % i, [128, 6, 516], F32) for i in range(2)]
        cv = S.sb(st, "c_cv", [128, 6, 512], F32)
        sl = [S.sb(st, "c_sl%d" % i, [128, 6, 512], BF16) for i in range(2)]
        xtk = [S.sb(st, "c_xtk%d" % i, [128, 640], BF16) for i in range(2)]
        psT = PS[7][:, :].bitcast(BF16)
        nb = 0
        for ti, (t0, W) in enumerate(TILES):
            seg0, seg1 = (0, NCTX) if t0 < NCTX else (NCTX, T)
            xi, sli = xin[ti % 2], sl[ti % 2]
            S.op("pool", lambda e: e.memset(xi[:, :, 0:2], 0.0), writes=[xi])
            S.op("pool", lambda e: e.memset(xi[:, :, W + 2:W + 4], 0.0), writes=[xi])
            lo = t0 - 2 if t0 > seg0 else t0
            hi = t0 + W + 2 if t0 + W < seg1 else t0 + W
            S.dma("sp", xi[:, :, lo - (t0 - 2):hi - (t0 - 2)], xbcT.ap()[:, :, lo:hi], writes=[xi])
            for c in range(6):
                wc = 76 + c * 5
                S.op("dve", lambda e: e.tensor_scalar(cv[:, c, 0:W], xi[:, c, 0:W], v128[:, wc:wc + 1], v128[:, 70 + c:71 + c], ALU.mult, ALU.add),
                     reads=[xi, v128], writes=[cv])
                for k in range(1, 5):
                    S.op("dve", lambda e: e.scalar_tensor_tensor(cv[:, c, 0:W], xi[:, c, k:k + W], v128[:, wc + k:wc + k + 1], cv[:, c, 0:W], ALU.mult, ALU.add),
                         reads=[xi, v128, cv], writes=[cv])
            S.op("act", lambda e: e.activation(sli[:, :, 0:W], cv[:, :, 0:W], AF.Silu), reads=[cv], writes=[sli])
            S.dma("pool", bcT.ap()[0].rearrange("g n t -> (g n) t")[:, t0:t0 + W], sli[:, 4, 0:W], reads=[sli])
            S.dma("pool", bcT.ap()[1].rearrange("g n t -> (g n) t")[:, t0:t0 + W], sli[:, 5, 0:W], reads=[sli])
            for blk in range(W // 128):
                r0 = t0 + blk * 128
                xt_ = xtk[nb % 2]; nb += 1
                for c in range(5):
                    S.op("pe", lambda e: e.transpose(psT[:, c * 128:(c + 1) * 128], sli[:, c, blk * 128:(blk + 1) * 128], cB(CI_ID)),
                         reads=[sli], writes=[PS[7]])
                S.op("act", lambda e: e.activation(xt_[:, :], psT[:, 0:640], AF.Copy), reads=[PS[7]], writes=[xt_])
                S.dma("pool", xtok.ap()[r0:r0 + 128, :], xt_[:, 0:512], reads=[xt_])
                S.dma("pool", btok.ap()[r0:r0 + 128, :], xt_[:, 512:640], reads=[xt_])
    S.barrier()
    yssd1 = G["yssd1"]
    with contextlib.ExitStack() as st:
        own_ck = set()
        for (t0_, W_) in G["out_tiles"]:
            own_ck.update(range(t0_ // 128, (t0_ + W_) // 128))
        last_own = max(own_ck)
        Dd = []
        for d in range(2):
            B = {}
            B["xk"] = [S.sb(st, "s_xk%d_%d" % (d, i), [128, 512], BF16) for i in range(2)]
            B["bk"] = [S.sb(st, "s_bk%d_%d" % (d, i), [128, 128], BF16) for i in range(2)]
            B["bct"] = [S.sb(st, "s_bct%d_%d" % (d, i), [64, 4, 128], BF16) for i in range(2)]
            B["dr"] = [S.sb(st, "s_dr%d_%d" % (d, i), [128, 16], F32) for i in range(2)]
            B["yv"] = [S.sb(st, "s_yv%d_%d" % (d, i), [128, 512], F32) for i in range(2)]
            for nm, shp, dt_ in (("av", [128, 16], F32), ("ab", [128, 8, 128], F32), ("ac", [128, 8], F32), ("ea", [128, 8], F32),
                                 ("cdec", [128, 8], F32), ("arg", [128, 8, 128], F32), ("dec", [128, 8, 128], F32),
                                 ("MT", [128, 8, 128], BF16), ("Bw", [128, 8, 64], BF16), ("xdt", [128, 8, 64], BF16),
                                 ("tmp", [128, 512], F32), ("h32", [64, 8, 64], F32), ("hb", [64, 8, 64], BF16), ("dte8", [128, 16], F32)):
                B[nm] = S.sb(st, "s_%s%d" % (nm, d), shp, dt_)
            B["n"] = 0
            B["pA"], B["pB"], B["pC"], B["pD"] = PS[4 * d], PS[4 * d + 1], PS[4 * d + 2], PS[4 * d + 3]
            S.op("pool", lambda e: e.memset(B["h32"][:], 0.0), writes=[B["h32"]])
            S.op("pool", lambda e: e.memset(B["hb"][:], 0.0), writes=[B["hb"]])
            Dd.append(B)
        orders = [[0, 1] + list(range(2, last_own + 1)), [1, 0] + list(range(NCH - 1, 1, -1))]

        def emit_chunk(d, ck):
            B = Dd[d]
            last_i = 127 if d == 0 else 0
            tri = cF(CI_TRI0 if d == 0 else CI_TRI1)
            mn = cF(CI_MN0 if d == 0 else CI_MN1)
            cols = slice(d * 8, d * 8 + 8)
            pA, pB, pC, pD = B["pA"], B["pB"], B["pC"], B["pD"]
            av, ab, ac, ea, cdec, arg, dec = B["av"], B["ab"], B["ac"], B["ea"], B["cdec"], B["arg"], B["dec"]
            MT, Bw, xdt, tmp, h32, hb, dte8 = B["MT"], B["Bw"], B["xdt"], B["tmp"], B["h32"], B["hb"], B["dte8"]
            c0 = ck * 128
            b = B["n"] % 2
            B["n"] += 1
            xk, bk, bct, dt = B["xk"][b], B["bk"][b], B["bct"][b], B["dr"][b]
            full = (d == 0) or (ck in own_ck)
            S.dma("sp", xk[:, :], xtok.ap()[c0:c0 + 128, :], writes=[xk])
            S.dma("sp", bk[:, :], btok.ap()[c0:c0 + 128, :], writes=[bk])
            if full:
                S.dma("sp", bct[:, :, :], bcT.ap()[:, :, :, c0:c0 + 128].rearrange("k g n t -> n (k g) t"), writes=[bct])
            S.dma("sp", dt[:, :], G["dtt"].ap()[c0:c0 + 128, :], writes=[dt])
            S.op("dve", lambda e: e.tensor_tensor(dt[:, :], dt[:, :], rows[:, 16:32], ALU.add), reads=[dt, rows], writes=[dt])
            S.op("act", lambda e: e.activation(dt[:, :], dt[:, :], AF.Exp), reads=[dt], writes=[dt])
            S.op("act", lambda e: e.activation(dt[:, :], dt[:, :], AF.Ln, bias=1.0, scale=1.0), reads=[dt], writes=[dt])
            S.op("dve", lambda e: e.tensor_tensor(av[:, :], dt[:, :], aneg[:, :], ALU.mult), reads=[dt, aneg], writes=[av])
            yield
            if full:
                S.op("pe", lambda e: e.matmul(pC[:, 0:8], tri, av[:, cols], start=True, stop=True), reads=[av], writes=[pC])
                S.op("dve", lambda e: e.tensor_copy(ab[:, :, :], av[:, cols].unsqueeze(2).broadcast_to([128, 8, 128])), reads=[av], writes=[ab])
                for h in range(8):
                    pr = pA if h < 4 else pB
                    S.op("pe", lambda e: e.matmul(pr[:, (h % 4) * 128:(h % 4 + 1) * 128], ab[:, h, :], tri, start=True, stop=True),
                         reads=[ab], writes=[pr])
                yield
                S.op("act", lambda e: e.activation(ac[:, :], pC[:, 0:8], AF.Copy), reads=[pC], writes=[ac])
                for h in range(8):
                    pr = pA if h < 4 else pB
                    S.op("dve", lambda e: e.scalar_tensor_tensor(arg[:, h, :], pr[:, (h % 4) * 128:(h % 4 + 1) * 128], ac[:, h:h + 1], mn,
                                                                 ALU.subtract, ALU.add), reads=[pr, ac], writes=[arg])
                yield
                S.op("act", lambda e: e.activation(dec[:, :, :], arg[:, :, :], AF.Exp), reads=[arg], writes=[dec])
                S.op("act", lambda e: e.activation(ea[:, :], ac[:, :], AF.Exp), reads=[ac], writes=[ea])
                for hh, pr in enumerate((pA, pB)):
                    S.op("act", lambda e: e.activation(cdec[:, hh * 4:hh * 4 + 4], pr[:, :].rearrange("p (h t) -> p h t", t=128)[:, :, last_i],
                                                       AF.Exp), reads=[pr], writes=[cdec])
                for g in range(2):
                    S.op("pe", lambda e: e.matmul(pC[:, 16 + g * 128:16 + (g + 1) * 128], bct[:, g, :], bct[:, 2 + g, :], start=True, stop=True),
                         reads=[bct], writes=[pC])
                yield
                for g in range(2):
                    S.op("dve", lambda e: e.tensor_tensor(MT[:, 4 * g:4 * g + 4, :], dec[:, 4 * g:4 * g + 4, :],
                                                          pC[:, 16 + g * 128:16 + (g + 1) * 128].unsqueeze(1).broadcast_to([128, 4, 128]), ALU.mult),
                         reads=[dec, pC], writes=[MT])
                    S.op("dve", lambda e: e.tensor_tensor(Bw[:, 4 * g:4 * g + 4, :],
                                                           bk[:, g * 64:(g + 1) * 64].unsqueeze(1).broadcast_to([128, 4, 64]),
                                                           dec[:, 4 * g:4 * g + 4, last_i:last_i + 1].broadcast_to([128, 4, 64]), ALU.mult),
                         reads=[bk, dec], writes=[Bw])
            else:
                S.op("pe", lambda e: e.matmul(pC[:, 0:8], tri, av[:, cols], start=True, stop=True), reads=[av], writes=[pC])
                S.op("pe", lambda e: e.matmul(pC[:, 8:16], cF(CI_ONES), av[:, cols], start=True, stop=True), reads=[av], writes=[pC])
                S.op("act", lambda e: e.activation(dte8[:, :], pC[:, 0:16], AF.Copy), reads=[pC], writes=[dte8])
                S.op("act", lambda e: e.activation(cdec[:, :], dte8[:, 8:16], AF.Exp), reads=[dte8], writes=[cdec])
                S.op("dve", lambda e: e.tensor_tensor(dte8[:, 0:8], dte8[:, 8:16], dte8[:, 0:8], ALU.subtract), reads=[dte8], writes=[dte8])
                S.op("act", lambda e: e.activation(dte8[:, 0:8], dte8[:, 0:8], AF.Exp), reads=[dte8], writes=[dte8])
                for g in range(2):
                    S.op("dve", lambda e: e.tensor_tensor(Bw[:, 4 * g:4 * g + 4, :],
                                                           bk[:, g * 64:(g + 1) * 64].unsqueeze(1).broadcast_to([128, 4, 64]),
                                                           dte8[:, 4 * g:4 * g + 4].unsqueeze(2).broadcast_to([128, 4, 64]), ALU.mult),
                         reads=[bk, dte8], writes=[Bw])
            S.op("dve", lambda e: e.tensor_tensor(xdt[:, :, :], xk[:, :].rearrange("p (h q) -> p h q", q=64),
                                                   dt[:, cols].unsqueeze(2).broadcast_to([128, 8, 64]), ALU.mult),
                 reads=[xk, dt], writes=[xdt])
            yield
            for h in range(8):
                g = h // 4
                hs = slice(h * 64, (h + 1) * 64)
                if full:
                    S.op("pe", lambda e: e.matmul(pA[:, hs], MT[:, h, :], xdt[:, h, :], start=True, stop=True), reads=[MT, xdt], writes=[pA])
                    S.op("pe", lambda e: e.matmul(pB[:, hs], bct[:, 2 + g, :], hb[:, h, :], start=True, stop=True), reads=[bct, hb], writes=[pB])
                S.op("pe", lambda e: e.matmul(pD[0:64, hs], Bw[:, h, :], xdt[:, h, :], start=True, stop=True), reads=[Bw, xdt], writes=[pD])
            yield
            y = B["yv"][b]
            if full:
                S.op("dve", lambda e: e.tensor_tensor(y[:, :].rearrange("p (h q) -> p h q", q=64), pB[:, :].rearrange("p (h q) -> p h q", q=64),
                                                      ea[:, :].unsqueeze(2).broadcast_to([128, 8, 64]), ALU.mult), reads=[pB, ea], writes=[y])
                S.op("dve", lambda e: e.tensor_tensor(y[:, :], y[:, :], pA[:, :], ALU.add), reads=[y, pA], writes=[y])
            S.op("dve", lambda e: e.tensor_tensor(h32[:, :, :], h32[:, :, :], cdec[0:64, :].unsqueeze(2).broadcast_to([64, 8, 64]), ALU.mult),
                 reads=[h32, cdec], writes=[h32])
            S.op("dve", lambda e: e.tensor_tensor(h32[:, :, :], h32[:, :, :], pD[0:64, :].rearrange("p (h q) -> p h q", q=64), ALU.add),
                 reads=[h32, pD], writes=[h32])
            S.op("act", lambda e: e.activation(hb[:, :, :], h32[:, :, :], AF.Copy), reads=[h32], writes=[hb])
            yield
            if not full:
                return
            if d == 0:
                S.op("dve", lambda e: e.tensor_tensor(tmp[:, :].rearrange("p (h q) -> p h q", q=64), xk[:, :].rearrange("p (h q) -> p h q", q=64),
                                                       rows[:, 32:40].unsqueeze(2).broadcast_to([128, 8, 64]), ALU.mult),
                     reads=[xk, rows], writes=[tmp])
                S.op("dve", lambda e: e.tensor_tensor(y[:, :], y[:, :], tmp[:, :], ALU.add), reads=[y, tmp], writes=[y])
                S.dma("pool", yssd.ap()[c0:c0 + 128, :], y[:, :], reads=[y])
            else:
                S.dma("pool", yssd1.ap()[c0:c0 + 128, :], y[:, :], reads=[y])

        for i in range(max(len(orders[0]), len(orders[1]))):
            active = [emit_chunk(d, orders[d][i]) for d in range(2) if i < len(orders[d])]
            while active:
                for g_ in list(active):
                    try:
                        next(g_)
                    except StopIteration:
                        active.remove(g_)
        S.barrier()
    with contextlib.ExitStack() as st:
        y0 = [S.sb(st, "f_y0%d" % i, [128, 512], F32) for i in range(2)]
        y1 = [S.sb(st, "f_y1%d" % i, [128, 512], F32) for i in range(2)]
        zk = [S.sb(st, "f_zk%d" % i, [128, 512], F32) for i in range(2)]
        tmp = S.sb(st, "f_tmp", [128, 512], F32)
        ss = [S.sb(st, "f_ss%d" % i, [128, 2], F32) for i in range(2)]
        yn = [S.sb(st, "f_yn%d" % i, [128, 512], BF16) for i in range(2)]
        yaS = [S.sb(st, "f_ya%d" % i, [128, 4, 128], BF16) for i in range(2)]
        for i, ck in enumerate(sorted(own_ck)):
            c0 = ck * 128
            b = i % 2
            psT = PS[6 + b][:, :].bitcast(BF16)
            S.dma("sp", y0[b][:, :], yssd.ap()[c0:c0 + 128, :], writes=[y0[b]])
            S.dma("sp", y1[b][:, :], yssd1.ap()[c0:c0 + 128, :], writes=[y1[b]])
            S.dma("sp", zk[b][:, :], G["zt"].ap()[c0:c0 + 128, :], writes=[zk[b]])
            y = y0[b]
            S.op("dve", lambda e: e.tensor_tensor(y[:, :], y[:, :], y1[b][:, :], ALU.add), reads=[y, y1[b]], writes=[y])
            S.op("act", lambda e: e.activation(zk[b][:, :], zk[b][:, :], AF.Silu), reads=[zk[b]], writes=[zk[b]])
            S.op("dve", lambda e: e.tensor_tensor(y[:, :], y[:, :], zk[b][:, :], ALU.mult), reads=[y, zk[b]], writes=[y])
            S.op("act", lambda e: e.activation(tmp[:, :], y[:, :], AF.Square, accum_out=ss[b][:, 0:1]), reads=[y], writes=[tmp, ss[b]])
            S.op("act", lambda e: e.activation(ss[b][:, 1:2], ss[b][:, 0:1], AF.Sqrt, bias=EPS, scale=1.0 / 512), reads=[ss[b]], writes=[ss[b]])
            S.op("dve", lambda e: e.reciprocal(ss[b][:, 1:2], ss[b][:, 1:2]), reads=[ss[b]], writes=[ss[b]])
            S.op("dve", lambda e: e.scalar_tensor_tensor(yn[b][:, :], y[:, :], ss[b][:, 1:2], rows[:, 40:552], ALU.mult, ALU.mult),
                 reads=[y, ss[b], rows], writes=[yn[b]])
            for c in range(4):
                S.op("pe", lambda e: e.transpose(psT[:, c * 128:(c + 1) * 128], yn[b][:, c * 128:(c + 1) * 128], cB(CI_ID)),
                     reads=[yn[b]], writes=[PS[6 + b]])
            ya = yaS[b]
            S.op("act", lambda e: e.activation(ya[:, :, :], psT[:, 0:512].rearrange("p (c t) -> p c t", t=128), AF.Copy),
                 reads=[PS[6 + b]], writes=[ya])
            S.dma("pool", G["yaT"].ap()[:, :, c0:c0 + 128], ya[:, :, :], reads=[ya])
        S.barrier()


def _rev(a, n):
    return bass.AP(a.tensor, a.offset + n - 1, [list(a.ap[0]), [-1, n]])


def phase4(nc, S, G):
    PS, layer, v128 = G["PS"], G["layer"], G["v128"]
    cF, cB = G["cF"], G["cB"]
    uT, ys5 = G["uT"], G["ys5"]
    I32 = mybir.dt.int32
    TWO_PI = 2.0 * math.pi
    with contextlib.ExitStack() as st:
        Bb = S.sb(st, "z_Bb", [128, 2, 12, 128], BF16)
        Cb = S.sb(st, "z_Cb", [128, 2, 12, 128], BF16)
        gw = S.sb(st, "z_gw", [128, 3, 768], BF16)
        for ri in range(2):
            S.dma("pool", Bb[:, ri, :, :], G["s5_B"].ap()[layer, ri].rearrange("gp k m -> k gp m"), writes=[Bb])
            S.dma("pool", Cb[:, ri, :, :], G["s5_C"].ap()[layer, ri].rearrange("gp k m -> k gp m"), writes=[Cb])
        S.dma("pool", gw[:, :, :], G["glu_w"].ap()[layer].rearrange("(kc p) n -> p kc n", p=128), writes=[gw])
        def tt(out, a, b, op, tiles_r, tiles_w, eng="dve"):
            S.op(eng, lambda e: e.tensor_tensor(out, a, b, op), reads=tiles_r, writes=tiles_w)

        def ts(out, a, s1, s2, op0, op1, tiles_r, tiles_w):
            if op1 is None:
                S.op("dve", lambda e: e.tensor_scalar(out, a, s1, None, op0), reads=tiles_r, writes=tiles_w)
            else:
                S.op("dve", lambda e: e.tensor_scalar(out, a, s1, s2, op0, op1), reads=tiles_r, writes=tiles_w)

        TL = 256
        ub = [S.sb(st, "z_ub%d" % i, [128, 3, 512], F32) for i in range(2)]
        ubb = [S.sb(st, "z_ubb%d" % i, [128, 3, 512], BF16) for i in range(2)]
        NB = 3
        br = [S.sb(st, "z_br%d" % i, [128, 512], F32) for i in range(NB)]
        bi = [S.sb(st, "z_bi%d" % i, [128, 512], F32) for i in range(NB)]
        m = [[S.sb(st, "z_m%d_%d" % (i, j), [128, 512], F32) for j in range(4)] for i in range(NB)]
        gr = [m[i][1] for i in range(NB)]
        gi = [m[i][3] for i in range(NB)]
        hrb = [S.sb(st, "z_hrb%d" % i, [128, 512], BF16) for i in range(NB)]
        hib = [S.sb(st, "z_hib%d" % i, [128, 512], BF16) for i in range(NB)]
        hst = S.sb(st, "z_hst", [128, 12, 2], F32)
        tn = S.sb(st, "z_tn", [128, 4], F32)
        yst = [S.sb(st, "z_yst%d" % i, [128, 512], F32) for i in range(2)]
        y0 = S.sb(st, "z_y0", [128, 3, 512], F32)
        gy = S.sb(st, "z_gy", [128, 3, 512], BF16)
        vl = [S.sb(st, "z_vl%d" % i, [128, 512], F32) for i in range(3)]
        sgt = [S.sb(st, "z_sg%d" % i, [128, 512], F32) for i in range(2)]
        ydo = [S.sb(st, "z_ydo%d" % i, [128, 512], BF16) for i in range(2)]
        nn = 0
        for d in range(2):
            with contextlib.ExitStack() as sd:
                Er = S.sb(sd, "z_Er", [128, 12, TL], F32)
                Ei = S.sb(sd, "z_Ei", [128, 12, TL], F32)
                Fr = S.sb(sd, "z_Fr", [128, 12, TL], F32)
                Fi = S.sb(sd, "z_Fi", [128, 12, TL], F32)
                rho = S.sb(sd, "z_rho", [128, 12], F32)
                EW = S.sb(sd, "z_EW", [128, 12, 2], F32)
                with contextlib.ExitStack() as st2:
                    t1 = S.sb(st2, "z_t1", [128, 12, TL], F32)
                    t2 = S.sb(st2, "z_t2", [128, 12, TL], F32)
                    names = ["lr", "li", "stp", "u", "f", "sphi", "s2", "c1", "ar", "ai", "den", "am1", "fr", "fi", "x1", "x2", "msk"]
                    P = {nm: S.sb(st2, "zp_%s" % nm, [128, 12], F32) for nm in names}
                    P["rho"] = rho
                    ki = S.sb(st2, "zp_ki", [128, 12], I32)
                    A = lambda nm: P[nm][:, :]
                    S.dma("sp", A("lr"), G["s5_lr"].ap()[layer, d], writes=[P["lr"]])
                    S.dma("sp", A("li"), G["s5_li"].ap()[layer, d], writes=[P["li"]])
                    S.dma("sp", A("stp"), G["s5_ldt"].ap()[layer, d], writes=[P["stp"]])
                    S.op("act", lambda e: e.activation(A("stp"), A("stp"), AF.Exp), reads=[P["stp"]], writes=[P["stp"]])
                    tt(A("rho"), A("lr"), A("stp"), ALU.mult, [P["lr"], P["stp"]], [P["rho"]])
                    S.op("act", lambda e: e.activation(A("rho"), A("rho"), AF.Exp), reads=[P["rho"]], writes=[P["rho"]])
                    tt(A("u"), A("li"), A("stp"), ALU.mult, [P["li"], P["stp"]], [P["u"]])
                    ts(A("u"), A("u"), 1.0 / TWO_PI, 0.5, ALU.mult, ALU.add, [P["u"]], [P["u"]])
                    S.op("dve", lambda e: e.tensor_copy(ki[:, :], A("u")), reads=[P["u"]], writes=[ki])
                    S.op("dve", lambda e: e.tensor_copy(A("f"), ki[:, :]), reads=[ki], writes=[P["f"]])
                    tt(A("f"), A("u"), A("f"), ALU.subtract, [P["u"], P["f"]], [P["f"]])
                    ts(A("msk"), A("f"), 0.5, None, ALU.is_ge, None, [P["f"]], [P["msk"]])
                    tt(A("f"), A("f"), A("msk"), ALU.subtract, [P["f"], P["msk"]], [P["f"]])
                    ts(A("f"), A("f"), -0.49999, 0.49999, ALU.max, ALU.min, [P["f"]], [P["f"]])
                    S.op("act", lambda e: e.activation(A("sphi"), A("f"), AF.Sin, scale=TWO_PI), reads=[P["f"]], writes=[P["sphi"]])
                    S.op("act", lambda e: e.activation(A("s2"), A("f"), AF.Sin, scale=math.pi), reads=[P["f"]], writes=[P["s2"]])
                    tt(A("c1"), A("s2"), A("s2"), ALU.mult, [P["s2"]], [P["c1"]])
                    ts(A("c1"), A("c1"), 2.0, -1.0, ALU.mult, ALU.add, [P["c1"]], [P["c1"]])
                    S.op("dve", lambda e: e.tensor_copy(Er[:, :, 0], A("c1")), reads=[P["c1"]], writes=[Er])
                    S.op("dve", lambda e: e.tensor_copy(Ei[:, :, 0], A("sphi")), reads=[P["sphi"]], writes=[Ei])
                    tt(A("ar"), A("rho"), A("c1"), ALU.mult, [P["rho"], P["c1"]], [P["ar"]])
                    tt(A("ai"), A("rho"), A("sphi"), ALU.mult, [P["rho"], P["sphi"]], [P["ai"]])
                    ts(A("ai"), A("ai"), -1.0, None, ALU.mult, None, [P["ai"]], [P["ai"]])
                    tt(A("den"), A("lr"), A("lr"), ALU.mult, [P["lr"]], [P["den"]])
                    tt(A("x1"), A("li"), A("li"), ALU.mult, [P["li"]], [P["x1"]])
                    tt(A("den"), A("den"), A("x1"), ALU.add, [P["den"], P["x1"]], [P["den"]])
                    S.op("dve", lambda e: e.reciprocal(A("den"), A("den")), reads=[P["den"]], writes=[P["den"]])
                    ts(A("am1"), A("ar"), -1.0, None, ALU.add, None, [P["ar"]], [P["am1"]])
                    tt(A("x1"), A("am1"), A("lr"), ALU.mult, [P["am1"], P["lr"]], [P["x1"]])
                    tt(A("x2"), A("ai"), A("li"), ALU.mult, [P["ai"], P["li"]], [P["x2"]])
                    tt(A("fr"), A("x1"), A("x2"), ALU.add, [P["x1"], P["x2"]], [P["fr"]])
                    tt(A("fr"), A("fr"), A("den"), ALU.mult, [P["fr"], P["den"]], [P["fr"]])
                    tt(A("x1"), A("ai"), A("lr"), ALU.mult, [P["ai"], P["lr"]], [P["x1"]])
                    tt(A("x2"), A("am1"), A("li"), ALU.mult, [P["am1"], P["li"]], [P["x2"]])
                    tt(A("fi"), A("x1"), A("x2"), ALU.subtract, [P["x1"], P["x2"]], [P["fi"]])
                    tt(A("fi"), A("fi"), A("den"), ALU.mult, [P["fi"], P["den"]], [P["fi"]])
                    n = 1
                    while n < TL:
                        cr = Er[:, :, n - 1:n].broadcast_to([128, 12, n])
                        ci = Ei[:, :, n - 1:n].broadcast_to([128, 12, n])
                        tt(t1[:, :, 0:n], Er[:, :, 0:n], cr, ALU.mult, [Er], [t1])
                        tt(t2[:, :, 0:n], Ei[:, :, 0:n], ci, ALU.mult, [Ei], [t2])
                        tt(Er[:, :, n:2 * n], t1[:, :, 0:n], t2[:, :, 0:n], ALU.subtract, [t1, t2], [Er])
                        tt(t1[:, :, 0:n], Er[:, :, 0:n], ci, ALU.mult, [Er, Ei], [t1])
                        tt(t2[:, :, 0:n], Ei[:, :, 0:n], cr, ALU.mult, [Ei, Er], [t2])
                        tt(Ei[:, :, n:2 * n], t1[:, :, 0:n], t2[:, :, 0:n], ALU.add, [t1, t2], [Ei])
                        n *= 2
                    S.op("dve", lambda e: e.tensor_copy(EW[:, :, 0], Er[:, :, TL - 1]), reads=[Er], writes=[EW])
                    S.op("dve", lambda e: e.tensor_copy(EW[:, :, 1], Ei[:, :, TL - 1]), reads=[Ei], writes=[EW])
                    frb = P["fr"][:, :].unsqueeze(2).broadcast_to([128, 12, TL])
                    fib = P["fi"][:, :].unsqueeze(2).broadcast_to([128, 12, TL])
                    tt(t1[:, :, :], Er[:, :, :], frb, ALU.mult, [Er, P["fr"]], [t1])
                    tt(t2[:, :, :], Ei[:, :, :], fib, ALU.mult, [Ei, P["fi"]], [t2])
                    tt(Fr[:, :, :], t1[:, :, :], t2[:, :, :], ALU.subtract, [t1, t2], [Fr])
                    tt(t1[:, :, :], Ei[:, :, :], frb, ALU.mult, [Ei, P["fr"]], [t1])
                    tt(t2[:, :, :], Er[:, :, :], fib, ALU.mult, [Er, P["fi"]], [t2])
                    tt(Fi[:, :, :], t1[:, :, :], t2[:, :, :], ALU.add, [t1, t2], [Fi])
                    if d == 1:
                        for tb in (Er, Ei, Fr, Fi):
                            for gp in range(12):
                                S.op("dve", lambda e: e.tensor_copy(t1[:, gp, :], _rev(tb[:, gp, :], TL)), reads=[tb], writes=[t1])
                            S.op("dve", lambda e: e.tensor_copy(tb[:, :, :], t1[:, :, :]), reads=[t1], writes=[tb])
                    S.barrier()

                own_set = set(G["out_tiles"]) | {TILES[0]}
                if d == 0:
                    last_own = max(i for i, tl in enumerate(TILES) if tl in own_set)
                    tiles = TILES[:last_own + 1]
                else:
                    tiles = [TILES[0]] + TILES[:0:-1]
                S.op("pool", lambda e: e.memset(hst[:], 0.0), writes=[hst])
                for ti, (t0, W) in enumerate(tiles):
                    nfr = W // TL
                    full = (t0, W) in set(G["out_tiles"]) or d == 0
                    u_, ub_ = ub[ti % 2], ubb[ti % 2]
                    S.dma("sp", u_[:, :, 0:W], uT.ap()[:, :, t0:t0 + W], writes=[u_])
                    S.op("act", lambda e: e.activation(ub_[:, :, 0:W], u_[:, :, 0:W], AF.Copy), reads=[u_], writes=[ub_])
                    if d == 1 and full:
                        S.dma("sp", y0[:, :, 0:W], ys5.ap()[:, :, t0:t0 + W], writes=[y0])
                    for gp in range(12):
                        uc = gp // 4
                        b = nn % NB; nn += 1
                        mm = m[b]
                        S.op("pe", lambda e: e.matmul(PS[0][:, 0:W], Bb[:, 0, gp, :], ub_[:, uc, 0:W], start=True, stop=True), reads=[Bb, ub_], writes=[PS[0]])
                        S.op("pe", lambda e: e.matmul(PS[1][:, 0:W], Bb[:, 1, gp, :], ub_[:, uc, 0:W], start=True, stop=True), reads=[Bb, ub_], writes=[PS[1]])
                        S.op("act", lambda e: e.activation(br[b][:, 0:W], PS[0][:, 0:W], AF.Copy), reads=[PS[0]], writes=[br[b]])
                        S.op("act", lambda e: e.activation(bi[b][:, 0:W], PS[1][:, 0:W], AF.Copy), reads=[PS[1]], writes=[bi[b]])

                        def bc(tb):
                            return tb[:, gp, :].unsqueeze(1).broadcast_to([128, nfr, TL])

                        def v3(t):
                            return t[:, 0:W].rearrange("p (c k) -> p c k", k=TL)
                        tt(v3(mm[0]), v3(br[b]), bc(Fr), ALU.mult, [br[b], Fr], [mm[0]], "dve")
                        tt(v3(mm[1]), v3(bi[b]), bc(Fi), ALU.mult, [bi[b], Fi], [mm[1]], "dve")
                        tt(v3(mm[2]), v3(bi[b]), bc(Fr), ALU.mult, [bi[b], Fr], [mm[2]], "dve")
                        tt(v3(mm[3]), v3(br[b]), bc(Fi), ALU.mult, [br[b], Fi], [mm[3]], "dve")
                        tt(mm[0][:, 0:W], mm[0][:, 0:W], mm[1][:, 0:W], ALU.subtract, [mm[0], mm[1]], [mm[0]], "dve")
                        tt(mm[2][:, 0:W], mm[2][:, 0:W], mm[3][:, 0:W], ALU.add, [mm[2], mm[3]], [mm[2]], "dve")
                        frs = list(range(nfr)) if d == 0 else list(range(nfr - 1, -1, -1))
                        for fk in frs:
                            cs = slice(fk * TL, (fk + 1) * TL)
                            for (gt, vt_, comp) in ((gr[b], mm[0], 0), (gi[b], mm[2], 1)):
                                o_ap, v_ap = gt[:, cs], vt_[:, cs]
                                if d == 1:
                                    o_ap, v_ap = _rev(o_ap, TL), _rev(v_ap, TL)
                                S.op("dve", lambda e: e.tensor_tensor_scan(o_ap, rho[:, gp:gp + 1].broadcast_to([128, TL]), v_ap,
                                                                           hst[:, gp, comp:comp + 1], ALU.mult, ALU.add),
                                     reads=[rho, vt_, hst], writes=[gt])
                            ie = fk * TL + (TL - 1 if d == 0 else 0)
                            gre, gie = gr[b][:, ie:ie + 1], gi[b][:, ie:ie + 1]
                            e_r, e_i = EW[:, gp, 0:1], EW[:, gp, 1:2]
                            tt(tn[:, 0:1], gie, e_i, ALU.mult, [gi[b], EW], [tn], "pool")
                            tt(tn[:, 1:2], gre, e_i, ALU.mult, [gr[b], EW], [tn], "pool")
                            tt(tn[:, 2:3], gre, e_r, ALU.mult, [gr[b], EW], [tn], "pool")
                            tt(tn[:, 3:4], gie, e_r, ALU.mult, [gi[b], EW], [tn], "pool")
                            tt(hst[:, gp, 0:1], tn[:, 2:3], tn[:, 0:1], ALU.add, [tn], [hst], "pool")
                            tt(hst[:, gp, 1:2], tn[:, 3:4], tn[:, 1:2], ALU.subtract, [tn], [hst], "pool")
                        if not full:
                            continue
                        tt(v3(mm[0]), v3(gr[b]), bc(Er), ALU.mult, [gr[b], Er], [mm[0]], "dve")
                        tt(v3(mm[2]), v3(gi[b]), bc(Ei), ALU.mult, [gi[b], Ei], [mm[2]], "dve")
                        tt(v3(br[b]), v3(gr[b]), bc(Ei), ALU.mult, [gr[b], Ei], [br[b]], "dve")
                        tt(v3(bi[b]), v3(gi[b]), bc(Er), ALU.mult, [gi[b], Er], [bi[b]], "dve")
                        tt(hrb[b][:, 0:W], mm[0][:, 0:W], mm[2][:, 0:W], ALU.add, [mm[0], mm[2]], [hrb[b]], "dve")
                        tt(hib[b][:, 0:W], br[b][:, 0:W], bi[b][:, 0:W], ALU.subtract, [br[b], bi[b]], [hib[b]], "dve")
                        py = PS[2 + uc]
                        S.op("pe", lambda e: e.matmul(py[:, 0:W], Cb[:, 0, gp, :], hrb[b][:, 0:W], start=(gp % 4 == 0), stop=False), reads=[Cb, hrb[b]], writes=[py])
                        S.op("pe", lambda e: e.matmul(py[:, 0:W], Cb[:, 1, gp, :], hib[b][:, 0:W], start=False, stop=(gp % 4 == 3)), reads=[Cb, hib[b]], writes=[py])
                    if not full:
                        continue
                    for uc in range(3):
                        py = PS[2 + uc]
                        ys_ = yst[uc % 2]
                        if d == 0:
                            S.op("dve", lambda e: e.scalar_tensor_tensor(ys_[:, 0:W], u_[:, uc, 0:W], v128[:, 106 + uc:107 + uc], py[:, 0:W], ALU.mult, ALU.add),
                                 reads=[u_, v128, py], writes=[ys_])
                            S.dma("pool", ys5.ap()[:, uc, t0:t0 + W], ys_[:, 0:W], reads=[ys_])
                        else:
                            tt(ys_[:, 0:W], y0[:, uc, 0:W], py[:, 0:W], ALU.add, [y0, py], [ys_], "dve")
                            S.op("act", lambda e: e.activation(gy[:, uc, 0:W], ys_[:, 0:W], AF.Gelu_apprx_tanh), reads=[ys_], writes=[gy])
                    if d == 1:
                        for j in range(6):
                            pg = PS[5 + j % 2]
                            for kc in range(3):
                                S.op("pe", lambda e: e.matmul(pg[:, 0:W], gw[:, kc, j * 128:(j + 1) * 128], gy[:, kc, 0:W], start=(kc == 0), stop=(kc == 2)),
                                     reads=[gw, gy], writes=[pg])
                            if j < 3:
                                S.op("act", lambda e: e.activation(vl[j][:, 0:W], pg[:, 0:W], AF.Identity, bias=v128[:, 109 + j:110 + j], scale=1.0),
                                     reads=[pg, v128], writes=[vl[j]])
                            else:
                                sg_ = sgt[j % 2]
                                yo_ = ydo[j % 2]
                                S.op("act", lambda e: e.activation(sg_[:, 0:W], pg[:, 0:W], AF.Sigmoid, bias=v128[:, 109 + j:110 + j], scale=1.0),
                                     reads=[pg, v128], writes=[sg_])
                                tt(yo_[:, 0:W], vl[j - 3][:, 0:W], sg_[:, 0:W], ALU.mult, [vl[j - 3], sg_], [yo_], "dve")
                                S.dma("pool", G["ydT"].ap()[:, j - 3, t0:t0 + W], yo_[:, 0:W], reads=[yo_])
                S.barrier()


def phase5(nc, S, G):
    PS, layer, v128, modv, A2 = G["PS"], G["layer"], G["v128"], G["modv"], G["A2"]
    cF, cB = G["cF"], G["cB"]
    h_src, h_dst, last = G["h_src"], G["h_dst"], G["last"]
    BRK = G["BRK"]
    ysrc = [G["yaT"], G["ybT"], G["ycT"], G["ydT"]]
    with contextlib.ExitStack() as st:
        xn = S.sb(st, "m_xn", [128, KC, 512], BF16)
        ys = [S.sb(st, "m_y%d" % i, [128, BRK[i], 512], BF16) for i in range(4)]
        ht = S.sb(st, "m_h", [128, KC, 512], F32)
        acc = S.sb(st, "m_acc", [128, KC, 512], F32)
        mb = S.sb(st, "m_mb", [128, KC, 512], BF16)
        h1 = S.sb(st, "m_h1", [128, KC, 512], F32)
        sq = S.sb(st, "m_sq", [128, KC, 512], BF16)
        rstd = S.sb(st, "m_rstd", [128, 512], F32)
        xf = S.sb(st, "m_xf", [128, KC, 512], BF16)
        hid = S.sb(st, "m_hid", [128, 22, 512], BF16)
        h2 = [S.sb(st, "m_h2%d" % i, [128, 512], F32) for i in range(2)]
        sg = [S.sb(st, "m_sg%d" % i, [128, 512], F32) for i in range(2)]
        tm = [S.sb(st, "m_tm%d" % i, [128, 512], F32) for i in range(2)]
        wg = [S.sb(st, "m_wg%d" % i, [128, 4, KC, 128], BF16) for i in range(2)]
        wb = [S.sb(st, "m_wb%d" % i, [128, 15, 128], BF16) for i in range(2)]
        wo = [S.sb(st, "m_wo%d" % i, [128, KC, 128], BF16) for i in range(2)]
        wgu = [S.sb(st, "m_wgu%d" % i, [128, 2, KC, 128], BF16) for i in range(2)]
        wdn = [S.sb(st, "m_wdn%d" % i, [128, 22, 128], BF16) for i in range(2)]
        BOFF = [0, 4, 8, 12]
        npp = 0
        for (t0, W) in G["out_tiles"]:
            s = 1 if t0 < NCTX else 0
            S.dma("sp", xn[:, :, 0:W], G["xnT"].ap()[:, :, t0:t0 + W], writes=[xn])
            for i in range(4):
                S.dma("sp", ys[i][:, :, 0:W], ysrc[i].ap()[:, :, t0:t0 + W], writes=[ys[i]])
            S.dma("sp", ht[:, :, 0:W], h_src.ap().rearrange("(kc p) t -> p kc t", p=128)[:, :, t0:t0 + W], writes=[ht])
            for j in range(8):
                wgj, wbj = wg[j % 2], wb[j % 2]
                for i in range(4):
                    S.dma("sp", wgj[:, i, :, :], G["wg_bf"].ap()[i * 8 + j], writes=[wgj])
                    S.dma("sp", wbj[:, BOFF[i]:BOFF[i] + BRK[i], :], G["wb_bf"][i].ap()[j], writes=[wbj])
                for i in range(4):
                    pg, pb = PS[npp % 2], PS[2 + npp % 2]
                    sgi, tmi = sg[npp % 2], tm[npp % 2]
                    npp += 1
                    for kc in range(KC):
                        S.op("pe", lambda e: e.matmul(pg[:, 0:W], wgj[:, i, kc, :], xn[:, kc, 0:W], start=(kc == 0), stop=(kc == KC - 1)),
                             reads=[wgj, xn], writes=[pg])
                    for kc in range(BRK[i]):
                        S.op("pe", lambda e: e.matmul(pb[:, 0:W], wbj[:, BOFF[i] + kc, :], ys[i][:, kc, 0:W], start=(kc == 0), stop=(kc == BRK[i] - 1)),
                             reads=[wbj, ys[i]], writes=[pb])
                    S.op("act", lambda e: e.activation(sgi[:, 0:W], pg[:, 0:W], AF.Sigmoid), reads=[pg], writes=[sgi])
                    if i == 0:
                        S.op("dve", lambda e: e.tensor_tensor(acc[:, j, 0:W], sgi[:, 0:W], pb[:, 0:W], ALU.mult), reads=[sgi, pb], writes=[acc])
                    else:
                        S.op("dve", lambda e: e.tensor_tensor(tmi[:, 0:W], sgi[:, 0:W], pb[:, 0:W], ALU.mult), reads=[sgi, pb], writes=[tmi])
                        S.op("dve", lambda e: e.tensor_tensor(acc[:, j, 0:W], acc[:, j, 0:W], tmi[:, 0:W], ALU.add), reads=[acc, tmi], writes=[acc])
                S.op("act", lambda e: e.activation(mb[:, j, 0:W], acc[:, j, 0:W], AF.Copy), reads=[acc], writes=[mb])
            for j in range(8):
                woj = wo[j % 2]
                S.dma("sp", woj[:], G["wo_bf"].ap()[j], writes=[woj])
                po = PS[4 + j % 2]
                for kc in range(KC):
                    S.op("pe", lambda e: e.matmul(po[:, 0:W], woj[:, kc, :], mb[:, kc, 0:W], start=(kc == 0), stop=(kc == KC - 1)),
                         reads=[woj, mb], writes=[po])
                S.op("dve", lambda e: e.scalar_tensor_tensor(h1[:, j, 0:W], po[:, 0:W], modv[:, 16 + j, s:s + 1], ht[:, j, 0:W], ALU.mult, ALU.add),
                     reads=[po, modv, ht], writes=[h1])
            S.op("act", lambda e: e.activation(sq[:, :, 0:W], h1[:, :, 0:W], AF.Square), reads=[h1], writes=[sq])
            for kc in range(KC):
                S.op("pe", lambda e: e.matmul(PS[6][:, 0:W], cB(CI_ONES), sq[:, kc, 0:W], start=(kc == 0), stop=(kc == KC - 1)),
                     reads=[sq], writes=[PS[6]])
            S.op("act", lambda e: e.activation(rstd[:, 0:W], PS[6][:, 0:W], AF.Sqrt, bias=EPS, scale=1.0 / D), reads=[PS[6]], writes=[rstd])
            S.op("dve", lambda e: e.reciprocal(rstd[:, 0:W], rstd[:, 0:W]), reads=[rstd], writes=[rstd])
            for kc in range(KC):
                tmi = tm[kc % 2]
                S.op("dve", lambda e: e.tensor_tensor(tmi[:, 0:W], h1[:, kc, 0:W], rstd[:, 0:W], ALU.mult), reads=[h1, rstd], writes=[tmi])
                S.op("act", lambda e: e.activation(xf[:, kc, 0:W], tmi[:, 0:W], AF.Identity, bias=modv[:, 24 + kc, s:s + 1], scale=A2[:, kc, s:s + 1]),
                     reads=[tmi, modv, A2], writes=[xf])
            for jj in range(22):
                w = wgu[jj % 2]
                S.dma("sp", w[:, 0, :, :], G["wgu_bf"].ap()[jj], writes=[w])
                S.dma("sp", w[:, 1, :, :], G["wgu_bf"].ap()[22 + jj], writes=[w])
                pg, pu = PS[jj % 2], PS[2 + jj % 2]
                sgi = sg[jj % 2]
                for kc in range(KC):
                    S.op("pe", lambda e: e.matmul(pg[:, 0:W], w[:, 0, kc, :], xf[:, kc, 0:W], start=(kc == 0), stop=(kc == KC - 1)),
                         reads=[w, xf], writes=[pg])
                for kc in range(KC):
                    S.op("pe", lambda e: e.matmul(pu[:, 0:W], w[:, 1, kc, :], xf[:, kc, 0:W], start=(kc == 0), stop=(kc == KC - 1)),
                         reads=[w, xf], writes=[pu])
                S.op("act", lambda e: e.activation(sgi[:, 0:W], pg[:, 0:W], AF.Silu), reads=[pg], writes=[sgi])
                S.op("dve", lambda e: e.tensor_tensor(hid[:, jj, 0:W], sgi[:, 0:W], pu[:, 0:W], ALU.mult), reads=[sgi, pu], writes=[hid])
            for j in range(8):
                w = wdn[j % 2]
                S.dma("sp", w[:], G["wdn_bf"].ap()[j], writes=[w])
                po = PS[4 + j % 2]
                for kc in range(22):
                    S.op("pe", lambda e: e.matmul(po[:, 0:W], w[:, kc, :], hid[:, kc, 0:W], start=(kc == 0), stop=(kc == 21)),
                         reads=[w, hid], writes=[po])
                o = h2[j % 2]
                S.op("dve", lambda e: e.scalar_tensor_tensor(o[:, 0:W], po[:, 0:W], modv[:, 40 + j, s:s + 1], h1[:, j, 0:W], ALU.mult, ALU.add),
                     reads=[po, modv, h1], writes=[o])
                if last:
                    S.dma("pool", h_dst.ap()[j * 128:(j + 1) * 128, t0 - NCTX:t0 - NCTX + W], o[:, 0:W], reads=[o])
                else:
                    S.dma("pool", h_dst.ap()[j * 128:(j + 1) * 128, t0:t0 + W], o[:, 0:W], reads=[o])


def _prep_shared(inp):
    out = {}
    f32 = np.float32
    for half in (0, 1):
        d = {}
        dsel = [0, 1] if half == 0 else [1, 0]
        for k in ("w_mod", "w_gate", "w_br_ssd", "w_br_diff", "w_br_gqa", "w_br_s5", "w_out", "ffn_w_gate_up",
                  "ffn_w_down", "s5_glu_w"):
            d[k] = np.ascontiguousarray(inp[k], dtype=f32)
        w_in = np.array(inp["w_in"], dtype=f32)
        if half == 1:
            tmp = w_in[:, :, C_DT:C_DT + 8].copy()
            w_in[:, :, C_DT:C_DT + 8] = w_in[:, :, C_DT + 8:C_DT + 16]
            w_in[:, :, C_DT + 8:C_DT + 16] = tmp
        d["w_in"] = w_in
        v = np.zeros((2, 128, NV), f32)
        r = np.zeros((2, NR), f32)
        for l in range(2):
            v[l, :, 0:8] = inp["norm1_g"][l].reshape(8, 128).T
            v[l, :, 8:16] = inp["norm2_g"][l].reshape(8, 128).T
            v[l, :, 16:64] = inp["b_mod"][l].reshape(48, 128).T
            v[l, :, 64] = np.tile(inp["diff_qn_g"][l], 2)
            v[l, :, 65] = np.tile(inp["diff_kn_g"][l], 2)
            v[l, :, 66] = np.tile(inp["gqa_qn_g"][l], 2)
            v[l, :, 67] = np.tile(inp["gqa_kn_g"][l], 2)
            v[l, :, 68] = np.tile(inp["diff_subln_g"][l][:64], 2)
            v[l, :, 69] = np.tile(inp["diff_subln_g"][l][64:], 2)
            v[l, :, 70:76] = inp["ssd_conv_b"][l].reshape(6, 128).T
            cw = inp["ssd_conv_w"][l]
            if half == 1:
                cw = cw[::-1]
            v[l, :, 76:106] = cw.reshape(5, 6, 128).transpose(2, 1, 0).reshape(128, 30)
            v[l, :, 106:109] = inp["s5_d"][l].reshape(3, 128).T
            v[l, :, 109:115] = inp["s5_glu_b"][l].reshape(6, 128).T
            r[l, 0:16] = inp["ssd_a_log"][l][dsel].reshape(16)
            r[l, 16:32] = inp["ssd_dt_bias"][l][dsel].reshape(16)
            r[l, 32:40] = inp["ssd_d"][l]
            r[l, 40:552] = inp["ssd_norm_g"][l]
            r[l, 552:616] = inp["diff_lam_q1"][l]
            r[l, 616:680] = inp["diff_lam_k1"][l]
            r[l, 680:744] = inp["diff_lam_q2"][l]
            r[l, 744:808] = inp["diff_lam_k2"][l]
        d["vec128"] = v
        d["rowvecs"] = r

        def s5lay(a):
            a = np.asarray(a, f32)[:, dsel]
            return np.ascontiguousarray(a.reshape(2, 2, 12, 2, 64).transpose(0, 1, 3, 4, 2).reshape(2, 2, 128, 12))
        d["s5_lr"] = s5lay(inp["s5_lam_re"])
        d["s5_li"] = s5lay(inp["s5_lam_im"])
        ldt = np.asarray(inp["s5_log_dt"], f32)
        d["s5_ldt"] = s5lay(np.broadcast_to(ldt[..., None], (2, 2, 24, 64)))
        Bb = np.zeros((2, 2, 12, 128, 128), f32)
        Cb = np.zeros((2, 2, 12, 128, 128), f32)
        for ri, (bk, ck) in enumerate((("s5_b_re", "s5_c_re"), ("s5_b_im", "s5_c_im"))):
            b = np.asarray(inp[bk], f32)
            c = np.asarray(inp[ck], f32)
            for gp in range(12):
                for two in range(2):
                    g = 2 * gp + two
                    r0 = 32 * (gp % 4) + 16 * two
                    Bb[:, ri, gp, r0:r0 + 16, 64 * two:64 * two + 64] = b[:, g].transpose(0, 2, 1)
                    Cb[:, ri, gp, 64 * two:64 * two + 64, r0:r0 + 16] = c[:, g].transpose(0, 2, 1)
        d["s5_Bblk"] = Bb
        d["s5_Cblk"] = Cb
        ct, sn = rope_tables(half == 1)
        d["rope_cos"] = ct
        d["rope_sin"] = sn
        d["consts"] = make_consts()
        out[half] = d
    return out


def make_in_map(inp, shared, core):
    b, half = core // 2, core % 2
    x = np.asarray(inp["x"][b], np.float32)
    cx = np.asarray(inp["ctx"][b], np.float32)
    if half == 1:
        x = x[::-1]
        cx = cx[::-1]
    m = dict(shared[half])
    m["hT0"] = np.ascontiguousarray(np.concatenate([cx, x], 0).T)
    c2 = np.stack([np.asarray(inp["c"][b], np.float32), np.asarray(inp["c_ctx"], np.float32)], -1)
    m["c2"] = np.ascontiguousarray(c2.reshape(8, 128, 2).transpose(1, 0, 2))
    return m


_NC_CACHE = {}


def kernel(**inputs):
    if "nc" not in _NC_CACHE:
        _NC_CACHE["nc"] = build()
    nc = _NC_CACHE["nc"]
    shared = _prep_shared(inputs)
    in_maps = [make_in_map(inputs, shared, core) for core in range(8)]
    res = run_bass_kernel_spmd(nc, in_maps, core_ids=list(range(8)))
    out = np.zeros((4, NLAT, D), np.float32)
    n = N_OUT_TILES_LAST * 512
    for core in range(8):
        b, half = core // 2, core % 2
        y = np.asarray(res.results[core]["y"]).T
        if half == 0:
            out[b, :n] = y
        else:
            out[b, NLAT - n:] = y[::-1]
    return out
```
